# Optimizing a Trainium2 kernel written in Bass

```python
import math
import jax, jax.numpy as jnp
from jax import lax
import numpy as np

D_MODEL = 2048
BATCH = 8
SEQ = 2048
DEPTH = 1

D_HYENA = D_MODEL // 2
HYENA_GROUPS = 8
HYENA_ORDER = 2
HYENA_SHORT = 3
FILTER_EMB = 33
FILTER_ORDER = 64
FAST_DECAY = 0.3
SLOW_DECAY = 1.5
DECAY_TARGET = 1e-2
MOD_SHIFT = 0.0
N_DIR = 2
D_SGU = D_MODEL // 2
SGU_HEADS = 8
SGU_HEAD_DIM = D_SGU // SGU_HEADS
CHUNK = 128
N_BRANCH = 2
D_FF = -(-8 * D_MODEL // (3 * 256)) * 256
D_IN = 3 * D_HYENA + 2 * D_SGU + N_BRANCH * D_MODEL
EPS = 1e-6

kernel_name = "hyena_sgu_gated_hybrid_encoder"

F32 = jnp.float32


def _rmsnorm(x, g):
    x32 = x.astype(F32)
    y = x32 * lax.rsqrt(jnp.mean(x32 * x32, axis=-1, keepdims=True) + EPS)
    return (y * g.astype(F32)).astype(x.dtype)


def _layernorm(x, g, b):
    x32 = x.astype(F32)
    mu = jnp.mean(x32, axis=-1, keepdims=True)
    var = jnp.mean(jnp.square(x32 - mu), axis=-1, keepdims=True)
    y = (x32 - mu) * lax.rsqrt(var + EPS)
    return (y * g.astype(F32) + b.astype(F32)).astype(x.dtype)


def _hyena_filters(L, w1, b1, w2, b2, w3, b3, freq, w4):
    t = jnp.linspace(0.0, 1.0, L, dtype=F32)[:, None]
    bands = (FILTER_EMB - 1) // 2
    w = 2.0 * math.pi * jnp.arange(L, dtype=F32)[:, None] / L
    f = jnp.linspace(1e-4, bands - 1, bands, dtype=F32)[None, :]
    z = jnp.concatenate([t, jnp.cos(f * w), -jnp.sin(f * w)], axis=-1)
    fr = freq.astype(F32)
    h = jnp.sin(fr * (z @ w1.astype(F32) + b1.astype(F32)))
    h = jnp.sin(fr * (h @ w2.astype(F32) + b2.astype(F32)))
    h = jnp.sin(fr * (h @ w3.astype(F32) + b3.astype(F32)))
    h = (h @ w4.astype(F32)).reshape(L, N_DIR, HYENA_ORDER, D_HYENA)
    max_decay = math.log(DECAY_TARGET) / FAST_DECAY
    min_decay = math.log(DECAY_TARGET) / SLOW_DECAY
    deltas = jnp.linspace(min_decay, max_decay, D_HYENA, dtype=F32)
    decay = jnp.exp(-t * jnp.abs(deltas))
    h = h * (decay + MOD_SHIFT)[:, None, None, :]
    h_fwd, h_bwd = h[:, 0], h[:, 1]
    k = jnp.concatenate([h_fwd, jnp.zeros((1, HYENA_ORDER, D_HYENA), F32), h_bwd[:0:-1]], axis=0)
    k = k / jnp.sum(jnp.abs(k), axis=0, keepdims=True)
    return jnp.fft.rfft(k, axis=0)


def _long_conv(z, k_f, bias):
    L = z.shape[1]
    zf = jnp.fft.rfft(z, n=2 * L, axis=1)
    y = jnp.fft.irfft(zf * k_f[None], n=2 * L, axis=1)[:, :L]
    return y + z * bias.astype(F32)


def _hyena_mixer(p, conv_w, conv_b, k_f, bias):
    pp = jnp.pad(p, ((0, 0), (1, 1), (0, 0)))
    p = pp[:, :-2] * conv_w[0] + pp[:, 1:-1] * conv_w[1] + pp[:, 2:] * conv_w[2] + conv_b
    v, x1, x2 = jnp.split(p, 3, axis=-1)
    z = v.astype(F32)
    z = x1.astype(F32) * _long_conv(z, k_f[:, 0], bias[0])
    z = x2.astype(F32) * _long_conv(z, k_f[:, 1], bias[1])
    return z.astype(p.dtype)


def _sgu_mixer(p, ln_g, ln_b, w_s, b_s):
    B, L, _ = p.shape
    u, v = jnp.split(jax.nn.gelu(p, approximate=False), 2, axis=-1)
    v = _layernorm(v, ln_g, ln_b)
    v = v.reshape(B, L // CHUNK, CHUNK, SGU_HEADS, SGU_HEAD_DIM)
    s = jnp.einsum('hpq,bnqhc->bnphc', w_s, v) + b_s.T[None, None, :, :, None]
    return u * s.reshape(B, L, D_SGU)


def setup_inputs(seed: int = 0) -> dict:
    key = jax.random.key(seed)
    ks = jax.random.split(key, 32)

    def nrm(k, shape, scale):
        return jax.random.normal(k, shape, F32) * scale

    return {
        "x": nrm(ks[0], (BATCH, SEQ, D_MODEL), 1.0),
        "norm_mix_g": 1.0 + nrm(ks[1], (DEPTH, D_MODEL), 0.01),
        "w_in": nrm(ks[2], (DEPTH, D_MODEL, D_IN), D_MODEL ** -0.5),
        "short_conv_w": nrm(ks[3], (DEPTH, HYENA_SHORT, 3 * D_HYENA), HYENA_SHORT ** -0.5),
        "short_conv_b": nrm(ks[4], (DEPTH, 3 * D_HYENA), 0.02),
        "filt_w1": nrm(ks[5], (DEPTH, FILTER_EMB, FILTER_ORDER), FILTER_EMB ** -0.5),
        "filt_b1": nrm(ks[6], (DEPTH, FILTER_ORDER), 0.1),
        "filt_w2": nrm(ks[7], (DEPTH, FILTER_ORDER, FILTER_ORDER), FILTER_ORDER ** -0.5),
        "filt_b2": nrm(ks[8], (DEPTH, FILTER_ORDER), 0.1),
        "filt_w3": nrm(ks[9], (DEPTH, FILTER_ORDER, FILTER_ORDER), FILTER_ORDER ** -0.5),
        "filt_b3": nrm(ks[10], (DEPTH, FILTER_ORDER), 0.1),
        "filt_freq": 1.0 + nrm(ks[11], (DEPTH, FILTER_ORDER), 0.01),
        "filt_w4": nrm(ks[12], (DEPTH, FILTER_ORDER, N_DIR * HYENA_ORDER * D_HYENA), FILTER_ORDER ** -0.5),
        "hyena_bias": nrm(ks[13], (DEPTH, HYENA_ORDER, D_HYENA), 0.5),
        "sgu_ln_g": 1.0 + nrm(ks[14], (DEPTH, D_SGU), 0.01),
        "sgu_ln_b": nrm(ks[15], (DEPTH, D_SGU), 0.02),
        "sgu_w_s": nrm(ks[16], (DEPTH, SGU_HEADS, CHUNK, CHUNK), CHUNK ** -0.5),
        "sgu_b_s": 1.0 + nrm(ks[17], (DEPTH, SGU_HEADS, CHUNK), 0.01),
        "w_branch_hyena": nrm(ks[18], (DEPTH, D_HYENA, D_MODEL), D_HYENA ** -0.5),
        "w_branch_sgu": nrm(ks[19], (DEPTH, D_SGU, D_MODEL), D_SGU ** -0.5),
        "w_out": nrm(ks[20], (DEPTH, D_MODEL, D_MODEL), D_MODEL ** -0.5),
        "norm_ffn_g": 1.0 + nrm(ks[21], (DEPTH, D_MODEL), 0.01),
        "w_ffn_in": nrm(ks[22], (DEPTH, D_MODEL, 2 * D_FF), D_MODEL ** -0.5),
        "w_ffn_out": nrm(ks[23], (DEPTH, D_FF, D_MODEL), D_FF ** -0.5),
        "norm_final_g": 1.0 + nrm(ks[24], (D_MODEL,), 0.01),
    }


def reference(x, norm_mix_g, w_in, short_conv_w, short_conv_b, filt_w1, filt_b1, filt_w2, filt_b2,
              filt_w3, filt_b3, filt_freq, filt_w4, hyena_bias, sgu_ln_g, sgu_ln_b, sgu_w_s, sgu_b_s,
              w_branch_hyena, w_branch_sgu, w_out, norm_ffn_g, w_ffn_in, w_ffn_out, norm_final_g):
    L = x.shape[1]
    split_pts = [3 * D_HYENA, 3 * D_HYENA + 2 * D_SGU, 3 * D_HYENA + 2 * D_SGU + D_MODEL]
    for l in range(DEPTH):
        h = _rmsnorm(x, norm_mix_g[l])
        proj = jnp.einsum('bsd,de->bse', h, w_in[l])
        p_hy, p_sgu, g_hy, g_sgu = jnp.split(proj, split_pts, axis=-1)
        k_f = _hyena_filters(L, filt_w1[l], filt_b1[l], filt_w2[l], filt_b2[l],
                             filt_w3[l], filt_b3[l], filt_freq[l], filt_w4[l])
        y_hy = _hyena_mixer(p_hy, short_conv_w[l], short_conv_b[l], k_f, hyena_bias[l])
        y_sgu = _sgu_mixer(p_sgu, sgu_ln_g[l], sgu_ln_b[l], sgu_w_s[l], sgu_b_s[l])
        merged = (jax.nn.sigmoid(g_hy) * jnp.einsum('bsc,cd->bsd', y_hy, w_branch_hyena[l])
                  + jax.nn.sigmoid(g_sgu) * jnp.einsum('bsc,cd->bsd', y_sgu, w_branch_sgu[l]))
        x = x + jnp.einsum('bsd,de->bse', merged, w_out[l])
        h = _rmsnorm(x, norm_ffn_g[l])
        gate, up = jnp.split(jnp.einsum('bsd,df->bsf', h, w_ffn_in[l]), 2, axis=-1)
        x = x + jnp.einsum('bsf,fd->bsd', jax.nn.silu(gate) * up, w_ffn_out[l])
    return _rmsnorm(x, norm_final_g)
```

```python
import math
import numpy as np
import ml_dtypes
import concourse.bass as bass
import concourse.mybir as mybir
from concourse.bass_utils import run_bass_kernel_spmd

F32 = mybir.dt.float32
BF16 = mybir.dt.bfloat16
AF = mybir.ActivationFunctionType
ALU = mybir.AluOpType

L = 2048
D = 2048
NT = 16
DH = 1024
DFF = 5632
NF = 44
CG = 256
NG = DH // CG
TT = 512
EPS = 1e-6
NFT = 17
TWO_PI = 2.0 * math.pi
MAGIC = 12582912.0
SIN_SCALE = 6.28318

DEBUG = {}
HYENA_ONLY = False


class Res:
    __slots__ = ("lw", "rd", "name")

    def __init__(self, name=""):
        self.lw = None
        self.rd = {}
        self.name = name


ALL_DSEMS = []


class DSem:
    def __init__(self, nc, name):
        self.sem = nc.alloc_semaphore(name=name)
        self.count = 0
        ALL_DSEMS.append(self)


class FW:
    def __init__(self, nc):
        self.nc = nc
        self.E = dict(pe=nc.tensor, act=nc.scalar, dve=nc.vector, pool=nc.gpsimd, sp=nc.sync)
        self.sem = {k: nc.alloc_semaphore(name="s_" + k) for k in self.E}
        self.cnt = {k: 0 for k in self.E}
        self.waited = {k: {} for k in self.E}
        self.nwait = 0
        self.ninst = 0

    def _wait(self, eng, tok):
        sem, val, src = tok
        if src == eng and eng == "pe":
            return
        key = id(sem)
        if self.waited[eng].get(key, 0) >= val:
            return
        self.E[eng].wait_ge(sem, val)
        self.waited[eng][key] = val
        self.nwait += 1

    def _deps(self, eng, reads, writes):
        for r in reads:
            if r.lw is not None:
                self._wait(eng, r.lw)
        for w in writes:
            if w.lw is not None:
                self._wait(eng, w.lw)
            for tok in w.rd.values():
                self._wait(eng, tok)

    def _record(self, tok, reads, writes):
        key = id(tok[0])
        for r in reads:
            old = r.rd.get(key)
            if old is None or old[1] < tok[1]:
                r.rd[key] = tok
        for w in writes:
            w.lw = tok
            w.rd = {}

    def op(self, eng, fn, reads=(), writes=(), track=True):
        self._deps(eng, reads, writes)
        ins = fn(self.E[eng])
        self.ninst += 1
        idx = self.cnt[eng] + 1
        if track:
            self.cnt[eng] = idx
            ins.then_inc(self.sem[eng], 1)
        self._record((self.sem[eng], idx, eng), reads, writes)
        return ins

    def barrier(self):
        for e in self.E:
            for f in self.E:
                if f != e and self.cnt[f] > 0:
                    self._wait(e, (self.sem[f], self.cnt[f], f))
            for ds in ALL_DSEMS:
                if ds.count > 0:
                    self._wait(e, (ds.sem, ds.count, "dma"))

    def dma(self, q, out, in_, dsem, reads=(), writes=()):
        self._deps(q, reads, writes)
        ins = self.E[q].dma_start(out=out, in_=in_)
        self.ninst += 1
        dsem.count += 16
        ins.then_inc(dsem.sem, 16)
        self._record((dsem.sem, dsem.count, "dma"), reads, writes)
        return ins


def _bcast_rows(ap_1d_tensor, offset, n, parts=128):
    return bass.AP(ap_1d_tensor, offset, [[0, parts], [1, n]])


def build_program(dbg=()):
    nc = bass.Bass("TRN2", target_bir_lowering=False)
    del ALL_DSEMS[:]
    fw = FW(nc)

    def din(name, shape, dt=F32):
        return nc.dram_tensor(name, list(shape), dt, kind="ExternalInput")

    x_d = din("x", [L, D])
    w_in_d = din("w_in", [D, 9216])
    w_bh_d = din("w_bh", [DH, D])
    w_bs_d = din("w_bs", [DH, D])
    w_out_d = din("w_out", [D, D])
    w_fi_d = din("w_fi", [D, 2 * DFF])
    w_fo_d = din("w_fo", [DFF, D])
    gcol_d = din("gcol", [128, 32])
    gfin_d = din("gfin", [D])
    scw_d = din("scw", [128, 24, 4])
    fmlp_d = din("fmlp", [64, 64 * 3 + 8])
    zT_d = din("zT", [33, L])
    w4_d = din("w4", [64, 4096])
    hbias_d = din("hbias", [2 * DH])
    lngb_d = din("lngb", [2 * DH])
    wsT_d = din("wsT", [128, 8, 128])
    bs_d = din("bs", [8 * 128])
    absd_d = din("absd", [DH])
    tneg_d = din("tneg", [128, 16])
    ident_d = din("ident", [128, 128], BF16)
    gf_d = din("gf", [NFT, 128, 2, 16, 128], BF16)
    gi_d = din("gi", [NFT, 128, 2, L], BF16)
    out_d = nc.dram_tensor("out", [L, D], F32, kind="ExternalOutput")
    dbg_d = {}
    for nm, shp, dt in dbg:
        dbg_d[nm] = nc.dram_tensor(nm, list(shp), dt, kind="ExternalOutput")

    x_ap = x_d.ap()
    out_ap = out_d.ap()
    w_in_v = w_in_d.ap().rearrange("(k p) e -> p k e", p=128)
    w_bh_v = w_bh_d.ap().rearrange("(k p) e -> p k e", p=128)
    w_bs_v = w_bs_d.ap().rearrange("(k p) e -> p k e", p=128)
    w_out_v = w_out_d.ap().rearrange("(k p) e -> p k e", p=128)
    w_fi_v = w_fi_d.ap().rearrange("(k p) e -> p k e", p=128)
    w_fo_v = w_fo_d.ap().rearrange("(k p) e -> p k e", p=128)

    def sb(name, shape, dt):
        return nc.sbuf_tensor('sb_' + name, list(shape), dt)
    ps = nc.psum_tensor
    from contextlib import ExitStack
    with ExitStack() as top:
        def SB(name, shape, dt):
            return top.enter_context(sb(name, list(shape), dt))

        ident = SB("ident", [128, 128], BF16)
        gcol = SB("gcol", [128, 32], F32)
        scw = SB("scw", [128, 24, 4], F32)
        fmlp = SB("fmlp", [64, 200], F32)
        tneg = SB("tneg", [128, 16], F32)
        fsc = SB("fsc", [64, 4], F32)
        ones_bf = SB("ones_bf", [128, 128], BF16)
        epsc = SB("epsc", [128, 1], F32)
        cs_const = DSem(nc, "cs_const")
        R_const = Res("const")
        for (t, src) in ((ident, ident_d.ap()), (gcol, gcol_d.ap()), (scw, scw_d.ap()),
                         (fmlp, fmlp_d.ap()), (tneg, tneg_d.ap())):
            fw.dma("sp", t[:], src, cs_const, writes=[R_const])
        R_misc = Res("misc")
        fw.op("pool", lambda e: e.memset(ones_bf[:], 1.0), writes=[R_misc])
        fw.op("pool", lambda e: e.memset(epsc[:], EPS), writes=[R_misc])
        fw.op("dve", lambda e: e.tensor_scalar(out=fsc[:, 0:1], in0=fmlp[:, 195:196], scalar1=1.0 / TWO_PI,
                                                scalar2=None, op0=ALU.mult), reads=[R_const], writes=[R_misc])
        for k in range(3):
            fw.op("dve", lambda e, k=k: e.tensor_scalar(out=fsc[:, k + 1:k + 2], in0=fmlp[:, 192 + k:193 + k],
                                                         scalar1=fsc[:, 0:1], scalar2=None, op0=ALU.mult), reads=[R_const, R_misc], writes=[R_misc])

        psum = top.enter_context(ps("psum", [128, 8, 512], F32))
        R_bank = [Res("bank%d" % i) for i in range(8)]

        def pbf(b0):
            return psum[:, b0:b0 + 2, :].bitcast(BF16).rearrange("p b (k c) -> p (b k) c", c=128)

        dbg_sems = []

        def dump(name, ap, R):
            if name in dbg_d:
                ds = DSem(nc, "dbg_" + name)
                fw.dma("sp", dbg_d[name].ap(), ap, ds, reads=R)
                dbg_sems.append(ds)

        def TT_(eng, out, in0, in1, op, R, W):
            return fw.op(eng, lambda e: e.tensor_tensor(out=out, in0=in0, in1=in1, op=op), reads=R, writes=W)

        def TS_(eng, out, in0, s1, s2, op0, op1, R, W):
            if op1 is None:
                return fw.op(eng, lambda e: e.tensor_scalar(out=out, in0=in0, scalar1=s1, scalar2=None, op0=op0),
                             reads=R, writes=W)
            return fw.op(eng, lambda e: e.tensor_scalar(out=out, in0=in0, scalar1=s1, scalar2=s2, op0=op0, op1=op1),
                         reads=R, writes=W)

        def STT_(eng, out, in0, sc, in1, op0, op1, R, W):
            return fw.op(eng, lambda e: e.scalar_tensor_tensor(out=out, in0=in0, scalar=sc, in1=in1, op0=op0, op1=op1),
                         reads=R, writes=W)

        def ACT_(out, in_, func, R, W, scale=None, bias=None, accum=None):
            kw = {}
            if scale is not None:
                kw["scale"] = scale
            if bias is not None:
                kw["bias"] = bias
            if accum is not None:
                kw["accum_out"] = accum
            return fw.op("act", lambda e: e.activation(out=out, in_=in_, func=func, **kw), reads=R, writes=W)

        def MM_(out, lhsT, rhs, start, stop, R, W, track):
            return fw.op("pe", lambda e: e.matmul(out, lhsT, rhs, start=start, stop=stop), reads=R, writes=W,
                         track=track)

        def TR_(out, in_, R, W, track):
            return fw.op("pe", lambda e: e.transpose(out, in_, ident[:]), reads=R + [R_const], writes=W, track=track)

        x2T = SB("x2T", [128, 8, L], BF16)
        R_x2T = [Res("x2T%d" % i) for i in range(8)]

        WS = []
        R_ws = []
        S_ws = [DSem(nc, "ws%d" % i) for i in range(3)]
        ws_rot = [0]

        def next_ws():
            i = ws_rot[0] % 2
            ws_rot[0] += 1
            return i

        scr_d = nc.dram_tensor("scr_vx1", [16, 128, L], BF16, kind="Internal")
        R_scr = [Res() for _ in range(16)]
        S_scr = DSem(nc, "scr")

        def norm_transpose(xt_ap, R_x, dstT, R_dst, tcol, gofs, scr, R_scr, bank0):
            xn, ss, rstd = scr
            ACT_(xn[:], xt_ap, AF.Square, [R_x], [R_scr], accum=ss[:, 0:1])
            ACT_(rstd[:, 0:1], ss[:, 0:1], AF.Sqrt, [R_scr, R_misc], [R_scr], scale=1.0 / D, bias=epsc[:, 0:1])
            fw.op("dve", lambda e: e.reciprocal(out=rstd[:, 0:1], in_=rstd[:, 0:1]), reads=[R_scr], writes=[R_scr])
            ACT_(xn[:], xt_ap, AF.Copy, [R_x, R_scr], [R_scr], scale=rstd[:, 0:1])
            pv = pbf(bank0)
            Rb = [R_bank[bank0], R_bank[bank0 + 1]]
            for k in range(16):
                TR_(pv[:, k, :], xn[:, k * 128:(k + 1) * 128], [R_scr], Rb, track=(k == 15))
            for k in range(16):
                if k % 2 == 0:
                    ACT_(dstT[:, k, tcol:tcol + 128], pv[:, k, :], AF.Copy, Rb + [R_const], [R_dst],
                         scale=gcol[:, gofs + k:gofs + k + 1])
                else:
                    TS_("dve", dstT[:, k, tcol:tcol + 128], pv[:, k, :], gcol[:, gofs + k:gofs + k + 1], None,
                        ALU.mult, None, Rb + [R_const], [R_dst])

        sh = top.enter_context(ExitStack())
        h3T = sh.enter_context(sb("h3T", [64, L], F32))
        R_h3T = Res("h3T")
        with ExitStack() as st:
            zT = st.enter_context(sb("zT", [33, L], F32))
            hA = st.enter_context(sb("hA", [64, L], F32))
            hB = st.enter_context(sb("hB", [64, L], F32))
            utmp = st.enter_context(sb("utmp", [64, 2, 512], F32))
            rtmp = st.enter_context(sb("rtmp", [64, 2, 512], F32))
            R_zT, R_hA, R_hB = Res(), Res(), Res()
            R_ut = [Res(), Res()]
            R_rt = [Res(), Res()]
            S_zT = DSem(nc, "zT")
            fw.dma("sp", zT[:], zT_d.ap(), S_zT, writes=[R_zT])
            layers = [(zT, R_zT, 33, 0, hA, R_hA), (hA, R_hA, 64, 64, hB, R_hB), (hB, R_hB, 64, 128, h3T, R_h3T)]
            it = 0
            for li, (src, R_src, K, wo, dst, R_dst) in enumerate(layers):
                for tb in range(4):
                    bk = it % 2
                    it += 1
                    MM_(psum[0:64, bk, :], fmlp[0:K, wo:wo + 64], src[0:K, tb * 512:(tb + 1) * 512], True, True,
                        [R_const, R_src], [R_bank[bk]], True)
                    TS_("dve", utmp[:, bk, :], psum[0:64, bk, :], fsc[:, 0:1], fsc[:, li + 1:li + 2], ALU.mult,
                        ALU.add, [R_bank[bk], R_misc], [R_ut[bk]])
                    TS_("dve", rtmp[:, bk, :], utmp[:, bk, :], MAGIC, MAGIC, ALU.add, ALU.subtract, [R_ut[bk]],
                        [R_rt[bk]])
                    TT_("dve", utmp[:, bk, :], utmp[:, bk, :], rtmp[:, bk, :], ALU.subtract, [R_ut[bk], R_rt[bk]],
                        [R_ut[bk]])
                    ACT_(dst[:, tb * 512:(tb + 1) * 512], utmp[:, bk, :], AF.Sin, [R_ut[bk]], [R_dst],
                         scale=SIN_SCALE)
        dump("h3T", h3T[:], [R_h3T])
        fw.barrier()

        if True:

            with ExitStack() as s12:
                hT = s12.enter_context(sb("hT", [128, 16, L], BF16))
                R_hT = Res("hT")
                xin = s12.enter_context(sb("xin", [128, 2, D], F32))
                R_xin = [Res(), Res()]
                S_xin = [DSem(nc, "xin0"), DSem(nc, "xin1")]
                WS[:] = [s12.enter_context(sb("wsh%d" % i, [128, 16, 512], BF16)) for i in range(2)]
                R_ws[:] = [Res(), Res()]
                stg = s12.enter_context(sb("stg", [128, 2, L], BF16))
                R_stg = [Res(), Res()]
                xn_ = s12.enter_context(sb("xn0", [128, D], BF16))
                xn2 = [xn_, xn_]
                ss2 = [s12.enter_context(sb("ss%d" % i, [128, 1], F32)) for i in range(2)]
                rs2 = [s12.enter_context(sb("rs%d" % i, [128, 1], F32)) for i in range(2)]
                R_one = Res()
                R_scr2 = [R_one, R_one]
                for tt in range(NT):
                    i = tt % 2
                    fw.dma("sp", xin[:, i, :], x_ap[tt * 128:(tt + 1) * 128, :], S_xin[i], writes=[R_xin[i]])
                    norm_transpose(xin[:, i, :], R_xin[i], hT, R_hT, tt * 128, 0, (xn2[i], ss2[i], rs2[i]),
                                   R_scr2[i], 4 + 2 * i)
                dump("hT", hT[:], [R_hT])

                pstage = s12.enter_context(sb("pstage", [128, 2, L + 2], F32))
                R_pst = [Res(), Res()]
                tmpc = s12.enter_context(sb("tmpc", [128, 1, L], F32))
                R_tmpc = [Res()]
                for i in range(2):
                    fw.op("pool", lambda e, i=i: e.memset(pstage[:, i, 0:1], 0.0), writes=[R_pst[i]])
                    fw.op("pool", lambda e, i=i: e.memset(pstage[:, i, L + 1:L + 2], 0.0), writes=[R_pst[i]])
                grp = 0
                for eb in range(6):
                    wi = next_ws()
                    fw.dma("pool", WS[wi][:, :, :], w_in_v[:, :, eb * 512:(eb + 1) * 512], S_ws[wi],
                           writes=[R_ws[wi]])
                    for ej in range(4):
                        et = eb * 4 + ej
                        si = et % 2
                        for th in range(2):
                            b0 = (grp % 4) * 2
                            grp += 1
                            for tb in range(2):
                                for k in range(16):
                                    MM_(psum[:, b0 + tb, :], WS[wi][:, k, ej * 128:(ej + 1) * 128],
                                        hT[:, k, th * 1024 + tb * 512: th * 1024 + (tb + 1) * 512],
                                        k == 0, k == 15, [R_ws[wi], R_hT], [R_bank[b0 + tb]], k == 15)
                            ACT_(pstage[:, si, 1 + th * 1024: 1 + (th + 1) * 1024],
                                 psum[:, b0:b0 + 2, :].rearrange("p b c -> p (b c)"), AF.Copy,
                                 [R_bank[b0], R_bank[b0 + 1]], [R_pst[si]])
                        tci = 0
                        if et < 16:
                            dstap = stg[:, si, :]
                            Rd = R_stg[si]
                        else:
                            dstap = x2T[:, et - 16, :]
                            Rd = R_x2T[et - 16]
                        TS_("dve", tmpc[:, tci, :], pstage[:, si, 1:L + 1], scw[:, et, 1:2], scw[:, et, 3:4], ALU.mult,
                            ALU.add, [R_pst[si], R_const], [R_tmpc[tci]])
                        STT_("dve", tmpc[:, tci, :], pstage[:, si, 0:L], scw[:, et, 0:1], tmpc[:, tci, :], ALU.mult,
                             ALU.add, [R_pst[si], R_tmpc[tci], R_const], [R_tmpc[tci]])
                        STT_("dve", dstap, pstage[:, si, 2:L + 2], scw[:, et, 2:3], tmpc[:, tci, :], ALU.mult,
                             ALU.add, [R_pst[si], R_tmpc[tci], R_const], [Rd])
                        if et < 16:
                            fw.dma("sp", scr_d.ap()[et], stg[:, si, :], S_scr, reads=[R_stg[si]], writes=[R_scr[et]])
                            if et == 0:
                                dump("vT0", stg[:, si, :], [R_stg[si]])
            for r_ in R_scr:
                r_.lw = (S_scr.sem, S_scr.count, "dma")
            dump("x2T_pre", x2T[:], R_x2T)
            fw.barrier()

            with ExitStack() as s3:
                DHt = s3.enter_context(sb("DHt", [128, 16, 3, CG], BF16))
                R_DHd = Res("DHdata")
                R_DHf = Res("DHfilt")
                Yr = s3.enter_context(sb("Yr", [128, NFT, CG], BF16))
                Ys = s3.enter_context(sb("Ys", [128, NFT, CG], BF16))
                R_Y = [Res() for _ in range(NFT)]
                z1T = s3.enter_context(sb("z1T", [128, 2, L], BF16))
                R_z1T = [Res(), Res()]
                vg = s3.enter_context(sb("vg", [128, 2, L], BF16))
                x1g = s3.enter_context(sb("x1g", [128, 2, L], BF16))
                R_vg = [Res(), Res()]
                R_x1g = [Res(), Res()]
                S_vg = DSem(nc, "vg")
                S_x1g = DSem(nc, "x1g")
                tabs = [s3.enter_context(sb("tab%d" % i, [128, 2, L], BF16)) for i in range(3)]
                R_tab = [Res() for _ in range(3)]
                S_tab = [DSem(nc, "tab%d" % i) for i in range(3)]
                tab_rot = [0]
                w4g = s3.enter_context(sb("w4g", [64, 2, 2, CG], F32))
                R_w4g = [Res(), Res()]
                S_w4g = [DSem(nc, "w4g0"), DSem(nc, "w4g1")]
                hb_bc = s3.enter_context(sb("hb_bc", [128, 2, CG], F32))
                absd_bc = s3.enter_context(sb("absd_bc", [128, CG], F32))
                R_gc = Res("groupconst")
                S_gc = DSem(nc, "groupconst")
                dec = s3.enter_context(sb("dec", [128, 2, CG], F32))
                R_dec = [Res(), Res()]
                hdec = s3.enter_context(sb("hdec", [128, 2, 2, CG], F32))
                R_hdec = [Res(), Res()]
                habs = s3.enter_context(sb("habs", [128, 2, 2, CG], BF16))
                R_habs = [Res(), Res()]
                rn = s3.enter_context(sb("rn", [128, 2, CG], F32))
                R_rn = [Res(), Res()]
                ytmp = s3.enter_context(sb("ytmp", [128, 2, 6, CG], F32))
                R_yt = [[Res() for _ in range(6)] for _ in range(2)]

                for g in range(NG):
                    c0 = g * CG
                    for ct in range(2):
                        fw.dma("sp", vg[:, ct, :], scr_d.ap()[g * 2 + ct], S_vg, reads=[R_scr[g * 2 + ct]],
                               writes=[R_vg[ct]])
                        fw.dma("sp", x1g[:, ct, :], scr_d.ap()[8 + g * 2 + ct], S_x1g, reads=[R_scr[8 + g * 2 + ct]],
                               writes=[R_x1g[ct]])
                    for ct in range(2):
                        R_vg[ct].lw = (S_vg.sem, S_vg.count, "dma")
                        R_x1g[ct].lw = (S_x1g.sem, S_x1g.count, "dma")
                    fw.dma("sp", absd_bc[:], _bcast_rows(absd_d, c0, CG), S_gc, writes=[R_gc])
                    for o in range(2):
                        fw.dma("sp", hb_bc[:, o, :], _bcast_rows(hbias_d, o * DH + c0, CG), S_gc, writes=[R_gc])
                    for o in range(2):
                        for d in range(2):
                            col = d * 2048 + o * DH + c0
                            fw.dma("sp", w4g[:, o, d, :], w4_d.ap()[:, col:col + CG], S_w4g[o], writes=[R_w4g[o]])
                        nb = 6 + o
                        for nt in range(NT):
                            i = nt % 2
                            fb = 4 + i
                            MM_(psum[:, fb, :], h3T[0:64, nt * 128:(nt + 1) * 128],
                                w4g[0:64, o, :, :].rearrange("p d c -> p (d c)"), True, True,
                                [R_h3T, R_w4g[o]], [R_bank[fb]], True)
                            ACT_(dec[:, i, :], absd_bc[:], AF.Exp, [R_gc, R_const], [R_dec[i]],
                                 scale=tneg[:, nt:nt + 1])
                            for d in range(2):
                                TT_("dve", hdec[:, i, d, :], psum[:, fb, d * CG:(d + 1) * CG], dec[:, i, :], ALU.mult,
                                    [R_bank[fb], R_dec[i]], [R_hdec[i]])
                            if nt == 0:
                                fw.op("pool", lambda e, i=i: e.memset(hdec[0:1, i, 1, :], 0.0), writes=[R_hdec[i]])
                            ACT_(habs[:, i, :, :], hdec[:, i, :, :], AF.Abs, [R_hdec[i]], [R_habs[i]])
                            for d in range(2):
                                MM_(psum[:, nb, 0:CG], ones_bf[:], habs[:, i, d, :], (nt == 0 and d == 0),
                                    (nt == NT - 1 and d == 1), [R_misc, R_habs[i]], [R_bank[nb]],
                                    (nt == NT - 1 and d == 1))
                            TT_("pool", DHt[:, nt, 1, :], hdec[:, i, 0, :], hdec[:, i, 1, :], ALU.add, [R_hdec[i]],
                                [R_DHf])
                            TT_("pool", DHt[:, nt, 2, :], hdec[:, i, 1, :], hdec[:, i, 0, :], ALU.subtract,
                                [R_hdec[i]], [R_DHf])
                        fw.op("dve", lambda e, o=o, nb=nb: e.reciprocal(out=rn[:, o, :], in_=psum[:, nb, 0:CG]),
                              reads=[R_bank[nb]], writes=[R_rn[o]])
                        if g == 0:
                            dump("hs%d" % o, DHt[:, :, 1, :], [R_DHf])
                            dump("rn%d" % o, rn[:, o, :], [R_rn[o]])

                        for ct in range(2):
                            src = vg[:, ct, :] if o == 0 else z1T[:, ct, :]
                            R_src = R_vg[ct] if o == 0 else R_z1T[ct]
                            b0 = 2 * ct
                            pv = pbf(b0)
                            Rb = [R_bank[b0], R_bank[b0 + 1]]
                            for tt in range(NT):
                                TR_(pv[:, tt, :], src[:, tt * 128:(tt + 1) * 128], [R_src], Rb, track=(tt == NT - 1))
                            ACT_(DHt[:, :, 0, ct * 128:(ct + 1) * 128], pv[:, :, :], AF.Copy, Rb, [R_DHd])

                        for ft in range(NFT):
                            ti = tab_rot[0] % 3
                            tab_rot[0] += 1
                            tf = tabs[ti][:, :, :].rearrange("p s (a f) -> p s a f", f=128)
                            fw.dma("sp", tabs[ti][:, :, :], gf_d.ap()[ft].rearrange("p s a f -> p s (a f)"), S_tab[ti],
                                   writes=[R_tab[ti]])
                            bc = (ft % 2) * 2
                            bs_ = bc + 1
                            for a in range(16):
                                MM_(psum[:, bc, :], tf[:, 0, a, :], DHt[:, a, 0:2, :].rearrange("p b c -> p (b c)"),
                                    a == 0, a == 15, [R_tab[ti], R_DHd, R_DHf], [R_bank[bc]], a == 15)
                            for a in range(16):
                                MM_(psum[:, bs_, :].rearrange("p (b c) -> p b c", b=2), tf[:, 1, a, :],
                                    DHt[:, a, 0:3:2, :], a == 0, a == 15, [R_tab[ti], R_DHd, R_DHf], [R_bank[bs_]],
                                    a == 15)
                            yi = ft % 2
                            kr, ki, a1, b1, c1, d1 = [ytmp[:, yi, j, :] for j in range(6)]
                            Rk = R_yt[yi]
                            ZC = psum[:, bc, 0:CG]
                            KC = psum[:, bc, CG:2 * CG]
                            ZS = psum[:, bs_, 0:CG]
                            KS = psum[:, bs_, CG:2 * CG]
                            TT_("dve", kr, KC, rn[:, o, :], ALU.mult, [R_bank[bc], R_rn[o]], [Rk[0]])
                            TT_("pool", kr, kr, hb_bc[:, o, :], ALU.add, [Rk[0], R_gc], [Rk[0]])
                            TT_("dve", ki, KS, rn[:, o, :], ALU.mult, [R_bank[bs_], R_rn[o]], [Rk[1]])
                            TT_("dve", a1, ZC, kr, ALU.mult, [R_bank[bc], Rk[0]], [Rk[2]])
                            TT_("dve", b1, ZS, ki, ALU.mult, [R_bank[bs_], Rk[1]], [Rk[3]])
                            TT_("pool", Yr[:, ft, :], a1, b1, ALU.add, [Rk[2], Rk[3]], [R_Y[ft]])
                            TT_("dve", c1, ZS, kr, ALU.mult, [R_bank[bs_], Rk[0]], [Rk[4]])
                            TT_("dve", d1, ZC, ki, ALU.mult, [R_bank[bc], Rk[1]], [Rk[5]])
                            TT_("pool", Ys[:, ft, :], c1, d1, ALU.subtract, [Rk[4], Rk[5]], [R_Y[ft]])
                        if g == 0:
                            dump("Yr%d" % o, Yr[:], R_Y)
                            dump("Ys%d" % o, Ys[:], R_Y)

                        for a in range(NFT):
                            ti = tab_rot[0] % 3
                            tab_rot[0] += 1
                            fw.dma("sp", tabs[ti][:, :, :], gi_d.ap()[a], S_tab[ti], writes=[R_tab[ti]])
                            for ct in range(2):
                                for tb in range(4):
                                    bk = ct * 4 + tb
                                    MM_(psum[:, bk, :], Yr[:, a, ct * 128:(ct + 1) * 128],
                                        tabs[ti][:, 0, tb * 512:(tb + 1) * 512], a == 0, False,
                                        [R_tab[ti], R_Y[a]], [R_bank[bk]], False)
                                    MM_(psum[:, bk, :], Ys[:, a, ct * 128:(ct + 1) * 128],
                                        tabs[ti][:, 1, tb * 512:(tb + 1) * 512], False, a == NFT - 1,
                                        [R_tab[ti], R_Y[a]], [R_bank[bk]], (a == NFT - 1) or (ct == 1 and tb == 3))
                        for ct in range(2):
                            for tb in range(4):
                                bk = ct * 4 + tb
                                cs = slice(tb * 512, (tb + 1) * 512)
                                if o == 0:
                                    TT_("dve", z1T[:, ct, cs], psum[:, bk, :], x1g[:, ct, cs], ALU.mult,
                                        [R_bank[bk], R_x1g[ct]], [R_z1T[ct]])
                                else:
                                    TT_("dve", x2T[:, g * 2 + ct, cs], psum[:, bk, :], x2T[:, g * 2 + ct, cs], ALU.mult,
                                        [R_bank[bk], R_x2T[g * 2 + ct]], [R_x2T[g * 2 + ct]])
                        if g == 0 and o == 0:
                            dump("z1T", z1T[:], R_z1T)
        sh.close()
        dump("yhT", x2T[:], R_x2T)
        fw.barrier()

        if not HYENA_ONLY:
            with ExitStack() as s4:
                x1b = s4.enter_context(sb("x1b", [128, 4, D], F32))
                R_x1b = [Res() for _ in range(4)]
                S_x1b = [DSem(nc, "x1b%d" % i) for i in range(4)]
                S_out = [DSem(nc, "outs%d" % i) for i in range(4)]
                hTt = s4.enter_context(sb("hTt", [128, 16, TT], BF16))
                R_hTt = Res("hTt")
                WS[:] = [s4.enter_context(sb("wst%d" % i, [128, 16, 512], BF16)) for i in range(2)]
                WS.append(s4.enter_context(sb("wst2", [128, 8, 512], BF16)))
                R_ws[:] = [Res(), Res(), Res()]
                xn_ = s4.enter_context(sb("xnt0", [128, D], BF16))
                xn2 = [xn_, xn_]
                ss2 = [s4.enter_context(sb("sst%d" % i, [128, 1], F32)) for i in range(2)]
                rs2 = [s4.enter_context(sb("rst%d" % i, [128, 1], F32)) for i in range(2)]
                R_one = Res()
                R_scr2 = [R_one, R_one]
                gfin_bc = s4.enter_context(sb("gfin_bc", [128, D], F32))
                lng_bc = s4.enter_context(sb("lng_bc", [128, 2, DH], F32))
                bs_bc = s4.enter_context(sb("bs_bc", [128, 8, 128], F32))
                wsT = s4.enter_context(sb("wsT", [128, 8, 128], BF16))
                R_tc = Res("tconst")
                S_tc = DSem(nc, "tconst")
                fw.dma("sp", gfin_bc[:], _bcast_rows(gfin_d, 0, D), S_tc, writes=[R_tc])
                fw.dma("sp", lng_bc[:, :, :].rearrange("p a c -> p (a c)"), _bcast_rows(lngb_d, 0, 2 * DH), S_tc,
                       writes=[R_tc])
                fw.dma("sp", bs_bc[:, :, :].rearrange("p a c -> p (a c)"), _bcast_rows(bs_d, 0, 1024), S_tc,
                       writes=[R_tc])
                S_tc2 = DSem(nc, "tconst2")
                R_tc2 = Res("tconst2")
                fw.dma("pool", wsT[:], wsT_d.ap(), S_tc2, writes=[R_tc2])
                uT = s4.enter_context(sb("uT", [128, 8, TT], BF16))
                R_uT = [Res() for _ in range(8)]
                mT = s4.enter_context(sb("mT", [128, 16, TT], BF16))
                R_mT = [Res() for _ in range(16)]
                v2g = s4.enter_context(sb("v2g", [128, 1, DH], F32))
                R_v2g = [Res()] * 2
                v2n = s4.enter_context(sb("v2n", [128, 1, DH], BF16))
                R_v2n = [Res()] * 2
                lnst = s4.enter_context(sb("lnst", [128, 1, 8], F32))
                sgt = s4.enter_context(sb("sgt", [128, 1, 8, 128], F32))
                R_sgt = [Res()] * 2
                etmp = s4.enter_context(sb("etmp", [128, 2, TT], F32))
                R_et = [Res() for _ in range(2)]
                actT = s4.enter_context(sb("actT", [128, 1, 11, TT], BF16))
                R_actT = [[Res() for _ in range(11)]] * 2
                bank_rot = [0]

                def nb1():
                    b = bank_rot[0] % 8
                    bank_rot[0] += 1
                    return b

                def nb2():
                    if bank_rot[0] % 2:
                        bank_rot[0] += 1
                    b = bank_rot[0] % 8
                    bank_rot[0] += 2
                    return b

                et_rot = [0]

                def net():
                    i = et_rot[0] % 2
                    et_rot[0] += 1
                    return i

                for J in range(L // TT):
                    t0 = J * TT
                    for tt in range(4):
                        fw.dma("sp", x1b[:, tt, :], x_ap[t0 + tt * 128:t0 + (tt + 1) * 128, :], S_x1b[tt],
                               writes=[R_x1b[tt]])
                    for tt in range(4):
                        i = tt % 2
                        norm_transpose(x1b[:, tt, :], R_x1b[tt], hTt, R_hTt, tt * 128, 0,
                                       (xn2[i], ss2[i], rs2[i]), R_scr2[i], nb2())
                    if J == 0:
                        dump("hTt", hTt[:], [R_hTt])
                    for eb in range(2):
                        wi = next_ws()
                        fw.dma("pool", WS[wi][:, :, :], w_in_v[:, :, 3072 + eb * 512:3072 + (eb + 1) * 512], S_ws[wi],
                               writes=[R_ws[wi]])
                        for ej in range(4):
                            et = eb * 4 + ej
                            bk = nb1()
                            for k in range(16):
                                MM_(psum[:, bk, :], WS[wi][:, k, ej * 128:(ej + 1) * 128], hTt[:, k, :], k == 0,
                                    k == 15, [R_ws[wi], R_hTt], [R_bank[bk]], k == 15)
                            ACT_(uT[:, et, :], psum[:, bk, :], AF.Gelu, [R_bank[bk]], [R_uT[et]])
                    wv = []
                    for eb in range(2):
                        wi = next_ws()
                        fw.dma("pool", WS[wi][:, :, :], w_in_v[:, :, 4096 + eb * 512:4096 + (eb + 1) * 512], S_ws[wi],
                               writes=[R_ws[wi]])
                        wv.append(wi)
                    for tt in range(4):
                        i = 0
                        for eb in range(2):
                            wi = wv[eb]
                            bk = nb1()
                            for k in range(16):
                                MM_(psum[:, bk, :], hTt[:, k, tt * 128:(tt + 1) * 128], WS[wi][:, k, :], k == 0,
                                    k == 15, [R_ws[wi], R_hTt], [R_bank[bk]], k == 15)
                            ACT_(v2g[:, i, eb * 512:(eb + 1) * 512], psum[:, bk, :], AF.Gelu, [R_bank[bk]],
                                 [R_v2g[i]], accum=lnst[:, i, eb:eb + 1])
                        st_ = lnst[:, i, :]
                        TS_("dve", st_[:, 2:3], st_[:, 0:1], st_[:, 1:2], -1.0 / DH, ALU.add, ALU.mult, [R_v2g[i]],
                            [R_v2g[i]])
                        ACT_(v2n[:, i, :], v2g[:, i, :], AF.Square, [R_v2g[i]], [R_v2n[i]], bias=st_[:, 2:3],
                             accum=st_[:, 3:4])
                        ACT_(st_[:, 4:5], st_[:, 3:4], AF.Sqrt, [R_v2n[i], R_v2g[i], R_misc], [R_v2g[i]], scale=1.0 / DH,
                             bias=epsc[:, 0:1])
                        fw.op("dve", lambda e, st_=st_: e.reciprocal(out=st_[:, 4:5], in_=st_[:, 4:5]),
                              reads=[R_v2g[i]], writes=[R_v2g[i]])
                        TS_("dve", v2g[:, i, :], v2g[:, i, :], st_[:, 2:3], st_[:, 4:5], ALU.add, ALU.mult,
                            [R_v2g[i]], [R_v2g[i]])
                        TT_("pool", v2g[:, i, :], v2g[:, i, :], lng_bc[:, 0, :], ALU.mult, [R_v2g[i], R_tc],
                            [R_v2g[i]])
                        TT_("pool", v2n[:, i, :], v2g[:, i, :], lng_bc[:, 1, :], ALU.add, [R_v2g[i], R_tc],
                            [R_v2n[i]])
                        if J == 0 and tt == 0:
                            dump("v2n", v2n[:, 0, :], [R_v2n[0]])
                        b0 = nb2()
                        for h in range(8):
                            MM_(psum[:, b0 + h // 4, (h % 4) * 128:(h % 4 + 1) * 128], v2n[:, i, h * 128:(h + 1) * 128],
                                wsT[:, h, :], True, True, [R_v2n[i], R_tc2], [R_bank[b0], R_bank[b0 + 1]], h == 7)
                        TT_("dve", sgt[:, i, :, :], psum[:, b0:b0 + 2, :].rearrange("p b (h c) -> p (b h) c", c=128),
                            bs_bc[:, :, :], ALU.add, [R_bank[b0], R_bank[b0 + 1], R_tc], [R_sgt[i]])
                        TT_("pool", uT[:, :, tt * 128:(tt + 1) * 128], sgt[:, i, :, :], uT[:, :, tt * 128:(tt + 1) * 128],
                            ALU.mult, [R_sgt[i]] + R_uT, R_uT)
                    if J == 0:
                        dump("ysT", uT[:], R_uT)
                    for br in range(2):
                        gofs = 5120 + br * 2048
                        wb_v = w_bh_v if br == 0 else w_bs_v
                        for db in range(4):
                            wi = next_ws()
                            fw.dma("pool", WS[wi][:, :, :], w_in_v[:, :, gofs + db * 512:gofs + (db + 1) * 512],
                                   S_ws[wi], writes=[R_ws[wi]])
                            wj = 2
                            fw.dma("pool", WS[wj][:, 0:8, :], wb_v[:, :, db * 512:(db + 1) * 512], S_ws[wj],
                                   writes=[R_ws[wj]])
                            for dj in range(4):
                                dt_ = db * 4 + dj
                                bA = nb1()
                                for k in range(16):
                                    MM_(psum[:, bA, :], WS[wi][:, k, dj * 128:(dj + 1) * 128], hTt[:, k, :], k == 0,
                                        k == 15, [R_ws[wi], R_hTt], [R_bank[bA]], k == 15)
                                bC = nb1()
                                for c in range(8):
                                    if br == 0:
                                        rhs = x2T[:, c, t0:t0 + TT]
                                        Rr = [R_x2T[c]]
                                    else:
                                        rhs = uT[:, c, :]
                                        Rr = [R_uT[c]]
                                    MM_(psum[:, bC, :], WS[wj][:, c, dj * 128:(dj + 1) * 128], rhs, c == 0, c == 7,
                                        [R_ws[wj]] + Rr, [R_bank[bC]], c == 7)
                                e1 = net()
                                ACT_(etmp[:, e1, :], psum[:, bA, :], AF.Sigmoid, [R_bank[bA]], [R_et[e1]])
                                if br == 0:
                                    TT_("dve", mT[:, dt_, :], psum[:, bC, :], etmp[:, e1, :], ALU.mult,
                                        [R_bank[bC], R_et[e1]], [R_mT[dt_]])
                                else:
                                    TT_("dve", etmp[:, e1, :], psum[:, bC, :], etmp[:, e1, :], ALU.mult,
                                        [R_bank[bC], R_et[e1]], [R_et[e1]])
                                    TT_("pool", mT[:, dt_, :], mT[:, dt_, :], etmp[:, e1, :], ALU.add,
                                        [R_et[e1], R_mT[dt_]], [R_mT[dt_]])
                    if J == 0:
                        dump("mT", mT[:], R_mT)
                    for db in range(4):
                        wi = next_ws()
                        fw.dma("pool", WS[wi][:, :, :], w_out_v[:, :, db * 512:(db + 1) * 512], S_ws[wi],
                               writes=[R_ws[wi]])
                        for tt in range(4):
                            bk = nb1()
                            for k in range(16):
                                MM_(psum[:, bk, :], mT[:, k, tt * 128:(tt + 1) * 128], WS[wi][:, k, :], k == 0, k == 15,
                                    [R_ws[wi], R_mT[k]], [R_bank[bk]], k == 15)
                            TT_("dve", x1b[:, tt, db * 512:(db + 1) * 512], psum[:, bk, :],
                                x1b[:, tt, db * 512:(db + 1) * 512], ALU.add, [R_bank[bk], R_x1b[tt]], [R_x1b[tt]])
                    if J == 0:
                        dump("xmid", x1b[:, 0, :], [R_x1b[0]])
                    for tt in range(4):
                        i = tt % 2
                        norm_transpose(x1b[:, tt, :], R_x1b[tt], hTt, R_hTt, tt * 128, 16,
                                       (xn2[i], ss2[i], rs2[i]), R_scr2[i], nb2())
                    for s in range(4):
                        ai = 0
                        f = 0
                        while f < 11:
                            nfl = min(2, 11 - f)
                            fg = s * 11 + f
                            wi = next_ws()
                            fw.dma("pool", WS[wi][:, :, 0:nfl * 128], w_fi_v[:, :, fg * 128:(fg + nfl) * 128], S_ws[wi],
                                   writes=[R_ws[wi]])
                            fw.dma("pool", WS[wi][:, :, 256:256 + nfl * 128],
                                   w_fi_v[:, :, DFF + fg * 128:DFF + (fg + nfl) * 128], S_ws[wi], writes=[R_ws[wi]])
                            for j in range(nfl):
                                bG = nb1()
                                for k in range(16):
                                    MM_(psum[:, bG, :], WS[wi][:, k, j * 128:(j + 1) * 128], hTt[:, k, :], k == 0,
                                        k == 15, [R_ws[wi], R_hTt], [R_bank[bG]], k == 15)
                                bU = nb1()
                                for k in range(16):
                                    MM_(psum[:, bU, :], WS[wi][:, k, 256 + j * 128:256 + (j + 1) * 128], hTt[:, k, :],
                                        k == 0, k == 15, [R_ws[wi], R_hTt], [R_bank[bU]], k == 15)
                                e1 = net()
                                ACT_(etmp[:, e1, :], psum[:, bG, :], AF.Silu, [R_bank[bG]], [R_et[e1]])
                                TT_("dve", actT[:, ai, f + j, :], psum[:, bU, :], etmp[:, e1, :], ALU.mult,
                                    [R_bank[bU], R_et[e1]], [R_actT[ai][f + j]])
                            f += nfl
                        for db in range(4):
                            wi = next_ws()
                            fw.dma("pool", WS[wi][:, 0:11, :], w_fo_v[:, s * 11:(s + 1) * 11, db * 512:(db + 1) * 512],
                                   S_ws[wi], writes=[R_ws[wi]])
                            for tt in range(4):
                                bk = nb1()
                                for j in range(11):
                                    MM_(psum[:, bk, :], actT[:, ai, j, tt * 128:(tt + 1) * 128], WS[wi][:, j, :],
                                        j == 0, j == 10, [R_ws[wi], R_actT[ai][j]], [R_bank[bk]], j == 10)
                                TT_("dve", x1b[:, tt, db * 512:(db + 1) * 512], psum[:, bk, :],
                                    x1b[:, tt, db * 512:(db + 1) * 512], ALU.add, [R_bank[bk], R_x1b[tt]],
                                    [R_x1b[tt]])
                    for tt in range(4):
                        i = tt % 2
                        ACT_(xn2[i][:], x1b[:, tt, :], AF.Square, [R_x1b[tt]], [R_scr2[i]], accum=ss2[i][:, 0:1])
                        ACT_(rs2[i][:, 0:1], ss2[i][:, 0:1], AF.Sqrt, [R_scr2[i], R_misc], [R_scr2[i]], scale=1.0 / D,
                             bias=epsc[:, 0:1])
                        fw.op("dve", lambda e, i=i: e.reciprocal(out=rs2[i][:, 0:1], in_=rs2[i][:, 0:1]),
                              reads=[R_scr2[i]], writes=[R_scr2[i]])
                        TS_("pool", x1b[:, tt, :], x1b[:, tt, :], rs2[i][:, 0:1], None, ALU.mult, None,
                            [R_x1b[tt], R_scr2[i]], [R_x1b[tt]])
                        TT_("pool", x1b[:, tt, :], x1b[:, tt, :], gfin_bc[:], ALU.mult, [R_x1b[tt], R_tc], [R_x1b[tt]])
                        fw.dma("sp", out_ap[t0 + tt * 128:t0 + (tt + 1) * 128, :], x1b[:, tt, :], S_out[tt],
                               reads=[R_x1b[tt]])
                for tt in range(4):
                    dbg_sems.append(S_out[tt])

        for k in ("pe", "act", "dve", "pool"):
            if fw.cnt[k] > 0:
                fw._wait("sp", (fw.sem[k], fw.cnt[k], k))
        for ds in dbg_sems:
            if ds.count > 0:
                fw._wait("sp", (ds.sem, ds.count, "dma"))
    return nc, fw


_CONST = {}


def _bf16(a):
    return np.asarray(a, dtype=np.float32).astype(ml_dtypes.bfloat16)


def host_constants():
    if _CONST:
        return _CONST
    N = 2 * L
    t = np.linspace(0.0, 1.0, L, dtype=np.float32)[:, None]
    w = (np.float32(2.0 * math.pi) * np.arange(L, dtype=np.float32)[:, None] / np.float32(L)).astype(np.float32)
    f = np.linspace(1e-4, 15, 16, dtype=np.float32)[None, :]
    z = np.concatenate([t, np.cos(f * w), -np.sin(f * w)], axis=-1).astype(np.float32)
    _CONST["zT"] = np.ascontiguousarray(z.T)
    max_decay = math.log(1e-2) / 0.3
    min_decay = math.log(1e-2) / 1.5
    deltas = np.linspace(min_decay, max_decay, DH, dtype=np.float32)
    _CONST["absd"] = np.abs(deltas).astype(np.float32)
    tn = np.linspace(0.0, 1.0, L, dtype=np.float32)
    _CONST["tneg"] = np.ascontiguousarray((-tn).reshape(16, 128).T)
    _CONST["ident"] = _bf16(np.eye(128))
    a = np.arange(NFT * 128, dtype=np.int64)
    tt_ = np.arange(L, dtype=np.int64)
    ff = a[:, None]
    ang = (2.0 * np.pi / N) * ((ff * tt_[None, :]) % N).astype(np.float64)
    valid = (a <= L)[:, None]
    C = np.where(valid, np.cos(ang), 0.0)
    S = np.where(valid, np.sin(ang), 0.0)
    Cf = C.reshape(NFT, 128, 16, 128)
    Sf = S.reshape(NFT, 128, 16, 128)
    gf = np.stack([Cf.transpose(0, 3, 2, 1), Sf.transpose(0, 3, 2, 1)], axis=2)
    _CONST["gf"] = _bf16(gf)
    wf = np.full((NFT * 128,), 2.0 / N)
    wf[0] = 1.0 / N
    wf[L] = 1.0 / N
    Ci = (C * wf[:, None]).reshape(NFT, 128, L)
    Si = (S * wf[:, None]).reshape(NFT, 128, L)
    _CONST["gi"] = _bf16(np.stack([Ci, Si], axis=2))
    return _CONST


def make_in_maps(inputs):
    c = host_constants()
    g = lambda k: np.asarray(inputs[k], dtype=np.float32)
    x = g("x")
    shared = {}
    shared["w_in"] = np.ascontiguousarray(g("w_in")[0])
    shared["w_bh"] = np.ascontiguousarray(g("w_branch_hyena")[0])
    shared["w_bs"] = np.ascontiguousarray(g("w_branch_sgu")[0])
    shared["w_out"] = np.ascontiguousarray(g("w_out")[0])
    shared["w_fi"] = np.ascontiguousarray(g("w_ffn_in")[0])
    shared["w_fo"] = np.ascontiguousarray(g("w_ffn_out")[0])
    gcol = np.concatenate([g("norm_mix_g")[0].reshape(16, 128).T, g("norm_ffn_g")[0].reshape(16, 128).T], axis=1)
    shared["gcol"] = np.ascontiguousarray(gcol)
    shared["gfin"] = np.ascontiguousarray(g("norm_final_g"))
    scw = np.concatenate([g("short_conv_w")[0], g("short_conv_b")], axis=0)
    shared["scw"] = np.ascontiguousarray(scw.reshape(4, 24, 128).transpose(2, 1, 0))
    fm = np.zeros((64, 200), np.float32)
    fm[0:33, 0:64] = g("filt_w1")[0]
    fm[:, 64:128] = g("filt_w2")[0]
    fm[:, 128:192] = g("filt_w3")[0]
    fm[:, 192] = g("filt_b1")[0]
    fm[:, 193] = g("filt_b2")[0]
    fm[:, 194] = g("filt_b3")[0]
    fm[:, 195] = g("filt_freq")[0]
    shared["fmlp"] = fm
    shared["zT"] = c["zT"]
    shared["w4"] = np.ascontiguousarray(g("filt_w4")[0])
    shared["hbias"] = np.ascontiguousarray(g("hyena_bias")[0].reshape(-1))
    shared["lngb"] = np.ascontiguousarray(np.concatenate([g("sgu_ln_g")[0], g("sgu_ln_b")[0]]))
    shared["wsT"] = np.ascontiguousarray(g("sgu_w_s")[0].transpose(2, 0, 1))
    shared["bs"] = np.ascontiguousarray(g("sgu_b_s")[0].reshape(-1))
    shared["absd"] = c["absd"]
    shared["tneg"] = c["tneg"]
    shared["ident"] = c["ident"]
    shared["gf"] = c["gf"]
    shared["gi"] = c["gi"]
    maps = []
    for b in range(x.shape[0]):
        m = dict(shared)
        m["x"] = np.ascontiguousarray(x[b])
        maps.append(m)
    return maps


_PROG = {}


def kernel(**inputs):
    if "nc" not in _PROG:
        _PROG["nc"] = build_program()[0]
    nc = _PROG["nc"]
    in_maps = make_in_maps(inputs)
    res = run_bass_kernel_spmd(nc, in_maps, core_ids=list(range(8)))
    out = np.stack([np.asarray(r["out"], dtype=np.float32) for r in res.results], axis=0)
    return out
```

```python
import math
import numpy as np
import ml_dtypes
import concourse.bass as bass
import concourse.mybir as mybir
from concourse.bass_utils import run_bass_kernel_spmd

F32 = mybir.dt.float32
BF16 = mybir.dt.bfloat16
AF = mybir.ActivationFunctionType
ALU = mybir.AluOpType

L = 2048
D = 2048
NT = 16
DH = 1024
DFF = 5632
NF = 44
CG = 256
NG = DH // CG
TT = 512
EPS = 1e-6
NFT = 17
TWO_PI = 2.0 * math.pi
MAGIC = 12582912.0
SIN_SCALE = 6.28318

DEBUG = {}
HYENA_ONLY = False


class Res:
    __slots__ = ("lw", "rd", "name")

    def __init__(self, name=""):
        self.lw = None
        self.rd = {}
        self.name = name


ALL_DSEMS = []


class DSem:
    def __init__(self, nc, name):
        self.sem = nc.alloc_semaphore(name=name)
        self.count = 0
        ALL_DSEMS.append(self)


class FW:
    def __init__(self, nc):
        self.nc = nc
        self.E = dict(pe=nc.tensor, act=nc.scalar, dve=nc.vector, pool=nc.gpsimd, sp=nc.sync)
        self.sem = {k: nc.alloc_semaphore(name="s_" + k) for k in self.E}
        self.cnt = {k: 0 for k in self.E}
        self.waited = {k: {} for k in self.E}
        self.nwait = 0
        self.ninst = 0

    def _wait(self, eng, tok):
        sem, val, src = tok
        if src == eng and eng == "pe":
            return
        key = id(sem)
        if self.waited[eng].get(key, 0) >= val:
            return
        self.E[eng].wait_ge(sem, val)
        self.waited[eng][key] = val
        self.nwait += 1

    def _deps(self, eng, reads, writes):
        for r in reads:
            if r.lw is not None:
                self._wait(eng, r.lw)
        for w in writes:
            if w.lw is not None:
                self._wait(eng, w.lw)
            for tok in w.rd.values():
                self._wait(eng, tok)

    def _record(self, tok, reads, writes):
        key = id(tok[0])
        for r in reads:
            old = r.rd.get(key)
            if old is None or old[1] < tok[1]:
                r.rd[key] = tok
        for w in writes:
            w.lw = tok
            w.rd = {}

    def op(self, eng, fn, reads=(), writes=(), track=True):
        self._deps(eng, reads, writes)
        ins = fn(self.E[eng])
        self.ninst += 1
        idx = self.cnt[eng] + 1
        if track:
            self.cnt[eng] = idx
            ins.then_inc(self.sem[eng], 1)
        self._record((self.sem[eng], idx, eng), reads, writes)
        return ins

    def barrier(self):
        for e in self.E:
            for f in self.E:
                if f != e and self.cnt[f] > 0:
                    self._wait(e, (self.sem[f], self.cnt[f], f))
            for ds in ALL_DSEMS:
                if ds.count > 0:
                    self._wait(e, (ds.sem, ds.count, "dma"))

    def dma(self, q, out, in_, dsem, reads=(), writes=()):
        self._deps(q, reads, writes)
        ins = self.E[q].dma_start(out=out, in_=in_)
        self.ninst += 1
        dsem.count += 16
        ins.then_inc(dsem.sem, 16)
        self._record((dsem.sem, dsem.count, "dma"), reads, writes)
        return ins


def _bcast_rows(ap_1d_tensor, offset, n, parts=128):
    return bass.AP(ap_1d_tensor, offset, [[0, parts], [1, n]])


def build_program(dbg=()):
    nc = bass.Bass("TRN2", target_bir_lowering=False)
    del ALL_DSEMS[:]
    fw = FW(nc)

    def din(name, shape, dt=F32):
        return nc.dram_tensor(name, list(shape), dt, kind="ExternalInput")

    x_d = din("x", [L, D])
    w_in_d = din("w_in", [D, 9216])
    w_bh_d = din("w_bh", [DH, D])
    w_bs_d = din("w_bs", [DH, D])
    w_out_d = din("w_out", [D, D])
    w_fi_d = din("w_fi", [D, 2 * DFF])
    w_fo_d = din("w_fo", [DFF, D])
    gcol_d = din("gcol", [128, 32])
    gfin_d = din("gfin", [D])
    scw_d = din("scw", [128, 24, 4])
    fmlp_d = din("fmlp", [64, 64 * 3 + 8])
    zT_d = din("zT", [33, L])
    w4_d = din("w4", [64, 4096])
    hbias_d = din("hbias", [2 * DH])
    lngb_d = din("lngb", [2 * DH])
    wsT_d = din("wsT", [128, 8, 128])
    bs_d = din("bs", [8 * 128])
    absd_d = din("absd", [DH])
    tneg_d = din("tneg", [128, 16])
    ident_d = din("ident", [128, 128], BF16)
    gf_d = din("gf", [NFT, 128, 2, 16, 128], BF16)
    gi_d = din("gi", [NFT, 128, 2, L], BF16)
    out_d = nc.dram_tensor("out", [L, D], F32, kind="ExternalOutput")
    dbg_d = {}
    for nm, shp, dt in dbg:
        dbg_d[nm] = nc.dram_tensor(nm, list(shp), dt, kind="ExternalOutput")

    x_ap = x_d.ap()
    out_ap = out_d.ap()
    w_in_v = w_in_d.ap().rearrange("(k p) e -> p k e", p=128)
    w_bh_v = w_bh_d.ap().rearrange("(k p) e -> p k e", p=128)
    w_bs_v = w_bs_d.ap().rearrange("(k p) e -> p k e", p=128)
    w_out_v = w_out_d.ap().rearrange("(k p) e -> p k e", p=128)
    w_fi_v = w_fi_d.ap().rearrange("(k p) e -> p k e", p=128)
    w_fo_v = w_fo_d.ap().rearrange("(k p) e -> p k e", p=128)

    def sb(name, shape, dt):
        return nc.sbuf_tensor('sb_' + name, list(shape), dt)
    ps = nc.psum_tensor
    from contextlib import ExitStack
    with ExitStack() as top:
        def SB(name, shape, dt):
            return top.enter_context(sb(name, list(shape), dt))

        ident = SB("ident", [128, 128], BF16)
        gcol = SB("gcol", [128, 32], F32)
        scw = SB("scw", [128, 24, 4], F32)
        fmlp = SB("fmlp", [64, 200], F32)
        tneg = SB("tneg", [128, 16], F32)
        fsc = SB("fsc", [64, 4], F32)
        ones_bf = SB("ones_bf", [128, 128], BF16)
        epsc = SB("epsc", [128, 1], F32)
        cs_const = DSem(nc, "cs_const")
        R_const = Res("const")
        for (t, src) in ((ident, ident_d.ap()), (gcol, gcol_d.ap()), (scw, scw_d.ap()),
                         (fmlp, fmlp_d.ap()), (tneg, tneg_d.ap())):
            fw.dma("sp", t[:], src, cs_const, writes=[R_const])
        R_misc = Res("misc")
        fw.op("pool", lambda e: e.memset(ones_bf[:], 1.0), writes=[R_misc])
        fw.op("pool", lambda e: e.memset(epsc[:], EPS), writes=[R_misc])
        fw.op("dve", lambda e: e.tensor_scalar(out=fsc[:, 0:1], in0=fmlp[:, 195:196], scalar1=1.0 / TWO_PI,
                                                scalar2=None, op0=ALU.mult), reads=[R_const], writes=[R_misc])
        for k in range(3):
            fw.op("dve", lambda e, k=k: e.tensor_scalar(out=fsc[:, k + 1:k + 2], in0=fmlp[:, 192 + k:193 + k],
                                                         scalar1=fsc[:, 0:1], scalar2=None, op0=ALU.mult), reads=[R_const, R_misc], writes=[R_misc])

        psum = top.enter_context(ps("psum", [128, 8, 512], F32))
        R_bank = [Res("bank%d" % i) for i in range(8)]

        def pbf(b0):
            return psum[:, b0:b0 + 2, :].bitcast(BF16).rearrange("p b (k c) -> p (b k) c", c=128)

        dbg_sems = []

        def dump(name, ap, R):
            if name in dbg_d:
                ds = DSem(nc, "dbg_" + name)
                fw.dma("sp", dbg_d[name].ap(), ap, ds, reads=R)
                dbg_sems.append(ds)

        def TT_(eng, out, in0, in1, op, R, W):
            return fw.op(eng, lambda e: e.tensor_tensor(out=out, in0=in0, in1=in1, op=op), reads=R, writes=W)

        def TS_(eng, out, in0, s1, s2, op0, op1, R, W):
            if op1 is None:
                return fw.op(eng, lambda e: e.tensor_scalar(out=out, in0=in0, scalar1=s1, scalar2=None, op0=op0),
                             reads=R, writes=W)
            return fw.op(eng, lambda e: e.tensor_scalar(out=out, in0=in0, scalar1=s1, scalar2=s2, op0=op0, op1=op1),
                         reads=R, writes=W)

        def STT_(eng, out, in0, sc, in1, op0, op1, R, W):
            return fw.op(eng, lambda e: e.scalar_tensor_tensor(out=out, in0=in0, scalar=sc, in1=in1, op0=op0, op1=op1),
                         reads=R, writes=W)

        def ACT_(out, in_, func, R, W, scale=None, bias=None, accum=None):
            kw = {}
            if scale is not None:
                kw["scale"] = scale
            if bias is not None:
                kw["bias"] = bias
            if accum is not None:
                kw["accum_out"] = accum
            return fw.op("act", lambda e: e.activation(out=out, in_=in_, func=func, **kw), reads=R, writes=W)

        def MM_(out, lhsT, rhs, start, stop, R, W, track):
            return fw.op("pe", lambda e: e.matmul(out, lhsT, rhs, start=start, stop=stop), reads=R, writes=W,
                         track=track)

        def TR_(out, in_, R, W, track):
            return fw.op("pe", lambda e: e.transpose(out, in_, ident[:]), reads=R + [R_const], writes=W, track=track)

        x2T = SB("x2T", [128, 8, L], BF16)
        R_x2T = [Res("x2T%d" % i) for i in range(8)]

        WS = []
        R_ws = []
        S_ws = [DSem(nc, "ws%d" % i) for i in range(3)]
        ws_rot = [0]

        def next_ws():
            i = ws_rot[0] % 2
            ws_rot[0] += 1
            return i

        pieces = []
        for eb in range(2):
            pieces.append((("u", eb), w_in_v[:, :, 3072 + eb * 512:3072 + (eb + 1) * 512], 16, 512))
        for eb in range(2):
            pieces.append((("v", eb), w_in_v[:, :, 4096 + eb * 512:4096 + (eb + 1) * 512], 16, 512))
        for br in range(2):
            gofs_ = 5120 + br * 2048
            wb_v_ = w_bh_v if br == 0 else w_bs_v
            for db in range(4):
                pieces.append((("g", br, db), w_in_v[:, :, gofs_ + db * 512:gofs_ + (db + 1) * 512], 16, 512))
                pieces.append((("b", br, db), wb_v_[:, :, db * 512:(db + 1) * 512], 8, 512))
        for db in range(4):
            pieces.append((("o", db), w_out_v[:, :, db * 512:(db + 1) * 512], 16, 512))
        for s_ in range(4):
            f_ = 0
            while f_ < 11:
                nfl_ = min(2, 11 - f_)
                fg_ = s_ * 11 + f_
                pieces.append((("fg", s_, f_), w_fi_v[:, :, fg_ * 128:(fg_ + nfl_) * 128], 16, nfl_ * 128))
                pieces.append((("fu", s_, f_), w_fi_v[:, :, DFF + fg_ * 128:DFF + (fg_ + nfl_) * 128], 16, nfl_ * 128))
                f_ += nfl_
            for db in range(4):
                pieces.append((("fo", s_, db), w_fo_v[:, s_ * 11:(s_ + 1) * 11, db * 512:(db + 1) * 512], 11, 512))
        wc_off = {}
        off_ = 0
        for key, src_, K_, C_ in pieces:
            wc_off[key] = (off_, K_, C_)
            off_ += 128 * K_ * C_
        wc_d = nc.dram_tensor("wcache", [off_], BF16, kind="Internal")

        def wc_ap(key):
            o_, K_, C_ = wc_off[key]
            return wc_d.ap()[o_:o_ + 128 * K_ * C_].rearrange("(p k c) -> p k c", p=128, k=K_)

        S_conv = DSem(nc, "wconv")
        conv_pos = [0]

        def emit_conv(n):
            for _ in range(n):
                if conv_pos[0] >= len(pieces):
                    return
                key, src_, K_, C_ = pieces[conv_pos[0]]
                conv_pos[0] += 1
                fw.dma("pool", wc_ap(key), src_, S_conv)

        def load_piece(key, dst, wi):
            fw.dma("sp", dst, wc_ap(key), S_ws[wi], writes=[R_ws[wi]])

        scr_d = nc.dram_tensor("scr_vx1", [16, 128, L], BF16, kind="Internal")
        R_scr = [Res() for _ in range(16)]
        S_scr = DSem(nc, "scr")

        def norm_transpose(xt_ap, R_x, dstT, R_dst, tcol, gofs, scr, R_scr, bank0):
            xn, ss, rstd = scr
            ACT_(xn[:], xt_ap, AF.Square, [R_x], [R_scr], accum=ss[:, 0:1])
            ACT_(rstd[:, 0:1], ss[:, 0:1], AF.Sqrt, [R_scr, R_misc], [R_scr], scale=1.0 / D, bias=epsc[:, 0:1])
            fw.op("dve", lambda e: e.reciprocal(out=rstd[:, 0:1], in_=rstd[:, 0:1]), reads=[R_scr], writes=[R_scr])
            ACT_(xn[:], xt_ap, AF.Copy, [R_x, R_scr], [R_scr], scale=rstd[:, 0:1])
            pv = pbf(bank0)
            Rb = [R_bank[bank0], R_bank[bank0 + 1]]
            for k in range(16):
                TR_(pv[:, k, :], xn[:, k * 128:(k + 1) * 128], [R_scr], Rb, track=(k == 15))
            for k in range(16):
                if k % 2 == 0:
                    ACT_(dstT[:, k, tcol:tcol + 128], pv[:, k, :], AF.Copy, Rb + [R_const], [R_dst],
                         scale=gcol[:, gofs + k:gofs + k + 1])
                else:
                    TS_("dve", dstT[:, k, tcol:tcol + 128], pv[:, k, :], gcol[:, gofs + k:gofs + k + 1], None,
                        ALU.mult, None, Rb + [R_const], [R_dst])

        sh = top.enter_context(ExitStack())
        h3T = sh.enter_context(sb("h3T", [64, L], F32))
        R_h3T = Res("h3T")
        with ExitStack() as st:
            zT = st.enter_context(sb("zT", [33, L], F32))
            hA = st.enter_context(sb("hA", [64, L], F32))
            hB = st.enter_context(sb("hB", [64, L], F32))
            utmp = st.enter_context(sb("utmp", [64, 2, 512], F32))
            rtmp = st.enter_context(sb("rtmp", [64, 2, 512], F32))
            R_zT, R_hA, R_hB = Res(), Res(), Res()
            R_ut = [Res(), Res()]
            R_rt = [Res(), Res()]
            S_zT = DSem(nc, "zT")
            fw.dma("sp", zT[:], zT_d.ap(), S_zT, writes=[R_zT])
            layers = [(zT, R_zT, 33, 0, hA, R_hA), (hA, R_hA, 64, 64, hB, R_hB), (hB, R_hB, 64, 128, h3T, R_h3T)]
            it = 0
            for li, (src, R_src, K, wo, dst, R_dst) in enumerate(layers):
                for tb in range(4):
                    bk = it % 2
                    it += 1
                    MM_(psum[0:64, bk, :], fmlp[0:K, wo:wo + 64], src[0:K, tb * 512:(tb + 1) * 512], True, True,
                        [R_const, R_src], [R_bank[bk]], True)
                    TS_("dve", utmp[:, bk, :], psum[0:64, bk, :], fsc[:, 0:1], fsc[:, li + 1:li + 2], ALU.mult,
                        ALU.add, [R_bank[bk], R_misc], [R_ut[bk]])
                    TS_("dve", rtmp[:, bk, :], utmp[:, bk, :], MAGIC, MAGIC, ALU.add, ALU.subtract, [R_ut[bk]],
                        [R_rt[bk]])
                    TT_("dve", utmp[:, bk, :], utmp[:, bk, :], rtmp[:, bk, :], ALU.subtract, [R_ut[bk], R_rt[bk]],
                        [R_ut[bk]])
                    ACT_(dst[:, tb * 512:(tb + 1) * 512], utmp[:, bk, :], AF.Sin, [R_ut[bk]], [R_dst],
                         scale=SIN_SCALE)
        dump("h3T", h3T[:], [R_h3T])
        fw.barrier()

        if True:

            with ExitStack() as s12:
                hT = s12.enter_context(sb("hT", [128, 16, L], BF16))
                R_hT = Res("hT")
                xin = s12.enter_context(sb("xin", [128, 2, D], F32))
                R_xin = [Res(), Res()]
                S_xin = [DSem(nc, "xin0"), DSem(nc, "xin1")]
                WS[:] = [s12.enter_context(sb("wsh%d" % i, [128, 16, 512], BF16)) for i in range(2)]
                R_ws[:] = [Res(), Res()]
                stg = s12.enter_context(sb("stg", [128, 2, L], BF16))
                R_stg = [Res(), Res()]
                xn_ = s12.enter_context(sb("xn0", [128, D], BF16))
                xn2 = [xn_, xn_]
                ss2 = [s12.enter_context(sb("ss%d" % i, [128, 1], F32)) for i in range(2)]
                rs2 = [s12.enter_context(sb("rs%d" % i, [128, 1], F32)) for i in range(2)]
                R_one = Res()
                R_scr2 = [R_one, R_one]
                for tt in range(NT):
                    i = tt % 2
                    fw.dma("sp", xin[:, i, :], x_ap[tt * 128:(tt + 1) * 128, :], S_xin[i], writes=[R_xin[i]])
                    norm_transpose(xin[:, i, :], R_xin[i], hT, R_hT, tt * 128, 0, (xn2[i], ss2[i], rs2[i]),
                                   R_scr2[i], 4 + 2 * i)
                dump("hT", hT[:], [R_hT])

                pstage = s12.enter_context(sb("pstage", [128, 2, L + 2], F32))
                R_pst = [Res(), Res()]
                tmpc = s12.enter_context(sb("tmpc", [128, 1, L], F32))
                R_tmpc = [Res()]
                for i in range(2):
                    fw.op("pool", lambda e, i=i: e.memset(pstage[:, i, 0:1], 0.0), writes=[R_pst[i]])
                    fw.op("pool", lambda e, i=i: e.memset(pstage[:, i, L + 1:L + 2], 0.0), writes=[R_pst[i]])
                grp = 0
                for eb in range(6):
                    wi = next_ws()
                    fw.dma("pool", WS[wi][:, :, :], w_in_v[:, :, eb * 512:(eb + 1) * 512], S_ws[wi],
                           writes=[R_ws[wi]])
                    for ej in range(4):
                        et = eb * 4 + ej
                        si = et % 2
                        emit_conv(1)
                        for th in range(2):
                            b0 = (grp % 4) * 2
                            grp += 1
                            for tb in range(2):
                                for k in range(16):
                                    MM_(psum[:, b0 + tb, :], WS[wi][:, k, ej * 128:(ej + 1) * 128],
                                        hT[:, k, th * 1024 + tb * 512: th * 1024 + (tb + 1) * 512],
                                        k == 0, k == 15, [R_ws[wi], R_hT], [R_bank[b0 + tb]], k == 15)
                            ACT_(pstage[:, si, 1 + th * 1024: 1 + (th + 1) * 1024],
                                 psum[:, b0:b0 + 2, :].rearrange("p b c -> p (b c)"), AF.Copy,
                                 [R_bank[b0], R_bank[b0 + 1]], [R_pst[si]])
                        tci = 0
                        if et < 16:
                            dstap = stg[:, si, :]
                            Rd = R_stg[si]
                        else:
                            dstap = x2T[:, et - 16, :]
                            Rd = R_x2T[et - 16]
                        TS_("dve", tmpc[:, tci, :], pstage[:, si, 1:L + 1], scw[:, et, 1:2], scw[:, et, 3:4], ALU.mult,
                            ALU.add, [R_pst[si], R_const], [R_tmpc[tci]])
                        STT_("dve", tmpc[:, tci, :], pstage[:, si, 0:L], scw[:, et, 0:1], tmpc[:, tci, :], ALU.mult,
                             ALU.add, [R_pst[si], R_tmpc[tci], R_const], [R_tmpc[tci]])
                        STT_("dve", dstap, pstage[:, si, 2:L + 2], scw[:, et, 2:3], tmpc[:, tci, :], ALU.mult,
                             ALU.add, [R_pst[si], R_tmpc[tci], R_const], [Rd])
                        if et < 16:
                            fw.dma("sp", scr_d.ap()[et], stg[:, si, :], S_scr, reads=[R_stg[si]], writes=[R_scr[et]])
                            if et == 0:
                                dump("vT0", stg[:, si, :], [R_stg[si]])
            for r_ in R_scr:
                r_.lw = (S_scr.sem, S_scr.count, "dma")
            dump("x2T_pre", x2T[:], R_x2T)
            fw.barrier()

            with ExitStack() as s3:
                DHt = s3.enter_context(sb("DHt", [128, 16, 3, CG], BF16))
                R_DHd = Res("DHdata")
                R_DHf = Res("DHfilt")
                Yr = s3.enter_context(sb("Yr", [128, NFT, CG], BF16))
                Ys = s3.enter_context(sb("Ys", [128, NFT, CG], BF16))
                R_Y = [Res() for _ in range(NFT)]
                z1T = s3.enter_context(sb("z1T", [128, 2, L], BF16))
                R_z1T = [Res(), Res()]
                vg = s3.enter_context(sb("vg", [128, 2, L], BF16))
                x1g = s3.enter_context(sb("x1g", [128, 2, L], BF16))
                R_vg = [Res(), Res()]
                R_x1g = [Res(), Res()]
                S_vg = DSem(nc, "vg")
                S_x1g = DSem(nc, "x1g")
                tabs = [s3.enter_context(sb("tab%d" % i, [128, 2, L], BF16)) for i in range(3)]
                R_tab = [Res() for _ in range(3)]
                S_tab = [DSem(nc, "tab%d" % i) for i in range(3)]
                tab_rot = [0]
                w4g = s3.enter_context(sb("w4g", [64, 2, 2, CG], F32))
                R_w4g = [Res(), Res()]
                S_w4g = [DSem(nc, "w4g0"), DSem(nc, "w4g1")]
                hb_bc = s3.enter_context(sb("hb_bc", [128, 2, CG], F32))
                absd_bc = s3.enter_context(sb("absd_bc", [128, CG], F32))
                R_gc = Res("groupconst")
                S_gc = DSem(nc, "groupconst")
                dec = s3.enter_context(sb("dec", [128, 2, CG], F32))
                R_dec = [Res(), Res()]
                hdec = s3.enter_context(sb("hdec", [128, 2, 2, CG], F32))
                R_hdec = [Res(), Res()]
                habs = s3.enter_context(sb("habs", [128, 2, 2, CG], BF16))
                R_habs = [Res(), Res()]
                rn = s3.enter_context(sb("rn", [128, 2, CG], F32))
                R_rn = [Res(), Res()]
                ytmp = s3.enter_context(sb("ytmp", [128, 2, 6, CG], F32))
                R_yt = [[Res() for _ in range(6)] for _ in range(2)]

                for g in range(NG):
                    c0 = g * CG
                    for ct in range(2):
                        fw.dma("sp", vg[:, ct, :], scr_d.ap()[g * 2 + ct], S_vg, reads=[R_scr[g * 2 + ct]],
                               writes=[R_vg[ct]])
                        fw.dma("sp", x1g[:, ct, :], scr_d.ap()[8 + g * 2 + ct], S_x1g, reads=[R_scr[8 + g * 2 + ct]],
                               writes=[R_x1g[ct]])
                    for ct in range(2):
                        R_vg[ct].lw = (S_vg.sem, S_vg.count, "dma")
                        R_x1g[ct].lw = (S_x1g.sem, S_x1g.count, "dma")
                    fw.dma("sp", absd_bc[:], _bcast_rows(absd_d, c0, CG), S_gc, writes=[R_gc])
                    for o in range(2):
                        fw.dma("sp", hb_bc[:, o, :], _bcast_rows(hbias_d, o * DH + c0, CG), S_gc, writes=[R_gc])
                    for o in range(2):
                        for d in range(2):
                            col = d * 2048 + o * DH + c0
                            fw.dma("sp", w4g[:, o, d, :], w4_d.ap()[:, col:col + CG], S_w4g[o], writes=[R_w4g[o]])
                        nb = 6 + o
                        for nt in range(NT):
                            i = nt % 2
                            fb = 4 + i
                            MM_(psum[:, fb, :], h3T[0:64, nt * 128:(nt + 1) * 128],
                                w4g[0:64, o, :, :].rearrange("p d c -> p (d c)"), True, True,
                                [R_h3T, R_w4g[o]], [R_bank[fb]], True)
                            ACT_(dec[:, i, :], absd_bc[:], AF.Exp, [R_gc, R_const], [R_dec[i]],
                                 scale=tneg[:, nt:nt + 1])
                            for d in range(2):
                                TT_("dve", hdec[:, i, d, :], psum[:, fb, d * CG:(d + 1) * CG], dec[:, i, :], ALU.mult,
                                    [R_bank[fb], R_dec[i]], [R_hdec[i]])
                            if nt == 0:
                                fw.op("pool", lambda e, i=i: e.memset(hdec[0:1, i, 1, :], 0.0), writes=[R_hdec[i]])
                            ACT_(habs[:, i, :, :], hdec[:, i, :, :], AF.Abs, [R_hdec[i]], [R_habs[i]])
                            for d in range(2):
                                MM_(psum[:, nb, 0:CG], ones_bf[:], habs[:, i, d, :], (nt == 0 and d == 0),
                                    (nt == NT - 1 and d == 1), [R_misc, R_habs[i]], [R_bank[nb]],
                                    (nt == NT - 1 and d == 1))
                            TT_("pool", DHt[:, nt, 1, :], hdec[:, i, 0, :], hdec[:, i, 1, :], ALU.add, [R_hdec[i]],
                                [R_DHf])
                            TT_("pool", DHt[:, nt, 2, :], hdec[:, i, 1, :], hdec[:, i, 0, :], ALU.subtract,
                                [R_hdec[i]], [R_DHf])
                        fw.op("dve", lambda e, o=o, nb=nb: e.reciprocal(out=rn[:, o, :], in_=psum[:, nb, 0:CG]),
                              reads=[R_bank[nb]], writes=[R_rn[o]])
                        if g == 0:
                            dump("hs%d" % o, DHt[:, :, 1, :], [R_DHf])
                            dump("rn%d" % o, rn[:, o, :], [R_rn[o]])

                        for ct in range(2):
                            src = vg[:, ct, :] if o == 0 else z1T[:, ct, :]
                            R_src = R_vg[ct] if o == 0 else R_z1T[ct]
                            b0 = 2 * ct
                            pv = pbf(b0)
                            Rb = [R_bank[b0], R_bank[b0 + 1]]
                            for tt in range(NT):
                                TR_(pv[:, tt, :], src[:, tt * 128:(tt + 1) * 128], [R_src], Rb, track=(tt == NT - 1))
                            ACT_(DHt[:, :, 0, ct * 128:(ct + 1) * 128], pv[:, :, :], AF.Copy, Rb, [R_DHd])

                        for ft in range(NFT):
                            if ft % 4 == 0:
                                emit_conv(1)
                            ti = tab_rot[0] % 3
                            tab_rot[0] += 1
                            tf = tabs[ti][:, :, :].rearrange("p s (a f) -> p s a f", f=128)
                            fw.dma("sp", tabs[ti][:, :, :], gf_d.ap()[ft].rearrange("p s a f -> p s (a f)"), S_tab[ti],
                                   writes=[R_tab[ti]])
                            bc = (ft % 2) * 2
                            bs_ = bc + 1
                            for a in range(16):
                                MM_(psum[:, bc, :], tf[:, 0, a, :], DHt[:, a, 0:2, :].rearrange("p b c -> p (b c)"),
                                    a == 0, a == 15, [R_tab[ti], R_DHd, R_DHf], [R_bank[bc]], a == 15)
                            for a in range(16):
                                MM_(psum[:, bs_, :].rearrange("p (b c) -> p b c", b=2), tf[:, 1, a, :],
                                    DHt[:, a, 0:3:2, :], a == 0, a == 15, [R_tab[ti], R_DHd, R_DHf], [R_bank[bs_]],
                                    a == 15)
                            yi = ft % 2
                            kr, ki, a1, b1, c1, d1 = [ytmp[:, yi, j, :] for j in range(6)]
                            Rk = R_yt[yi]
                            ZC = psum[:, bc, 0:CG]
                            KC = psum[:, bc, CG:2 * CG]
                            ZS = psum[:, bs_, 0:CG]
                            KS = psum[:, bs_, CG:2 * CG]
                            TT_("dve", kr, KC, rn[:, o, :], ALU.mult, [R_bank[bc], R_rn[o]], [Rk[0]])
                            TT_("pool", kr, kr, hb_bc[:, o, :], ALU.add, [Rk[0], R_gc], [Rk[0]])
                            TT_("dve", ki, KS, rn[:, o, :], ALU.mult, [R_bank[bs_], R_rn[o]], [Rk[1]])
                            TT_("dve", a1, ZC, kr, ALU.mult, [R_bank[bc], Rk[0]], [Rk[2]])
                            TT_("dve", b1, ZS, ki, ALU.mult, [R_bank[bs_], Rk[1]], [Rk[3]])
                            TT_("pool", Yr[:, ft, :], a1, b1, ALU.add, [Rk[2], Rk[3]], [R_Y[ft]])
                            TT_("dve", c1, ZS, kr, ALU.mult, [R_bank[bs_], Rk[0]], [Rk[4]])
                            TT_("dve", d1, ZC, ki, ALU.mult, [R_bank[bc], Rk[1]], [Rk[5]])
                            TT_("pool", Ys[:, ft, :], c1, d1, ALU.subtract, [Rk[4], Rk[5]], [R_Y[ft]])
                        if g == 0:
                            dump("Yr%d" % o, Yr[:], R_Y)
                            dump("Ys%d" % o, Ys[:], R_Y)

                        for a in range(NFT):
                            if a % 4 == 1:
                                emit_conv(1)
                            ti = tab_rot[0] % 3
                            tab_rot[0] += 1
                            fw.dma("sp", tabs[ti][:, :, :], gi_d.ap()[a], S_tab[ti], writes=[R_tab[ti]])
                            for ct in range(2):
                                for tb in range(4):
                                    bk = ct * 4 + tb
                                    MM_(psum[:, bk, :], Yr[:, a, ct * 128:(ct + 1) * 128],
                                        tabs[ti][:, 0, tb * 512:(tb + 1) * 512], a == 0, False,
                                        [R_tab[ti], R_Y[a]], [R_bank[bk]], False)
                                    MM_(psum[:, bk, :], Ys[:, a, ct * 128:(ct + 1) * 128],
                                        tabs[ti][:, 1, tb * 512:(tb + 1) * 512], False, a == NFT - 1,
                                        [R_tab[ti], R_Y[a]], [R_bank[bk]], (a == NFT - 1) or (ct == 1 and tb == 3))
                        for ct in range(2):
                            for tb in range(4):
                                bk = ct * 4 + tb
                                cs = slice(tb * 512, (tb + 1) * 512)
                                if o == 0:
                                    TT_("dve", z1T[:, ct, cs], psum[:, bk, :], x1g[:, ct, cs], ALU.mult,
                                        [R_bank[bk], R_x1g[ct]], [R_z1T[ct]])
                                else:
                                    TT_("dve", x2T[:, g * 2 + ct, cs], psum[:, bk, :], x2T[:, g * 2 + ct, cs], ALU.mult,
                                        [R_bank[bk], R_x2T[g * 2 + ct]], [R_x2T[g * 2 + ct]])
                        if g == 0 and o == 0:
                            dump("z1T", z1T[:], R_z1T)
        emit_conv(len(pieces))
        sh.close()
        dump("yhT", x2T[:], R_x2T)
        fw.barrier()

        if not HYENA_ONLY:
            with ExitStack() as s4:
                x1b = s4.enter_context(sb("x1b", [128, 4, D], F32))
                R_x1b = [Res() for _ in range(4)]
                S_x1b = [DSem(nc, "x1b%d" % i) for i in range(4)]
                S_out = [DSem(nc, "outs%d" % i) for i in range(4)]
                hTt = s4.enter_context(sb("hTt", [128, 16, TT], BF16))
                R_hTt = Res("hTt")
                WS[:] = [s4.enter_context(sb("wst%d" % i, [128, 16, 512], BF16)) for i in range(2)]
                WS.append(s4.enter_context(sb("wst2", [128, 8, 512], BF16)))
                R_ws[:] = [Res(), Res(), Res()]
                xn_ = s4.enter_context(sb("xnt0", [128, D], BF16))
                xn2 = [xn_, xn_]
                ss2 = [s4.enter_context(sb("sst%d" % i, [128, 1], F32)) for i in range(2)]
                rs2 = [s4.enter_context(sb("rst%d" % i, [128, 1], F32)) for i in range(2)]
                R_one = Res()
                R_scr2 = [R_one, R_one]
                gfin_bc = s4.enter_context(sb("gfin_bc", [128, D], F32))
                lng_bc = s4.enter_context(sb("lng_bc", [128, 2, DH], F32))
                bs_bc = s4.enter_context(sb("bs_bc", [128, 8, 128], F32))
                wsT = s4.enter_context(sb("wsT", [128, 8, 128], BF16))
                R_tc = Res("tconst")
                S_tc = DSem(nc, "tconst")
                fw.dma("sp", gfin_bc[:], _bcast_rows(gfin_d, 0, D), S_tc, writes=[R_tc])
                fw.dma("sp", lng_bc[:, :, :].rearrange("p a c -> p (a c)"), _bcast_rows(lngb_d, 0, 2 * DH), S_tc,
                       writes=[R_tc])
                fw.dma("sp", bs_bc[:, :, :].rearrange("p a c -> p (a c)"), _bcast_rows(bs_d, 0, 1024), S_tc,
                       writes=[R_tc])
                S_tc2 = DSem(nc, "tconst2")
                R_tc2 = Res("tconst2")
                fw.dma("pool", wsT[:], wsT_d.ap(), S_tc2, writes=[R_tc2])
                uT = s4.enter_context(sb("uT", [128, 8, TT], BF16))
                R_uT = [Res() for _ in range(8)]
                mT = s4.enter_context(sb("mT", [128, 16, TT], BF16))
                R_mT = [Res() for _ in range(16)]
                v2g = s4.enter_context(sb("v2g", [128, 1, DH], F32))
                R_v2g = [Res()] * 2
                v2n = s4.enter_context(sb("v2n", [128, 1, DH], BF16))
                R_v2n = [Res()] * 2
                lnst = s4.enter_context(sb("lnst", [128, 1, 8], F32))
                sgt = s4.enter_context(sb("sgt", [128, 1, 8, 128], F32))
                R_sgt = [Res()] * 2
                etmp = s4.enter_context(sb("etmp", [128, 2, TT], F32))
                R_et = [Res() for _ in range(2)]
                actT = s4.enter_context(sb("actT", [128, 1, 11, TT], BF16))
                R_actT = [[Res() for _ in range(11)]] * 2
                bank_rot = [0]

                def nb1():
                    b = bank_rot[0] % 8
                    bank_rot[0] += 1
                    return b

                def nb2():
                    if bank_rot[0] % 2:
                        bank_rot[0] += 1
                    b = bank_rot[0] % 8
                    bank_rot[0] += 2
                    return b

                et_rot = [0]

                def net():
                    i = et_rot[0] % 2
                    et_rot[0] += 1
                    return i

                for J in range(L // TT):
                    t0 = J * TT
                    for tt in range(4):
                        fw.dma("sp", x1b[:, tt, :], x_ap[t0 + tt * 128:t0 + (tt + 1) * 128, :], S_x1b[tt],
                               writes=[R_x1b[tt]])
                    for tt in range(4):
                        i = tt % 2
                        norm_transpose(x1b[:, tt, :], R_x1b[tt], hTt, R_hTt, tt * 128, 0,
                                       (xn2[i], ss2[i], rs2[i]), R_scr2[i], nb2())
                    if J == 0:
                        dump("hTt", hTt[:], [R_hTt])
                    for eb in range(2):
                        wi = next_ws()
                        load_piece(("u", eb), WS[wi][:, :, :], wi)
                        for ej in range(4):
                            et = eb * 4 + ej
                            bk = nb1()
                            for k in range(16):
                                MM_(psum[:, bk, :], WS[wi][:, k, ej * 128:(ej + 1) * 128], hTt[:, k, :], k == 0,
                                    k == 15, [R_ws[wi], R_hTt], [R_bank[bk]], k == 15)
                            ACT_(uT[:, et, :], psum[:, bk, :], AF.Gelu, [R_bank[bk]], [R_uT[et]])
                    wv = []
                    for eb in range(2):
                        wi = next_ws()
                        load_piece(("v", eb), WS[wi][:, :, :], wi)
                        wv.append(wi)
                    for tt in range(4):
                        i = 0
                        for eb in range(2):
                            wi = wv[eb]
                            bk = nb1()
                            for k in range(16):
                                MM_(psum[:, bk, :], hTt[:, k, tt * 128:(tt + 1) * 128], WS[wi][:, k, :], k == 0,
                                    k == 15, [R_ws[wi], R_hTt], [R_bank[bk]], k == 15)
                            ACT_(v2g[:, i, eb * 512:(eb + 1) * 512], psum[:, bk, :], AF.Gelu, [R_bank[bk]],
                                 [R_v2g[i]], accum=lnst[:, i, eb:eb + 1])
                        st_ = lnst[:, i, :]
                        TS_("dve", st_[:, 2:3], st_[:, 0:1], st_[:, 1:2], -1.0 / DH, ALU.add, ALU.mult, [R_v2g[i]],
                            [R_v2g[i]])
                        ACT_(v2n[:, i, :], v2g[:, i, :], AF.Square, [R_v2g[i]], [R_v2n[i]], bias=st_[:, 2:3],
                             accum=st_[:, 3:4])
                        ACT_(st_[:, 4:5], st_[:, 3:4], AF.Sqrt, [R_v2n[i], R_v2g[i], R_misc], [R_v2g[i]], scale=1.0 / DH,
                             bias=epsc[:, 0:1])
                        fw.op("dve", lambda e, st_=st_: e.reciprocal(out=st_[:, 4:5], in_=st_[:, 4:5]),
                              reads=[R_v2g[i]], writes=[R_v2g[i]])
                        TS_("dve", v2g[:, i, :], v2g[:, i, :], st_[:, 2:3], st_[:, 4:5], ALU.add, ALU.mult,
                            [R_v2g[i]], [R_v2g[i]])
                        TT_("pool", v2g[:, i, :], v2g[:, i, :], lng_bc[:, 0, :], ALU.mult, [R_v2g[i], R_tc],
                            [R_v2g[i]])
                        TT_("pool", v2n[:, i, :], v2g[:, i, :], lng_bc[:, 1, :], ALU.add, [R_v2g[i], R_tc],
                            [R_v2n[i]])
                        if J == 0 and tt == 0:
                            dump("v2n", v2n[:, 0, :], [R_v2n[0]])
                        b0 = nb2()
                        for h in range(8):
                            MM_(psum[:, b0 + h // 4, (h % 4) * 128:(h % 4 + 1) * 128], v2n[:, i, h * 128:(h + 1) * 128],
                                wsT[:, h, :], True, True, [R_v2n[i], R_tc2], [R_bank[b0], R_bank[b0 + 1]], h == 7)
                        TT_("dve", sgt[:, i, :, :], psum[:, b0:b0 + 2, :].rearrange("p b (h c) -> p (b h) c", c=128),
                            bs_bc[:, :, :], ALU.add, [R_bank[b0], R_bank[b0 + 1], R_tc], [R_sgt[i]])
                        TT_("pool", uT[:, :, tt * 128:(tt + 1) * 128], sgt[:, i, :, :], uT[:, :, tt * 128:(tt + 1) * 128],
                            ALU.mult, [R_sgt[i]] + R_uT, R_uT)
                    if J == 0:
                        dump("ysT", uT[:], R_uT)
                    for br in range(2):
                        gofs = 5120 + br * 2048
                        wb_v = w_bh_v if br == 0 else w_bs_v
                        for db in range(4):
                            wi = next_ws()
                            load_piece(("g", br, db), WS[wi][:, :, :], wi)
                            wj = 2
                            load_piece(("b", br, db), WS[wj][:, 0:8, :], wj)
                            for dj in range(4):
                                dt_ = db * 4 + dj
                                bA = nb1()
                                for k in range(16):
                                    MM_(psum[:, bA, :], WS[wi][:, k, dj * 128:(dj + 1) * 128], hTt[:, k, :], k == 0,
                                        k == 15, [R_ws[wi], R_hTt], [R_bank[bA]], k == 15)
                                bC = nb1()
                                for c in range(8):
                                    if br == 0:
                                        rhs = x2T[:, c, t0:t0 + TT]
                                        Rr = [R_x2T[c]]
                                    else:
                                        rhs = uT[:, c, :]
                                        Rr = [R_uT[c]]
                                    MM_(psum[:, bC, :], WS[wj][:, c, dj * 128:(dj + 1) * 128], rhs, c == 0, c == 7,
                                        [R_ws[wj]] + Rr, [R_bank[bC]], c == 7)
                                e1 = net()
                                ACT_(etmp[:, e1, :], psum[:, bA, :], AF.Sigmoid, [R_bank[bA]], [R_et[e1]])
                                if br == 0:
                                    TT_("dve", mT[:, dt_, :], psum[:, bC, :], etmp[:, e1, :], ALU.mult,
                                        [R_bank[bC], R_et[e1]], [R_mT[dt_]])
                                else:
                                    TT_("dve", etmp[:, e1, :], psum[:, bC, :], etmp[:, e1, :], ALU.mult,
                                        [R_bank[bC], R_et[e1]], [R_et[e1]])
                                    TT_("pool", mT[:, dt_, :], mT[:, dt_, :], etmp[:, e1, :], ALU.add,
                                        [R_et[e1], R_mT[dt_]], [R_mT[dt_]])
                    if J == 0:
                        dump("mT", mT[:], R_mT)
                    for db in range(4):
                        wi = next_ws()
                        load_piece(("o", db), WS[wi][:, :, :], wi)
                        for tt in range(4):
                            bk = nb1()
                            for k in range(16):
                                MM_(psum[:, bk, :], mT[:, k, tt * 128:(tt + 1) * 128], WS[wi][:, k, :], k == 0, k == 15,
                                    [R_ws[wi], R_mT[k]], [R_bank[bk]], k == 15)
                            TT_("dve", x1b[:, tt, db * 512:(db + 1) * 512], psum[:, bk, :],
                                x1b[:, tt, db * 512:(db + 1) * 512], ALU.add, [R_bank[bk], R_x1b[tt]], [R_x1b[tt]])
                    if J == 0:
                        dump("xmid", x1b[:, 0, :], [R_x1b[0]])
                    for tt in range(4):
                        i = tt % 2
                        norm_transpose(x1b[:, tt, :], R_x1b[tt], hTt, R_hTt, tt * 128, 16,
                                       (xn2[i], ss2[i], rs2[i]), R_scr2[i], nb2())
                    for s in range(4):
                        ai = 0
                        f = 0
                        while f < 11:
                            nfl = min(2, 11 - f)
                            fg = s * 11 + f
                            wi = next_ws()
                            load_piece(("fg", s, f), WS[wi][:, :, 0:nfl * 128], wi)
                            load_piece(("fu", s, f), WS[wi][:, :, 256:256 + nfl * 128], wi)
                            for j in range(nfl):
                                bG = nb1()
                                for k in range(16):
                                    MM_(psum[:, bG, :], WS[wi][:, k, j * 128:(j + 1) * 128], hTt[:, k, :], k == 0,
                                        k == 15, [R_ws[wi], R_hTt], [R_bank[bG]], k == 15)
                                bU = nb1()
                                for k in range(16):
                                    MM_(psum[:, bU, :], WS[wi][:, k, 256 + j * 128:256 + (j + 1) * 128], hTt[:, k, :],
                                        k == 0, k == 15, [R_ws[wi], R_hTt], [R_bank[bU]], k == 15)
                                e1 = net()
                                ACT_(etmp[:, e1, :], psum[:, bG, :], AF.Silu, [R_bank[bG]], [R_et[e1]])
                                TT_("dve", actT[:, ai, f + j, :], psum[:, bU, :], etmp[:, e1, :], ALU.mult,
                                    [R_bank[bU], R_et[e1]], [R_actT[ai][f + j]])
                            f += nfl
                        for db in range(4):
                            wi = next_ws()
                            load_piece(("fo", s, db), WS[wi][:, 0:11, :], wi)
                            for tt in range(4):
                                bk = nb1()
                                for j in range(11):
                                    MM_(psum[:, bk, :], actT[:, ai, j, tt * 128:(tt + 1) * 128], WS[wi][:, j, :],
                                        j == 0, j == 10, [R_ws[wi], R_actT[ai][j]], [R_bank[bk]], j == 10)
                                TT_("dve", x1b[:, tt, db * 512:(db + 1) * 512], psum[:, bk, :],
                                    x1b[:, tt, db * 512:(db + 1) * 512], ALU.add, [R_bank[bk], R_x1b[tt]],
                                    [R_x1b[tt]])
                    for tt in range(4):
                        i = tt % 2
                        ACT_(xn2[i][:], x1b[:, tt, :], AF.Square, [R_x1b[tt]], [R_scr2[i]], accum=ss2[i][:, 0:1])
                        ACT_(rs2[i][:, 0:1], ss2[i][:, 0:1], AF.Sqrt, [R_scr2[i], R_misc], [R_scr2[i]], scale=1.0 / D,
                             bias=epsc[:, 0:1])
                        fw.op("dve", lambda e, i=i: e.reciprocal(out=rs2[i][:, 0:1], in_=rs2[i][:, 0:1]),
                              reads=[R_scr2[i]], writes=[R_scr2[i]])
                        TS_("pool", x1b[:, tt, :], x1b[:, tt, :], rs2[i][:, 0:1], None, ALU.mult, None,
                            [R_x1b[tt], R_scr2[i]], [R_x1b[tt]])
                        TT_("pool", x1b[:, tt, :], x1b[:, tt, :], gfin_bc[:], ALU.mult, [R_x1b[tt], R_tc], [R_x1b[tt]])
                        fw.dma("sp", out_ap[t0 + tt * 128:t0 + (tt + 1) * 128, :], x1b[:, tt, :], S_out[tt],
                               reads=[R_x1b[tt]])
                for tt in range(4):
                    dbg_sems.append(S_out[tt])

        for k in ("pe", "act", "dve", "pool"):
            if fw.cnt[k] > 0:
                fw._wait("sp", (fw.sem[k], fw.cnt[k], k))
        for ds in dbg_sems:
            if ds.count > 0:
                fw._wait("sp", (ds.sem, ds.count, "dma"))
    return nc, fw


_CONST = {}


def _bf16(a):
    return np.asarray(a, dtype=np.float32).astype(ml_dtypes.bfloat16)


def host_constants():
    if _CONST:
        return _CONST
    N = 2 * L
    t = np.linspace(0.0, 1.0, L, dtype=np.float32)[:, None]
    w = (np.float32(2.0 * math.pi) * np.arange(L, dtype=np.float32)[:, None] / np.float32(L)).astype(np.float32)
    f = np.linspace(1e-4, 15, 16, dtype=np.float32)[None, :]
    z = np.concatenate([t, np.cos(f * w), -np.sin(f * w)], axis=-1).astype(np.float32)
    _CONST["zT"] = np.ascontiguousarray(z.T)
    max_decay = math.log(1e-2) / 0.3
    min_decay = math.log(1e-2) / 1.5
    deltas = np.linspace(min_decay, max_decay, DH, dtype=np.float32)
    _CONST["absd"] = np.abs(deltas).astype(np.float32)
    tn = np.linspace(0.0, 1.0, L, dtype=np.float32)
    _CONST["tneg"] = np.ascontiguousarray((-tn).reshape(16, 128).T)
    _CONST["ident"] = _bf16(np.eye(128))
    a = np.arange(NFT * 128, dtype=np.int64)
    tt_ = np.arange(L, dtype=np.int64)
    ff = a[:, None]
    ang = (2.0 * np.pi / N) * ((ff * tt_[None, :]) % N).astype(np.float64)
    valid = (a <= L)[:, None]
    C = np.where(valid, np.cos(ang), 0.0)
    S = np.where(valid, np.sin(ang), 0.0)
    Cf = C.reshape(NFT, 128, 16, 128)
    Sf = S.reshape(NFT, 128, 16, 128)
    gf = np.stack([Cf.transpose(0, 3, 2, 1), Sf.transpose(0, 3, 2, 1)], axis=2)
    _CONST["gf"] = _bf16(gf)
    wf = np.full((NFT * 128,), 2.0 / N)
    wf[0] = 1.0 / N
    wf[L] = 1.0 / N
    Ci = (C * wf[:, None]).reshape(NFT, 128, L)
    Si = (S * wf[:, None]).reshape(NFT, 128, L)
    _CONST["gi"] = _bf16(np.stack([Ci, Si], axis=2))
    return _CONST


def make_in_maps(inputs):
    c = host_constants()
    g = lambda k: np.asarray(inputs[k], dtype=np.float32)
    x = g("x")
    shared = {}
    shared["w_in"] = np.ascontiguousarray(g("w_in")[0])
    shared["w_bh"] = np.ascontiguousarray(g("w_branch_hyena")[0])
    shared["w_bs"] = np.ascontiguousarray(g("w_branch_sgu")[0])
    shared["w_out"] = np.ascontiguousarray(g("w_out")[0])
    shared["w_fi"] = np.ascontiguousarray(g("w_ffn_in")[0])
    shared["w_fo"] = np.ascontiguousarray(g("w_ffn_out")[0])
    gcol = np.concatenate([g("norm_mix_g")[0].reshape(16, 128).T, g("norm_ffn_g")[0].reshape(16, 128).T], axis=1)
    shared["gcol"] = np.ascontiguousarray(gcol)
    shared["gfin"] = np.ascontiguousarray(g("norm_final_g"))
    scw = np.concatenate([g("short_conv_w")[0], g("short_conv_b")], axis=0)
    shared["scw"] = np.ascontiguousarray(scw.reshape(4, 24, 128).transpose(2, 1, 0))
    fm = np.zeros((64, 200), np.float32)
    fm[0:33, 0:64] = g("filt_w1")[0]
    fm[:, 64:128] = g("filt_w2")[0]
    fm[:, 128:192] = g("filt_w3")[0]
    fm[:, 192] = g("filt_b1")[0]
    fm[:, 193] = g("filt_b2")[0]
    fm[:, 194] = g("filt_b3")[0]
    fm[:, 195] = g("filt_freq")[0]
    shared["fmlp"] = fm
    shared["zT"] = c["zT"]
    shared["w4"] = np.ascontiguousarray(g("filt_w4")[0])
    shared["hbias"] = np.ascontiguousarray(g("hyena_bias")[0].reshape(-1))
    shared["lngb"] = np.ascontiguousarray(np.concatenate([g("sgu_ln_g")[0], g("sgu_ln_b")[0]]))
    shared["wsT"] = np.ascontiguousarray(g("sgu_w_s")[0].transpose(2, 0, 1))
    shared["bs"] = np.ascontiguousarray(g("sgu_b_s")[0].reshape(-1))
    shared["absd"] = c["absd"]
    shared["tneg"] = c["tneg"]
    shared["ident"] = c["ident"]
    shared["gf"] = c["gf"]
    shared["gi"] = c["gi"]
    maps = []
    for b in range(x.shape[0]):
        m = dict(shared)
        m["x"] = np.ascontiguousarray(x[b])
        maps.append(m)
    return maps


_PROG = {}


def kernel(**inputs):
    if "nc" not in _PROG:
        _PROG["nc"] = build_program()[0]
    nc = _PROG["nc"]
    in_maps = make_in_maps(inputs)
    res = run_bass_kernel_spmd(nc, in_maps, core_ids=list(range(8)))
    out = np.stack([np.asarray(r["out"], dtype=np.float32) for r in res.results], axis=0)
    return out
```

```python
import math
import numpy as np
import ml_dtypes
import concourse.bass as bass
import concourse.mybir as mybir
from concourse.bass_utils import run_bass_kernel_spmd

F32 = mybir.dt.float32
BF16 = mybir.dt.bfloat16
AF = mybir.ActivationFunctionType
ALU = mybir.AluOpType

L = 2048
D = 2048
NT = 16
DH = 1024
DFF = 5632
NF = 44
CG = 256
NG = DH // CG
TT = 512
EPS = 1e-6
NFT = 17
TWO_PI = 2.0 * math.pi
MAGIC = 12582912.0
SIN_SCALE = 6.28318

DEBUG = {}
HYENA_ONLY = False


class Res:
    __slots__ = ("lw", "rd", "name")

    def __init__(self, name=""):
        self.lw = None
        self.rd = {}
        self.name = name


ALL_DSEMS = []


class DSem:
    def __init__(self, nc, name):
        self.sem = nc.alloc_semaphore(name=name)
        self.count = 0
        ALL_DSEMS.append(self)


class FW:
    def __init__(self, nc):
        self.nc = nc
        self.E = dict(pe=nc.tensor, act=nc.scalar, dve=nc.vector, pool=nc.gpsimd, sp=nc.sync)
        self.sem = {k: nc.alloc_semaphore(name="s_" + k) for k in self.E}
        self.cnt = {k: 0 for k in self.E}
        self.waited = {k: {} for k in self.E}
        self.nwait = 0
        self.ninst = 0

    def _wait(self, eng, tok):
        sem, val, src = tok
        if src == eng and eng == "pe":
            return
        key = id(sem)
        if self.waited[eng].get(key, 0) >= val:
            return
        self.E[eng].wait_ge(sem, val)
        self.waited[eng][key] = val
        self.nwait += 1

    def _deps(self, eng, reads, writes):
        for r in reads:
            if r.lw is not None:
                self._wait(eng, r.lw)
        for w in writes:
            if w.lw is not None:
                self._wait(eng, w.lw)
            for tok in w.rd.values():
                self._wait(eng, tok)

    def _record(self, tok, reads, writes):
        key = id(tok[0])
        for r in reads:
            old = r.rd.get(key)
            if old is None or old[1] < tok[1]:
                r.rd[key] = tok
        for w in writes:
            w.lw = tok
            w.rd = {}

    def op(self, eng, fn, reads=(), writes=(), track=True):
        self._deps(eng, reads, writes)
        ins = fn(self.E[eng])
        self.ninst += 1
        idx = self.cnt[eng] + 1
        if track:
            self.cnt[eng] = idx
            ins.then_inc(self.sem[eng], 1)
        self._record((self.sem[eng], idx, eng), reads, writes)
        return ins

    def barrier(self):
        for e in self.E:
            for f in self.E:
                if f != e and self.cnt[f] > 0:
                    self._wait(e, (self.sem[f], self.cnt[f], f))
            for ds in ALL_DSEMS:
                if ds.count > 0:
                    self._wait(e, (ds.sem, ds.count, "dma"))

    def dma(self, q, out, in_, dsem, reads=(), writes=()):
        self._deps(q, reads, writes)
        ins = self.E[q].dma_start(out=out, in_=in_)
        self.ninst += 1
        dsem.count += 16
        ins.then_inc(dsem.sem, 16)
        self._record((dsem.sem, dsem.count, "dma"), reads, writes)
        return ins


def _bcast_rows(ap_1d_tensor, offset, n, parts=128):
    return bass.AP(ap_1d_tensor, offset, [[0, parts], [1, n]])


def build_program(dbg=()):
    nc = bass.Bass("TRN2", target_bir_lowering=False)
    del ALL_DSEMS[:]
    fw = FW(nc)

    def din(name, shape, dt=F32):
        return nc.dram_tensor(name, list(shape), dt, kind="ExternalInput")

    x_d = din("x", [L, D])
    w_in_d = din("w_in", [D, 9216])
    w_bh_d = din("w_bh", [DH, D])
    w_bs_d = din("w_bs", [DH, D])
    w_out_d = din("w_out", [D, D])
    w_fi_d = din("w_fi", [D, 2 * DFF])
    w_fo_d = din("w_fo", [DFF, D])
    gcol_d = din("gcol", [128, 32])
    gfin_d = din("gfin", [D])
    scw_d = din("scw", [128, 24, 4])
    fmlp_d = din("fmlp", [64, 64 * 3 + 8])
    zT_d = din("zT", [33, L])
    w4_d = din("w4", [64, 4096])
    hbias_d = din("hbias", [2 * DH])
    lngb_d = din("lngb", [2 * DH])
    wsT_d = din("wsT", [128, 8, 128])
    bs_d = din("bs", [8 * 128])
    absd_d = din("absd", [DH])
    tneg_d = din("tneg", [128, 16])
    ident_d = din("ident", [128, 128], BF16)
    gf_d = din("gf", [NFT, 128, 2, 16, 128], BF16)
    gi_d = din("gi", [NFT, 128, 2, L], BF16)
    out_d = nc.dram_tensor("out", [L, D], F32, kind="ExternalOutput")
    dbg_d = {}
    for nm, shp, dt in dbg:
        dbg_d[nm] = nc.dram_tensor(nm, list(shp), dt, kind="ExternalOutput")

    x_ap = x_d.ap()
    out_ap = out_d.ap()
    w_in_v = w_in_d.ap().rearrange("(k p) e -> p k e", p=128)
    w_bh_v = w_bh_d.ap().rearrange("(k p) e -> p k e", p=128)
    w_bs_v = w_bs_d.ap().rearrange("(k p) e -> p k e", p=128)
    w_out_v = w_out_d.ap().rearrange("(k p) e -> p k e", p=128)
    w_fi_v = w_fi_d.ap().rearrange("(k p) e -> p k e", p=128)
    w_fo_v = w_fo_d.ap().rearrange("(k p) e -> p k e", p=128)

    def sb(name, shape, dt):
        return nc.sbuf_tensor('sb_' + name, list(shape), dt)
    ps = nc.psum_tensor
    from contextlib import ExitStack
    with ExitStack() as top:
        def SB(name, shape, dt):
            return top.enter_context(sb(name, list(shape), dt))

        ident = SB("ident", [128, 128], BF16)
        gcol = SB("gcol", [128, 32], F32)
        scw = SB("scw", [128, 24, 4], F32)
        fmlp = SB("fmlp", [64, 200], F32)
        tneg = SB("tneg", [128, 16], F32)
        fsc = SB("fsc", [64, 4], F32)
        ones_bf = SB("ones_bf", [128, 128], BF16)
        epsc = SB("epsc", [128, 1], F32)
        cs_const = DSem(nc, "cs_const")
        R_const = Res("const")
        for (t, src) in ((ident, ident_d.ap()), (gcol, gcol_d.ap()), (scw, scw_d.ap()),
                         (fmlp, fmlp_d.ap()), (tneg, tneg_d.ap())):
            fw.dma("sp", t[:], src, cs_const, writes=[R_const])
        R_misc = Res("misc")
        fw.op("pool", lambda e: e.memset(ones_bf[:], 1.0), writes=[R_misc])
        fw.op("pool", lambda e: e.memset(epsc[:], EPS), writes=[R_misc])
        fw.op("dve", lambda e: e.tensor_scalar(out=fsc[:, 0:1], in0=fmlp[:, 195:196], scalar1=1.0 / TWO_PI,
                                                scalar2=None, op0=ALU.mult), reads=[R_const], writes=[R_misc])
        for k in range(3):
            fw.op("dve", lambda e, k=k: e.tensor_scalar(out=fsc[:, k + 1:k + 2], in0=fmlp[:, 192 + k:193 + k],
                                                         scalar1=fsc[:, 0:1], scalar2=None, op0=ALU.mult), reads=[R_const, R_misc], writes=[R_misc])

        psum = top.enter_context(ps("psum", [128, 8, 512], F32))
        R_bank = [Res("bank%d" % i) for i in range(8)]

        def pbf(b0):
            return psum[:, b0:b0 + 2, :].bitcast(BF16).rearrange("p b (k c) -> p (b k) c", c=128)

        dbg_sems = []

        def dump(name, ap, R):
            if name in dbg_d:
                ds = DSem(nc, "dbg_" + name)
                fw.dma("sp", dbg_d[name].ap(), ap, ds, reads=R)
                dbg_sems.append(ds)

        def TT_(eng, out, in0, in1, op, R, W):
            return fw.op(eng, lambda e: e.tensor_tensor(out=out, in0=in0, in1=in1, op=op), reads=R, writes=W)

        def TS_(eng, out, in0, s1, s2, op0, op1, R, W):
            if op1 is None:
                return fw.op(eng, lambda e: e.tensor_scalar(out=out, in0=in0, scalar1=s1, scalar2=None, op0=op0),
                             reads=R, writes=W)
            return fw.op(eng, lambda e: e.tensor_scalar(out=out, in0=in0, scalar1=s1, scalar2=s2, op0=op0, op1=op1),
                         reads=R, writes=W)

        def STT_(eng, out, in0, sc, in1, op0, op1, R, W):
            return fw.op(eng, lambda e: e.scalar_tensor_tensor(out=out, in0=in0, scalar=sc, in1=in1, op0=op0, op1=op1),
                         reads=R, writes=W)

        def ACT_(out, in_, func, R, W, scale=None, bias=None, accum=None):
            kw = {}
            if scale is not None:
                kw["scale"] = scale
            if bias is not None:
                kw["bias"] = bias
            if accum is not None:
                kw["accum_out"] = accum
            return fw.op("act", lambda e: e.activation(out=out, in_=in_, func=func, **kw), reads=R, writes=W)

        def MM_(out, lhsT, rhs, start, stop, R, W, track):
            return fw.op("pe", lambda e: e.matmul(out, lhsT, rhs, start=start, stop=stop), reads=R, writes=W,
                         track=track)

        def TR_(out, in_, R, W, track):
            return fw.op("pe", lambda e: e.transpose(out, in_, ident[:]), reads=R + [R_const], writes=W, track=track)

        x2T = SB("x2T", [128, 8, L], BF16)
        R_x2T = [Res("x2T%d" % i) for i in range(8)]

        WS = []
        R_ws = []
        S_ws = [DSem(nc, "ws%d" % i) for i in range(3)]
        ws_rot = [0]

        def next_ws():
            i = ws_rot[0] % 2
            ws_rot[0] += 1
            return i

        pieces = []
        for eb in range(2):
            pieces.append((("u", eb), w_in_v[:, :, 3072 + eb * 512:3072 + (eb + 1) * 512], 16, 512))
        for eb in range(2):
            pieces.append((("v", eb), w_in_v[:, :, 4096 + eb * 512:4096 + (eb + 1) * 512], 16, 512))
        for br in range(2):
            gofs_ = 5120 + br * 2048
            wb_v_ = w_bh_v if br == 0 else w_bs_v
            for db in range(4):
                pieces.append((("g", br, db), w_in_v[:, :, gofs_ + db * 512:gofs_ + (db + 1) * 512], 16, 512))
                pieces.append((("b", br, db), wb_v_[:, :, db * 512:(db + 1) * 512], 8, 512))
        for db in range(4):
            pieces.append((("o", db), w_out_v[:, :, db * 512:(db + 1) * 512], 16, 512))
        for s_ in range(4):
            f_ = 0
            while f_ < 11:
                nfl_ = min(2, 11 - f_)
                fg_ = s_ * 11 + f_
                pieces.append((("fg", s_, f_), w_fi_v[:, :, fg_ * 128:(fg_ + nfl_) * 128], 16, nfl_ * 128))
                pieces.append((("fu", s_, f_), w_fi_v[:, :, DFF + fg_ * 128:DFF + (fg_ + nfl_) * 128], 16, nfl_ * 128))
                f_ += nfl_
            for db in range(4):
                pieces.append((("fo", s_, db), w_fo_v[:, s_ * 11:(s_ + 1) * 11, db * 512:(db + 1) * 512], 11, 512))
        wc_off = {}
        off_ = 0
        for key, src_, K_, C_ in pieces:
            wc_off[key] = (off_, K_, C_)
            off_ += 128 * K_ * C_
        wc_d = nc.dram_tensor("wcache", [off_], BF16, kind="Internal")

        def wc_ap(key):
            o_, K_, C_ = wc_off[key]
            return wc_d.ap()[o_:o_ + 128 * K_ * C_].rearrange("(p k c) -> p k c", p=128, k=K_)

        S_conv = DSem(nc, "wconv")
        conv_pos = [0]

        def emit_conv(n):
            for _ in range(n):
                if conv_pos[0] >= len(pieces):
                    return
                key, src_, K_, C_ = pieces[conv_pos[0]]
                conv_pos[0] += 1
                fw.dma("pool", wc_ap(key), src_, S_conv)

        def load_piece(key, dst, wi):
            fw.dma("sp", dst, wc_ap(key), S_ws[wi], writes=[R_ws[wi]])

        scr_d = nc.dram_tensor("scr_vx1", [16, 128, L], BF16, kind="Internal")
        R_scr = [Res() for _ in range(16)]
        S_scr = DSem(nc, "scr")

        def norm_transpose(xt_ap, R_x, dstT, R_dst, tcol, gofs, scr, R_scr, bank0):
            xn, ss, rstd = scr
            ACT_(xn[:], xt_ap, AF.Square, [R_x], [R_scr], accum=ss[:, 0:1])
            ACT_(rstd[:, 0:1], ss[:, 0:1], AF.Sqrt, [R_scr, R_misc], [R_scr], scale=1.0 / D, bias=epsc[:, 0:1])
            fw.op("dve", lambda e: e.reciprocal(out=rstd[:, 0:1], in_=rstd[:, 0:1]), reads=[R_scr], writes=[R_scr])
            ACT_(xn[:], xt_ap, AF.Copy, [R_x, R_scr], [R_scr], scale=rstd[:, 0:1])
            pv = pbf(bank0)
            Rb = [R_bank[bank0], R_bank[bank0 + 1]]
            for k in range(16):
                TR_(pv[:, k, :], xn[:, k * 128:(k + 1) * 128], [R_scr], Rb, track=(k == 15))
            for k in range(16):
                if k % 2 == 0:
                    ACT_(dstT[:, k, tcol:tcol + 128], pv[:, k, :], AF.Copy, Rb + [R_const], [R_dst],
                         scale=gcol[:, gofs + k:gofs + k + 1])
                else:
                    TS_("dve", dstT[:, k, tcol:tcol + 128], pv[:, k, :], gcol[:, gofs + k:gofs + k + 1], None,
                        ALU.mult, None, Rb + [R_const], [R_dst])

        sh = top.enter_context(ExitStack())
        h3T = sh.enter_context(sb("h3T", [64, L], F32))
        R_h3T = Res("h3T")
        with ExitStack() as st:
            zT = st.enter_context(sb("zT", [33, L], F32))
            hA = st.enter_context(sb("hA", [64, L], F32))
            hB = st.enter_context(sb("hB", [64, L], F32))
            utmp = st.enter_context(sb("utmp", [64, 2, 512], F32))
            rtmp = st.enter_context(sb("rtmp", [64, 2, 512], F32))
            R_zT, R_hA, R_hB = Res(), Res(), Res()
            R_ut = [Res(), Res()]
            R_rt = [Res(), Res()]
            S_zT = DSem(nc, "zT")
            fw.dma("sp", zT[:], zT_d.ap(), S_zT, writes=[R_zT])
            layers = [(zT, R_zT, 33, 0, hA, R_hA), (hA, R_hA, 64, 64, hB, R_hB), (hB, R_hB, 64, 128, h3T, R_h3T)]
            it = 0
            for li, (src, R_src, K, wo, dst, R_dst) in enumerate(layers):
                for tb in range(4):
                    bk = it % 2
                    it += 1
                    MM_(psum[0:64, bk, :], fmlp[0:K, wo:wo + 64], src[0:K, tb * 512:(tb + 1) * 512], True, True,
                        [R_const, R_src], [R_bank[bk]], True)
                    TS_("dve", utmp[:, bk, :], psum[0:64, bk, :], fsc[:, 0:1], fsc[:, li + 1:li + 2], ALU.mult,
                        ALU.add, [R_bank[bk], R_misc], [R_ut[bk]])
                    TS_("dve", rtmp[:, bk, :], utmp[:, bk, :], MAGIC, MAGIC, ALU.add, ALU.subtract, [R_ut[bk]],
                        [R_rt[bk]])
                    TT_("dve", utmp[:, bk, :], utmp[:, bk, :], rtmp[:, bk, :], ALU.subtract, [R_ut[bk], R_rt[bk]],
                        [R_ut[bk]])
                    ACT_(dst[:, tb * 512:(tb + 1) * 512], utmp[:, bk, :], AF.Sin, [R_ut[bk]], [R_dst],
                         scale=SIN_SCALE)
        dump("h3T", h3T[:], [R_h3T])
        fw.barrier()

        if True:

            with ExitStack() as s12:
                hT = s12.enter_context(sb("hT", [128, 16, L], BF16))
                R_hT = Res("hT")
                xin = s12.enter_context(sb("xin", [128, 2, D], F32))
                R_xin = [Res(), Res()]
                S_xin = [DSem(nc, "xin0"), DSem(nc, "xin1")]
                WS[:] = [s12.enter_context(sb("wsh%d" % i, [128, 16, 512], BF16)) for i in range(2)]
                R_ws[:] = [Res(), Res()]
                stg = s12.enter_context(sb("stg", [128, 2, L], BF16))
                R_stg = [Res(), Res()]
                xn_ = s12.enter_context(sb("xn0", [128, D], BF16))
                xn2 = [xn_, xn_]
                ss2 = [s12.enter_context(sb("ss%d" % i, [128, 1], F32)) for i in range(2)]
                rs2 = [s12.enter_context(sb("rs%d" % i, [128, 1], F32)) for i in range(2)]
                R_one = Res()
                R_scr2 = [R_one, R_one]
                for tt in range(NT):
                    i = tt % 2
                    fw.dma("sp", xin[:, i, :], x_ap[tt * 128:(tt + 1) * 128, :], S_xin[i], writes=[R_xin[i]])
                    norm_transpose(xin[:, i, :], R_xin[i], hT, R_hT, tt * 128, 0, (xn2[i], ss2[i], rs2[i]),
                                   R_scr2[i], 4 + 2 * i)
                dump("hT", hT[:], [R_hT])

                pstage = s12.enter_context(sb("pstage", [128, 2, L + 2], F32))
                R_pst = [Res(), Res()]
                tmpc = s12.enter_context(sb("tmpc", [128, 1, L], F32))
                R_tmpc = [Res()]
                for i in range(2):
                    fw.op("pool", lambda e, i=i: e.memset(pstage[:, i, 0:1], 0.0), writes=[R_pst[i]])
                    fw.op("pool", lambda e, i=i: e.memset(pstage[:, i, L + 1:L + 2], 0.0), writes=[R_pst[i]])
                grp = 0
                for eb in range(6):
                    wi = next_ws()
                    fw.dma("pool", WS[wi][:, :, :], w_in_v[:, :, eb * 512:(eb + 1) * 512], S_ws[wi],
                           writes=[R_ws[wi]])
                    for ej in range(4):
                        et = eb * 4 + ej
                        si = et % 2
                        emit_conv(1)
                        for th in range(2):
                            b0 = (grp % 4) * 2
                            grp += 1
                            for tb in range(2):
                                for k in range(16):
                                    MM_(psum[:, b0 + tb, :], WS[wi][:, k, ej * 128:(ej + 1) * 128],
                                        hT[:, k, th * 1024 + tb * 512: th * 1024 + (tb + 1) * 512],
                                        k == 0, k == 15, [R_ws[wi], R_hT], [R_bank[b0 + tb]], k == 15)
                            ACT_(pstage[:, si, 1 + th * 1024: 1 + (th + 1) * 1024],
                                 psum[:, b0:b0 + 2, :].rearrange("p b c -> p (b c)"), AF.Copy,
                                 [R_bank[b0], R_bank[b0 + 1]], [R_pst[si]])
                        tci = 0
                        if et < 16:
                            dstap = stg[:, si, :]
                            Rd = R_stg[si]
                        else:
                            dstap = x2T[:, et - 16, :]
                            Rd = R_x2T[et - 16]
                        TS_("dve", tmpc[:, tci, :], pstage[:, si, 1:L + 1], scw[:, et, 1:2], scw[:, et, 3:4], ALU.mult,
                            ALU.add, [R_pst[si], R_const], [R_tmpc[tci]])
                        STT_("dve", tmpc[:, tci, :], pstage[:, si, 0:L], scw[:, et, 0:1], tmpc[:, tci, :], ALU.mult,
                             ALU.add, [R_pst[si], R_tmpc[tci], R_const], [R_tmpc[tci]])
                        STT_("dve", dstap, pstage[:, si, 2:L + 2], scw[:, et, 2:3], tmpc[:, tci, :], ALU.mult,
                             ALU.add, [R_pst[si], R_tmpc[tci], R_const], [Rd])
                        if et < 16:
                            fw.dma("sp", scr_d.ap()[et], stg[:, si, :], S_scr, reads=[R_stg[si]], writes=[R_scr[et]])
                            if et == 0:
                                dump("vT0", stg[:, si, :], [R_stg[si]])
            for r_ in R_scr:
                r_.lw = (S_scr.sem, S_scr.count, "dma")
            dump("x2T_pre", x2T[:], R_x2T)
            fw.barrier()

            with ExitStack() as s3:
                DHt = s3.enter_context(sb("DHt", [128, 16, 5, CG], BF16))
                R_DHd = Res("DHdata")
                R_DHf = [Res("DHfiltA"), Res("DHfiltB")]
                Yr = s3.enter_context(sb("Yr", [128, NFT, CG], BF16))
                Ys = s3.enter_context(sb("Ys", [128, NFT, CG], BF16))
                R_Y = [Res() for _ in range(NFT)]
                z1T = s3.enter_context(sb("z1T", [128, 2, L], BF16))
                R_z1T = [Res(), Res()]
                vg = s3.enter_context(sb("vg", [128, 2, L], BF16))
                x1g = s3.enter_context(sb("x1g", [128, 2, L], BF16))
                R_vg = [Res(), Res()]
                R_x1g = [Res(), Res()]
                S_vg = DSem(nc, "vg")
                S_x1g = DSem(nc, "x1g")
                tabs = [s3.enter_context(sb("tab%d" % i, [128, 2, L], BF16)) for i in range(3)]
                R_tab = [Res() for _ in range(3)]
                S_tab = [DSem(nc, "tab%d" % i) for i in range(3)]
                tab_rot = [0]
                w4g = s3.enter_context(sb("w4g", [64, 2, 2, CG], F32))
                R_w4g = [Res(), Res()]
                S_w4g = [DSem(nc, "w4g0"), DSem(nc, "w4g1")]
                hb_bc = s3.enter_context(sb("hb_bc", [128, 2, 2, CG], F32))
                absd_bc = s3.enter_context(sb("absd_bc", [128, 2, CG], F32))
                R_gc = [Res("groupconst0"), Res("groupconst1")]
                S_gc = [DSem(nc, "groupconst0"), DSem(nc, "groupconst1")]
                dec = s3.enter_context(sb("dec", [128, 2, CG], F32))
                R_dec = [Res(), Res()]
                hdec = s3.enter_context(sb("hdec", [128, 2, 2, CG], F32))
                R_hdec = [Res(), Res()]
                habs = s3.enter_context(sb("habs", [128, 2, 2, CG], BF16))
                R_habs = [Res(), Res()]
                rn = s3.enter_context(sb("rn", [128, 2, CG], F32))
                R_rn = [Res(), Res()]
                ytmp = s3.enter_context(sb("ytmp", [128, 2, 6, CG], F32))
                R_yt = [[Res() for _ in range(6)] for _ in range(2)]

                def load_group_consts(g):
                    gp = g % 2
                    c0 = g * CG
                    fw.dma("sp", absd_bc[:, gp, :], _bcast_rows(absd_d, c0, CG), S_gc[gp], writes=[R_gc[gp]])
                    for o in range(2):
                        fw.dma("sp", hb_bc[:, gp, o, :], _bcast_rows(hbias_d, o * DH + c0, CG), S_gc[gp],
                               writes=[R_gc[gp]])

                def filter_steps(g, o, fs):
                    gp = g % 2
                    c0 = g * CG
                    for d in range(2):
                        col = d * 2048 + o * DH + c0
                        fw.dma("sp", w4g[:, fs, d, :], w4_d.ap()[:, col:col + CG], S_w4g[fs], writes=[R_w4g[fs]])
                    nb = 6 + fs
                    for k in range(NT + 1):
                        if k < NT:
                            nt = k
                            i = nt % 2
                            fb = 4 + i
                            MM_(psum[:, fb, :], h3T[0:64, nt * 128:(nt + 1) * 128],
                                w4g[0:64, fs, :, :].rearrange("p d c -> p (d c)"), True, True,
                                [R_h3T, R_w4g[fs]], [R_bank[fb]], True)
                            ACT_(dec[:, i, :], absd_bc[:, gp, :], AF.Exp, [R_gc[gp], R_const], [R_dec[i]],
                                 scale=tneg[:, nt:nt + 1])
                            for d in range(2):
                                TT_("dve", hdec[:, i, d, :], psum[:, fb, d * CG:(d + 1) * CG], dec[:, i, :], ALU.mult,
                                    [R_bank[fb], R_dec[i]], [R_hdec[i]])
                            if nt == 0:
                                fw.op("pool", lambda e, i=i: e.memset(hdec[0:1, i, 1, :], 0.0), writes=[R_hdec[i]])
                            ACT_(habs[:, i, :, :], hdec[:, i, :, :], AF.Abs, [R_hdec[i]], [R_habs[i]])
                            TT_("pool", DHt[:, nt, 1 + 2 * fs, :], hdec[:, i, 0, :], hdec[:, i, 1, :], ALU.add,
                                [R_hdec[i]], [R_DHf[fs]])
                            TT_("pool", DHt[:, nt, 2 + 2 * fs, :], hdec[:, i, 1, :], hdec[:, i, 0, :], ALU.subtract,
                                [R_hdec[i]], [R_DHf[fs]])
                        if k >= 1:
                            nt = k - 1
                            i = nt % 2
                            for d in range(2):
                                MM_(psum[:, nb, 0:CG], ones_bf[:], habs[:, i, d, :], (nt == 0 and d == 0),
                                    (nt == NT - 1 and d == 1), [R_misc, R_habs[i]], [R_bank[nb]], d == 1)
                        if k == NT:
                            fw.op("dve", lambda e, fs=fs, nb=nb: e.reciprocal(out=rn[:, fs, :], in_=psum[:, nb, 0:CG]),
                                  reads=[R_bank[nb]], writes=[R_rn[fs]])
                        yield

                convs = [(g, o) for g in range(NG) for o in range(2)]
                load_group_consts(0)
                for _ in filter_steps(0, 0, 0):
                    pass
                for ci, (g, o) in enumerate(convs):
                    fs = ci % 2
                    gp = g % 2
                    if o == 0:
                        for ct in range(2):
                            fw.dma("sp", vg[:, ct, :], scr_d.ap()[g * 2 + ct], S_vg, reads=[R_scr[g * 2 + ct]],
                                   writes=[R_vg[ct]])
                            fw.dma("sp", x1g[:, ct, :], scr_d.ap()[8 + g * 2 + ct], S_x1g,
                                   reads=[R_scr[8 + g * 2 + ct]], writes=[R_x1g[ct]])
                        for ct in range(2):
                            R_vg[ct].lw = (S_vg.sem, S_vg.count, "dma")
                            R_x1g[ct].lw = (S_x1g.sem, S_x1g.count, "dma")
                    nxt = None
                    if ci + 1 < len(convs):
                        g2, o2 = convs[ci + 1]
                        if o2 == 0:
                            load_group_consts(g2)
                        nxt = filter_steps(g2, o2, 1 - fs)

                    for ct in range(2):
                        src = vg[:, ct, :] if o == 0 else z1T[:, ct, :]
                        R_src = R_vg[ct] if o == 0 else R_z1T[ct]
                        b0 = 2 * ct
                        pv = pbf(b0)
                        Rb = [R_bank[b0], R_bank[b0 + 1]]
                        for tt in range(NT):
                            TR_(pv[:, tt, :], src[:, tt * 128:(tt + 1) * 128], [R_src], Rb, track=(tt == NT - 1))
                        ACT_(DHt[:, :, 0, ct * 128:(ct + 1) * 128], pv[:, :, :], AF.Copy, Rb, [R_DHd])

                    for ft in range(NFT):
                        if ft % 4 == 0:
                            emit_conv(1)
                        ti = tab_rot[0] % 3
                        tab_rot[0] += 1
                        tf = tabs[ti][:, :, :].rearrange("p s (a f) -> p s a f", f=128)
                        fw.dma("sp", tabs[ti][:, :, :], gf_d.ap()[ft].rearrange("p s a f -> p s (a f)"), S_tab[ti],
                               writes=[R_tab[ti]])
                        bc = (ft % 2) * 2
                        bs_ = bc + 1
                        hi_c = 1 + 2 * fs
                        hi_s = 2 + 2 * fs
                        for a in range(16):
                            MM_(psum[:, bc, :].rearrange("p (b c) -> p b c", b=2), tf[:, 0, a, :],
                                DHt[:, a, 0:hi_c + 1:hi_c, :], a == 0, a == 15, [R_tab[ti], R_DHd, R_DHf[fs]],
                                [R_bank[bc]], a == 15)
                        for a in range(16):
                            MM_(psum[:, bs_, :].rearrange("p (b c) -> p b c", b=2), tf[:, 1, a, :],
                                DHt[:, a, 0:hi_s + 1:hi_s, :], a == 0, a == 15, [R_tab[ti], R_DHd, R_DHf[fs]],
                                [R_bank[bs_]], a == 15)
                        if nxt is not None:
                            next(nxt, None)
                        yi = ft % 2
                        kr, ki, a1, b1, c1, d1 = [ytmp[:, yi, j, :] for j in range(6)]
                        Rk = R_yt[yi]
                        ZC = psum[:, bc, 0:CG]
                        KC = psum[:, bc, CG:2 * CG]
                        ZS = psum[:, bs_, 0:CG]
                        KS = psum[:, bs_, CG:2 * CG]
                        TT_("dve", kr, KC, rn[:, fs, :], ALU.mult, [R_bank[bc], R_rn[fs]], [Rk[0]])
                        TT_("pool", kr, kr, hb_bc[:, gp, o, :], ALU.add, [Rk[0], R_gc[gp]], [Rk[0]])
                        TT_("dve", ki, KS, rn[:, fs, :], ALU.mult, [R_bank[bs_], R_rn[fs]], [Rk[1]])
                        TT_("dve", a1, ZC, kr, ALU.mult, [R_bank[bc], Rk[0]], [Rk[2]])
                        TT_("dve", b1, ZS, ki, ALU.mult, [R_bank[bs_], Rk[1]], [Rk[3]])
                        TT_("pool", Yr[:, ft, :], a1, b1, ALU.add, [Rk[2], Rk[3]], [R_Y[ft]])
                        TT_("dve", c1, ZS, kr, ALU.mult, [R_bank[bs_], Rk[0]], [Rk[4]])
                        TT_("dve", d1, ZC, ki, ALU.mult, [R_bank[bc], Rk[1]], [Rk[5]])
                        TT_("pool", Ys[:, ft, :], c1, d1, ALU.subtract, [Rk[4], Rk[5]], [R_Y[ft]])
                    if nxt is not None:
                        for _ in nxt:
                            pass

                    for a in range(NFT):
                        if a % 4 == 1:
                            emit_conv(1)
                        ti = tab_rot[0] % 3
                        tab_rot[0] += 1
                        fw.dma("sp", tabs[ti][:, :, :], gi_d.ap()[a], S_tab[ti], writes=[R_tab[ti]])
                        for ct in range(2):
                            for tb in range(4):
                                bk = ct * 4 + tb
                                MM_(psum[:, bk, :], Yr[:, a, ct * 128:(ct + 1) * 128],
                                    tabs[ti][:, 0, tb * 512:(tb + 1) * 512], a == 0, False,
                                    [R_tab[ti], R_Y[a]], [R_bank[bk]], False)
                                MM_(psum[:, bk, :], Ys[:, a, ct * 128:(ct + 1) * 128],
                                    tabs[ti][:, 1, tb * 512:(tb + 1) * 512], False, a == NFT - 1,
                                    [R_tab[ti], R_Y[a]], [R_bank[bk]], (a == NFT - 1) or (ct == 1 and tb == 3))
                    for ct in range(2):
                        for tb in range(4):
                            bk = ct * 4 + tb
                            cs = slice(tb * 512, (tb + 1) * 512)
                            eng = "dve" if tb % 2 == 0 else "dve"
                            if o == 0:
                                TT_(eng, z1T[:, ct, cs], psum[:, bk, :], x1g[:, ct, cs], ALU.mult,
                                    [R_bank[bk], R_x1g[ct]], [R_z1T[ct]])
                            else:
                                TT_(eng, x2T[:, g * 2 + ct, cs], psum[:, bk, :], x2T[:, g * 2 + ct, cs], ALU.mult,
                                    [R_bank[bk], R_x2T[g * 2 + ct]], [R_x2T[g * 2 + ct]])
                    if g == 0 and o == 0:
                        dump("z1T", z1T[:], R_z1T)
        emit_conv(len(pieces))
        sh.close()
        dump("yhT", x2T[:], R_x2T)
        fw.barrier()

        if not HYENA_ONLY:
            with ExitStack() as s4:
                x1b = s4.enter_context(sb("x1b", [128, 4, D], F32))
                R_x1b = [Res() for _ in range(4)]
                S_x1b = [DSem(nc, "x1b%d" % i) for i in range(4)]
                S_out = [DSem(nc, "outs%d" % i) for i in range(4)]
                hTt = s4.enter_context(sb("hTt", [128, 16, TT], BF16))
                R_hTt = Res("hTt")
                WS[:] = [s4.enter_context(sb("wst%d" % i, [128, 16, 512], BF16)) for i in range(2)]
                WS.append(s4.enter_context(sb("wst2", [128, 8, 512], BF16)))
                R_ws[:] = [Res(), Res(), Res()]
                xn_ = s4.enter_context(sb("xnt0", [128, D], BF16))
                xn2 = [xn_, xn_]
                ss2 = [s4.enter_context(sb("sst%d" % i, [128, 1], F32)) for i in range(2)]
                rs2 = [s4.enter_context(sb("rst%d" % i, [128, 1], F32)) for i in range(2)]
                R_one = Res()
                R_scr2 = [R_one, R_one]
                gfin_bc = s4.enter_context(sb("gfin_bc", [128, D], F32))
                lng_bc = s4.enter_context(sb("lng_bc", [128, 2, DH], F32))
                bs_bc = s4.enter_context(sb("bs_bc", [128, 8, 128], F32))
                wsT = s4.enter_context(sb("wsT", [128, 8, 128], BF16))
                R_tc = Res("tconst")
                S_tc = DSem(nc, "tconst")
                fw.dma("sp", gfin_bc[:], _bcast_rows(gfin_d, 0, D), S_tc, writes=[R_tc])
                fw.dma("sp", lng_bc[:, :, :].rearrange("p a c -> p (a c)"), _bcast_rows(lngb_d, 0, 2 * DH), S_tc,
                       writes=[R_tc])
                fw.dma("sp", bs_bc[:, :, :].rearrange("p a c -> p (a c)"), _bcast_rows(bs_d, 0, 1024), S_tc,
                       writes=[R_tc])
                S_tc2 = DSem(nc, "tconst2")
                R_tc2 = Res("tconst2")
                fw.dma("pool", wsT[:], wsT_d.ap(), S_tc2, writes=[R_tc2])
                uT = s4.enter_context(sb("uT", [128, 8, TT], BF16))
                R_uT = [Res() for _ in range(8)]
                mT = s4.enter_context(sb("mT", [128, 16, TT], BF16))
                R_mT = [Res() for _ in range(16)]
                v2g = s4.enter_context(sb("v2g", [128, 1, DH], F32))
                R_v2g = [Res()] * 2
                v2n = s4.enter_context(sb("v2n", [128, 1, DH], BF16))
                R_v2n = [Res()] * 2
                lnst = s4.enter_context(sb("lnst", [128, 1, 8], F32))
                sgt = s4.enter_context(sb("sgt", [128, 1, 8, 128], F32))
                R_sgt = [Res()] * 2
                etmp = s4.enter_context(sb("etmp", [128, 2, TT], F32))
                R_et = [Res() for _ in range(2)]
                actT = s4.enter_context(sb("actT", [128, 1, 11, TT], BF16))
                R_actT = [[Res() for _ in range(11)]] * 2
                bank_rot = [0]

                def nb1():
                    b = bank_rot[0] % 8
                    bank_rot[0] += 1
                    return b

                def nb2():
                    if bank_rot[0] % 2:
                        bank_rot[0] += 1
                    b = bank_rot[0] % 8
                    bank_rot[0] += 2
                    return b

                et_rot = [0]

                def net():
                    i = et_rot[0] % 2
                    et_rot[0] += 1
                    return i

                for J in range(L // TT):
                    t0 = J * TT
                    for tt in range(4):
                        fw.dma("sp", x1b[:, tt, :], x_ap[t0 + tt * 128:t0 + (tt + 1) * 128, :], S_x1b[tt],
                               writes=[R_x1b[tt]])
                    for tt in range(4):
                        i = tt % 2
                        norm_transpose(x1b[:, tt, :], R_x1b[tt], hTt, R_hTt, tt * 128, 0,
                                       (xn2[i], ss2[i], rs2[i]), R_scr2[i], nb2())
                    if J == 0:
                        dump("hTt", hTt[:], [R_hTt])
                    for eb in range(2):
                        wi = next_ws()
                        load_piece(("u", eb), WS[wi][:, :, :], wi)
                        for ej in range(4):
                            et = eb * 4 + ej
                            bk = nb1()
                            for k in range(16):
                                MM_(psum[:, bk, :], WS[wi][:, k, ej * 128:(ej + 1) * 128], hTt[:, k, :], k == 0,
                                    k == 15, [R_ws[wi], R_hTt], [R_bank[bk]], k == 15)
                            ACT_(uT[:, et, :], psum[:, bk, :], AF.Gelu, [R_bank[bk]], [R_uT[et]])
                    wv = []
                    for eb in range(2):
                        wi = next_ws()
                        load_piece(("v", eb), WS[wi][:, :, :], wi)
                        wv.append(wi)
                    for tt in range(4):
                        i = 0
                        for eb in range(2):
                            wi = wv[eb]
                            bk = nb1()
                            for k in range(16):
                                MM_(psum[:, bk, :], hTt[:, k, tt * 128:(tt + 1) * 128], WS[wi][:, k, :], k == 0,
                                    k == 15, [R_ws[wi], R_hTt], [R_bank[bk]], k == 15)
                            ACT_(v2g[:, i, eb * 512:(eb + 1) * 512], psum[:, bk, :], AF.Gelu, [R_bank[bk]],
                                 [R_v2g[i]], accum=lnst[:, i, eb:eb + 1])
                        st_ = lnst[:, i, :]
                        TS_("dve", st_[:, 2:3], st_[:, 0:1], st_[:, 1:2], -1.0 / DH, ALU.add, ALU.mult, [R_v2g[i]],
                            [R_v2g[i]])
                        ACT_(v2n[:, i, :], v2g[:, i, :], AF.Square, [R_v2g[i]], [R_v2n[i]], bias=st_[:, 2:3],
                             accum=st_[:, 3:4])
                        ACT_(st_[:, 4:5], st_[:, 3:4], AF.Sqrt, [R_v2n[i], R_v2g[i], R_misc], [R_v2g[i]], scale=1.0 / DH,
                             bias=epsc[:, 0:1])
                        fw.op("dve", lambda e, st_=st_: e.reciprocal(out=st_[:, 4:5], in_=st_[:, 4:5]),
                              reads=[R_v2g[i]], writes=[R_v2g[i]])
                        TS_("dve", v2g[:, i, :], v2g[:, i, :], st_[:, 2:3], st_[:, 4:5], ALU.add, ALU.mult,
                            [R_v2g[i]], [R_v2g[i]])
                        TT_("pool", v2g[:, i, :], v2g[:, i, :], lng_bc[:, 0, :], ALU.mult, [R_v2g[i], R_tc],
                            [R_v2g[i]])
                        TT_("pool", v2n[:, i, :], v2g[:, i, :], lng_bc[:, 1, :], ALU.add, [R_v2g[i], R_tc],
                            [R_v2n[i]])
                        if J == 0 and tt == 0:
                            dump("v2n", v2n[:, 0, :], [R_v2n[0]])
                        b0 = nb2()
                        for h in range(8):
                            MM_(psum[:, b0 + h // 4, (h % 4) * 128:(h % 4 + 1) * 128], v2n[:, i, h * 128:(h + 1) * 128],
                                wsT[:, h, :], True, True, [R_v2n[i], R_tc2], [R_bank[b0], R_bank[b0 + 1]], h == 7)
                        TT_("dve", sgt[:, i, :, :], psum[:, b0:b0 + 2, :].rearrange("p b (h c) -> p (b h) c", c=128),
                            bs_bc[:, :, :], ALU.add, [R_bank[b0], R_bank[b0 + 1], R_tc], [R_sgt[i]])
                        TT_("pool", uT[:, :, tt * 128:(tt + 1) * 128], sgt[:, i, :, :], uT[:, :, tt * 128:(tt + 1) * 128],
                            ALU.mult, [R_sgt[i]] + R_uT, R_uT)
                    if J == 0:
                        dump("ysT", uT[:], R_uT)
                    for br in range(2):
                        gofs = 5120 + br * 2048
                        wb_v = w_bh_v if br == 0 else w_bs_v
                        for db in range(4):
                            wi = next_ws()
                            load_piece(("g", br, db), WS[wi][:, :, :], wi)
                            wj = 2
                            load_piece(("b", br, db), WS[wj][:, 0:8, :], wj)
                            for dj in range(4):
                                dt_ = db * 4 + dj
                                bA = nb1()
                                for k in range(16):
                                    MM_(psum[:, bA, :], WS[wi][:, k, dj * 128:(dj + 1) * 128], hTt[:, k, :], k == 0,
                                        k == 15, [R_ws[wi], R_hTt], [R_bank[bA]], k == 15)
                                bC = nb1()
                                for c in range(8):
                                    if br == 0:
                                        rhs = x2T[:, c, t0:t0 + TT]
                                        Rr = [R_x2T[c]]
                                    else:
                                        rhs = uT[:, c, :]
                                        Rr = [R_uT[c]]
                                    MM_(psum[:, bC, :], WS[wj][:, c, dj * 128:(dj + 1) * 128], rhs, c == 0, c == 7,
                                        [R_ws[wj]] + Rr, [R_bank[bC]], c == 7)
                                e1 = net()
                                ACT_(etmp[:, e1, :], psum[:, bA, :], AF.Sigmoid, [R_bank[bA]], [R_et[e1]])
                                if br == 0:
                                    TT_("dve", mT[:, dt_, :], psum[:, bC, :], etmp[:, e1, :], ALU.mult,
                                        [R_bank[bC], R_et[e1]], [R_mT[dt_]])
                                else:
                                    TT_("dve", etmp[:, e1, :], psum[:, bC, :], etmp[:, e1, :], ALU.mult,
                                        [R_bank[bC], R_et[e1]], [R_et[e1]])
                                    TT_("pool", mT[:, dt_, :], mT[:, dt_, :], etmp[:, e1, :], ALU.add,
                                        [R_et[e1], R_mT[dt_]], [R_mT[dt_]])
                    if J == 0:
                        dump("mT", mT[:], R_mT)
                    for db in range(4):
                        wi = next_ws()
                        load_piece(("o", db), WS[wi][:, :, :], wi)
                        for tt in range(4):
                            bk = nb1()
                            for k in range(16):
                                MM_(psum[:, bk, :], mT[:, k, tt * 128:(tt + 1) * 128], WS[wi][:, k, :], k == 0, k == 15,
                                    [R_ws[wi], R_mT[k]], [R_bank[bk]], k == 15)
                            TT_("dve", x1b[:, tt, db * 512:(db + 1) * 512], psum[:, bk, :],
                                x1b[:, tt, db * 512:(db + 1) * 512], ALU.add, [R_bank[bk], R_x1b[tt]], [R_x1b[tt]])
                    if J == 0:
                        dump("xmid", x1b[:, 0, :], [R_x1b[0]])
                    for tt in range(4):
                        i = tt % 2
                        norm_transpose(x1b[:, tt, :], R_x1b[tt], hTt, R_hTt, tt * 128, 16,
                                       (xn2[i], ss2[i], rs2[i]), R_scr2[i], nb2())
                    for s in range(4):
                        ai = 0
                        f = 0
                        while f < 11:
                            nfl = min(2, 11 - f)
                            fg = s * 11 + f
                            wi = next_ws()
                            load_piece(("fg", s, f), WS[wi][:, :, 0:nfl * 128], wi)
                            load_piece(("fu", s, f), WS[wi][:, :, 256:256 + nfl * 128], wi)
                            for j in range(nfl):
                                bG = nb1()
                                for k in range(16):
                                    MM_(psum[:, bG, :], WS[wi][:, k, j * 128:(j + 1) * 128], hTt[:, k, :], k == 0,
                                        k == 15, [R_ws[wi], R_hTt], [R_bank[bG]], k == 15)
                                bU = nb1()
                                for k in range(16):
                                    MM_(psum[:, bU, :], WS[wi][:, k, 256 + j * 128:256 + (j + 1) * 128], hTt[:, k, :],
                                        k == 0, k == 15, [R_ws[wi], R_hTt], [R_bank[bU]], k == 15)
                                e1 = net()
                                ACT_(etmp[:, e1, :], psum[:, bG, :], AF.Silu, [R_bank[bG]], [R_et[e1]])
                                TT_("dve", actT[:, ai, f + j, :], psum[:, bU, :], etmp[:, e1, :], ALU.mult,
                                    [R_bank[bU], R_et[e1]], [R_actT[ai][f + j]])
                            f += nfl
                        for db in range(4):
                            wi = next_ws()
                            load_piece(("fo", s, db), WS[wi][:, 0:11, :], wi)
                            for tt in range(4):
                                bk = nb1()
                                for j in range(11):
                                    MM_(psum[:, bk, :], actT[:, ai, j, tt * 128:(tt + 1) * 128], WS[wi][:, j, :],
                                        j == 0, j == 10, [R_ws[wi], R_actT[ai][j]], [R_bank[bk]], j == 10)
                                TT_("dve", x1b[:, tt, db * 512:(db + 1) * 512], psum[:, bk, :],
                                    x1b[:, tt, db * 512:(db + 1) * 512], ALU.add, [R_bank[bk], R_x1b[tt]],
                                    [R_x1b[tt]])
                    for tt in range(4):
                        i = tt % 2
                        ACT_(xn2[i][:], x1b[:, tt, :], AF.Square, [R_x1b[tt]], [R_scr2[i]], accum=ss2[i][:, 0:1])
                        ACT_(rs2[i][:, 0:1], ss2[i][:, 0:1], AF.Sqrt, [R_scr2[i], R_misc], [R_scr2[i]], scale=1.0 / D,
                             bias=epsc[:, 0:1])
                        fw.op("dve", lambda e, i=i: e.reciprocal(out=rs2[i][:, 0:1], in_=rs2[i][:, 0:1]),
                              reads=[R_scr2[i]], writes=[R_scr2[i]])
                        TS_("pool", x1b[:, tt, :], x1b[:, tt, :], rs2[i][:, 0:1], None, ALU.mult, None,
                            [R_x1b[tt], R_scr2[i]], [R_x1b[tt]])
                        TT_("pool", x1b[:, tt, :], x1b[:, tt, :], gfin_bc[:], ALU.mult, [R_x1b[tt], R_tc], [R_x1b[tt]])
                        fw.dma("sp", out_ap[t0 + tt * 128:t0 + (tt + 1) * 128, :], x1b[:, tt, :], S_out[tt],
                               reads=[R_x1b[tt]])
                for tt in range(4):
                    dbg_sems.append(S_out[tt])

        for k in ("pe", "act", "dve", "pool"):
            if fw.cnt[k] > 0:
                fw._wait("sp", (fw.sem[k], fw.cnt[k], k))
        for ds in dbg_sems:
            if ds.count > 0:
                fw._wait("sp", (ds.sem, ds.count, "dma"))
    return nc, fw


_CONST = {}


def _bf16(a):
    return np.asarray(a, dtype=np.float32).astype(ml_dtypes.bfloat16)


def host_constants():
    if _CONST:
        return _CONST
    N = 2 * L
    t = np.linspace(0.0, 1.0, L, dtype=np.float32)[:, None]
    w = (np.float32(2.0 * math.pi) * np.arange(L, dtype=np.float32)[:, None] / np.float32(L)).astype(np.float32)
    f = np.linspace(1e-4, 15, 16, dtype=np.float32)[None, :]
    z = np.concatenate([t, np.cos(f * w), -np.sin(f * w)], axis=-1).astype(np.float32)
    _CONST["zT"] = np.ascontiguousarray(z.T)
    max_decay = math.log(1e-2) / 0.3
    min_decay = math.log(1e-2) / 1.5
    deltas = np.linspace(min_decay, max_decay, DH, dtype=np.float32)
    _CONST["absd"] = np.abs(deltas).astype(np.float32)
    tn = np.linspace(0.0, 1.0, L, dtype=np.float32)
    _CONST["tneg"] = np.ascontiguousarray((-tn).reshape(16, 128).T)
    _CONST["ident"] = _bf16(np.eye(128))
    a = np.arange(NFT * 128, dtype=np.int64)
    tt_ = np.arange(L, dtype=np.int64)
    ff = a[:, None]
    ang = (2.0 * np.pi / N) * ((ff * tt_[None, :]) % N).astype(np.float64)
    valid = (a <= L)[:, None]
    C = np.where(valid, np.cos(ang), 0.0)
    S = np.where(valid, np.sin(ang), 0.0)
    Cf = C.reshape(NFT, 128, 16, 128)
    Sf = S.reshape(NFT, 128, 16, 128)
    gf = np.stack([Cf.transpose(0, 3, 2, 1), Sf.transpose(0, 3, 2, 1)], axis=2)
    _CONST["gf"] = _bf16(gf)
    wf = np.full((NFT * 128,), 2.0 / N)
    wf[0] = 1.0 / N
    wf[L] = 1.0 / N
    Ci = (C * wf[:, None]).reshape(NFT, 128, L)
    Si = (S * wf[:, None]).reshape(NFT, 128, L)
    _CONST["gi"] = _bf16(np.stack([Ci, Si], axis=2))
    return _CONST


def make_in_maps(inputs):
    c = host_constants()
    g = lambda k: np.asarray(inputs[k], dtype=np.float32)
    x = g("x")
    shared = {}
    shared["w_in"] = np.ascontiguousarray(g("w_in")[0])
    shared["w_bh"] = np.ascontiguousarray(g("w_branch_hyena")[0])
    shared["w_bs"] = np.ascontiguousarray(g("w_branch_sgu")[0])
    shared["w_out"] = np.ascontiguousarray(g("w_out")[0])
    shared["w_fi"] = np.ascontiguousarray(g("w_ffn_in")[0])
    shared["w_fo"] = np.ascontiguousarray(g("w_ffn_out")[0])
    gcol = np.concatenate([g("norm_mix_g")[0].reshape(16, 128).T, g("norm_ffn_g")[0].reshape(16, 128).T], axis=1)
    shared["gcol"] = np.ascontiguousarray(gcol)
    shared["gfin"] = np.ascontiguousarray(g("norm_final_g"))
    scw = np.concatenate([g("short_conv_w")[0], g("short_conv_b")], axis=0)
    shared["scw"] = np.ascontiguousarray(scw.reshape(4, 24, 128).transpose(2, 1, 0))
    fm = np.zeros((64, 200), np.float32)
    fm[0:33, 0:64] = g("filt_w1")[0]
    fm[:, 64:128] = g("filt_w2")[0]
    fm[:, 128:192] = g("filt_w3")[0]
    fm[:, 192] = g("filt_b1")[0]
    fm[:, 193] = g("filt_b2")[0]
    fm[:, 194] = g("filt_b3")[0]
    fm[:, 195] = g("filt_freq")[0]
    shared["fmlp"] = fm
    shared["zT"] = c["zT"]
    shared["w4"] = np.ascontiguousarray(g("filt_w4")[0])
    shared["hbias"] = np.ascontiguousarray(g("hyena_bias")[0].reshape(-1))
    shared["lngb"] = np.ascontiguousarray(np.concatenate([g("sgu_ln_g")[0], g("sgu_ln_b")[0]]))
    shared["wsT"] = np.ascontiguousarray(g("sgu_w_s")[0].transpose(2, 0, 1))
    shared["bs"] = np.ascontiguousarray(g("sgu_b_s")[0].reshape(-1))
    shared["absd"] = c["absd"]
    shared["tneg"] = c["tneg"]
    shared["ident"] = c["ident"]
    shared["gf"] = c["gf"]
    shared["gi"] = c["gi"]
    maps = []
    for b in range(x.shape[0]):
        m = dict(shared)
        m["x"] = np.ascontiguousarray(x[b])
        maps.append(m)
    return maps


_PROG = {}


def kernel(**inputs):
    if "nc" not in _PROG:
        _PROG["nc"] = build_program()[0]
    nc = _PROG["nc"]
    in_maps = make_in_maps(inputs)
    res = run_bass_kernel_spmd(nc, in_maps, core_ids=list(range(8)))
    out = np.stack([np.asarray(r["out"], dtype=np.float32) for r in res.results], axis=0)
    return out
```

```python
import math
import numpy as np
import ml_dtypes
import concourse.bass as bass
import concourse.mybir as mybir
from concourse.bass_utils import run_bass_kernel_spmd

F32 = mybir.dt.float32
BF16 = mybir.dt.bfloat16
AF = mybir.ActivationFunctionType
ALU = mybir.AluOpType

L = 2048
D = 2048
NT = 16
DH = 1024
DFF = 5632
NF = 44
CG = 256
NG = DH // CG
TT = 512
EPS = 1e-6
NFT = 17
TWO_PI = 2.0 * math.pi
MAGIC = 12582912.0
SIN_SCALE = 6.28318

DEBUG = {}
HYENA_ONLY = False


class Res:
    __slots__ = ("lw", "rd", "name")

    def __init__(self, name=""):
        self.lw = None
        self.rd = {}
        self.name = name


ALL_DSEMS = []


class DSem:
    def __init__(self, nc, name):
        self.sem = nc.alloc_semaphore(name=name)
        self.count = 0
        ALL_DSEMS.append(self)


class FW:
    def __init__(self, nc):
        self.nc = nc
        self.E = dict(pe=nc.tensor, act=nc.scalar, dve=nc.vector, pool=nc.gpsimd, sp=nc.sync)
        self.sem = {k: nc.alloc_semaphore(name="s_" + k) for k in self.E}
        self.cnt = {k: 0 for k in self.E}
        self.waited = {k: {} for k in self.E}
        self.nwait = 0
        self.ninst = 0

    def _wait(self, eng, tok):
        sem, val, src = tok
        if src == eng and eng == "pe":
            return
        key = id(sem)
        if self.waited[eng].get(key, 0) >= val:
            return
        self.E[eng].wait_ge(sem, val)
        self.waited[eng][key] = val
        self.nwait += 1

    def _deps(self, eng, reads, writes):
        for r in reads:
            if r.lw is not None:
                self._wait(eng, r.lw)
        for w in writes:
            if w.lw is not None:
                self._wait(eng, w.lw)
            for tok in w.rd.values():
                self._wait(eng, tok)

    def _record(self, tok, reads, writes):
        key = id(tok[0])
        for r in reads:
            old = r.rd.get(key)
            if old is None or old[1] < tok[1]:
                r.rd[key] = tok
        for w in writes:
            w.lw = tok
            w.rd = {}

    def op(self, eng, fn, reads=(), writes=(), track=True):
        self._deps(eng, reads, writes)
        ins = fn(self.E[eng])
        self.ninst += 1
        idx = self.cnt[eng] + 1
        if track:
            self.cnt[eng] = idx
            ins.then_inc(self.sem[eng], 1)
        self._record((self.sem[eng], idx, eng), reads, writes)
        return ins

    def barrier(self):
        for e in self.E:
            for f in self.E:
                if f != e and self.cnt[f] > 0:
                    self._wait(e, (self.sem[f], self.cnt[f], f))
            for ds in ALL_DSEMS:
                if ds.count > 0:
                    self._wait(e, (ds.sem, ds.count, "dma"))

    def dma(self, q, out, in_, dsem, reads=(), writes=()):
        self._deps(q, reads, writes)
        ins = self.E[q].dma_start(out=out, in_=in_)
        self.ninst += 1
        dsem.count += 16
        ins.then_inc(dsem.sem, 16)
        self._record((dsem.sem, dsem.count, "dma"), reads, writes)
        return ins


def _bcast_rows(ap_1d_tensor, offset, n, parts=128):
    return bass.AP(ap_1d_tensor, offset, [[0, parts], [1, n]])


def build_program(dbg=()):
    nc = bass.Bass("TRN2", target_bir_lowering=False)
    del ALL_DSEMS[:]
    fw = FW(nc)

    def din(name, shape, dt=F32):
        return nc.dram_tensor(name, list(shape), dt, kind="ExternalInput")

    x_d = din("x", [L, D])
    w_in_d = din("w_in", [D, 9216])
    w_bh_d = din("w_bh", [DH, D])
    w_bs_d = din("w_bs", [DH, D])
    w_out_d = din("w_out", [D, D])
    w_fi_d = din("w_fi", [D, 2 * DFF])
    w_fo_d = din("w_fo", [DFF, D])
    gcol_d = din("gcol", [128, 32])
    gfin_d = din("gfin", [D])
    scw_d = din("scw", [128, 24, 4])
    fmlp_d = din("fmlp", [64, 64 * 3 + 8])
    zT_d = din("zT", [33, L])
    w4_d = din("w4", [64, 4096])
    hbias_d = din("hbias", [2 * DH])
    lngb_d = din("lngb", [2 * DH])
    wsT_d = din("wsT", [128, 8, 128])
    bs_d = din("bs", [8 * 128])
    absd_d = din("absd", [DH])
    tneg_d = din("tneg", [128, 16])
    ident_d = din("ident", [128, 128], BF16)
    gf_d = din("gf", [NFT, 128, 2, 16, 128], BF16)
    gi_d = din("gi", [NFT, 128, 2, L], BF16)
    out_d = nc.dram_tensor("out", [L, D], F32, kind="ExternalOutput")
    dbg_d = {}
    for nm, shp, dt in dbg:
        dbg_d[nm] = nc.dram_tensor(nm, list(shp), dt, kind="ExternalOutput")

    x_ap = x_d.ap()
    out_ap = out_d.ap()
    w_in_v = w_in_d.ap().rearrange("(k p) e -> p k e", p=128)
    w_bh_v = w_bh_d.ap().rearrange("(k p) e -> p k e", p=128)
    w_bs_v = w_bs_d.ap().rearrange("(k p) e -> p k e", p=128)
    w_out_v = w_out_d.ap().rearrange("(k p) e -> p k e", p=128)
    w_fi_v = w_fi_d.ap().rearrange("(k p) e -> p k e", p=128)
    w_fo_v = w_fo_d.ap().rearrange("(k p) e -> p k e", p=128)

    def sb(name, shape, dt):
        return nc.sbuf_tensor('sb_' + name, list(shape), dt)
    ps = nc.psum_tensor
    from contextlib import ExitStack
    with ExitStack() as top:
        def SB(name, shape, dt):
            return top.enter_context(sb(name, list(shape), dt))

        ident = SB("ident", [128, 128], BF16)
        gcol = SB("gcol", [128, 32], F32)
        scw = SB("scw", [128, 24, 4], F32)
        fmlp = SB("fmlp", [64, 200], F32)
        tneg = SB("tneg", [128, 16], F32)
        fsc = SB("fsc", [64, 4], F32)
        ones_bf = SB("ones_bf", [128, 128], BF16)
        epsc = SB("epsc", [128, 1], F32)
        cs_const = DSem(nc, "cs_const")
        R_const = Res("const")
        for (t, src) in ((ident, ident_d.ap()), (gcol, gcol_d.ap()), (scw, scw_d.ap()),
                         (fmlp, fmlp_d.ap()), (tneg, tneg_d.ap())):
            fw.dma("sp", t[:], src, cs_const, writes=[R_const])
        R_misc = Res("misc")
        fw.op("pool", lambda e: e.memset(ones_bf[:], 1.0), writes=[R_misc])
        fw.op("pool", lambda e: e.memset(epsc[:], EPS), writes=[R_misc])
        fw.op("dve", lambda e: e.tensor_scalar(out=fsc[:, 0:1], in0=fmlp[:, 195:196], scalar1=1.0 / TWO_PI,
                                                scalar2=None, op0=ALU.mult), reads=[R_const], writes=[R_misc])
        for k in range(3):
            fw.op("dve", lambda e, k=k: e.tensor_scalar(out=fsc[:, k + 1:k + 2], in0=fmlp[:, 192 + k:193 + k],
                                                         scalar1=fsc[:, 0:1], scalar2=None, op0=ALU.mult), reads=[R_const, R_misc], writes=[R_misc])

        psum = top.enter_context(ps("psum", [128, 8, 512], F32))
        R_bank = [Res("bank%d" % i) for i in range(8)]

        def pbf(b0):
            return psum[:, b0:b0 + 2, :].bitcast(BF16).rearrange("p b (k c) -> p (b k) c", c=128)

        dbg_sems = []

        def dump(name, ap, R):
            if name in dbg_d:
                ds = DSem(nc, "dbg_" + name)
                fw.dma("sp", dbg_d[name].ap(), ap, ds, reads=R)
                dbg_sems.append(ds)

        def TT_(eng, out, in0, in1, op, R, W):
            return fw.op(eng, lambda e: e.tensor_tensor(out=out, in0=in0, in1=in1, op=op), reads=R, writes=W)

        def TS_(eng, out, in0, s1, s2, op0, op1, R, W):
            if op1 is None:
                return fw.op(eng, lambda e: e.tensor_scalar(out=out, in0=in0, scalar1=s1, scalar2=None, op0=op0),
                             reads=R, writes=W)
            return fw.op(eng, lambda e: e.tensor_scalar(out=out, in0=in0, scalar1=s1, scalar2=s2, op0=op0, op1=op1),
                         reads=R, writes=W)

        def STT_(eng, out, in0, sc, in1, op0, op1, R, W):
            return fw.op(eng, lambda e: e.scalar_tensor_tensor(out=out, in0=in0, scalar=sc, in1=in1, op0=op0, op1=op1),
                         reads=R, writes=W)

        def ACT_(out, in_, func, R, W, scale=None, bias=None, accum=None):
            kw = {}
            if scale is not None:
                kw["scale"] = scale
            if bias is not None:
                kw["bias"] = bias
            if accum is not None:
                kw["accum_out"] = accum
            return fw.op("act", lambda e: e.activation(out=out, in_=in_, func=func, **kw), reads=R, writes=W)

        def MM_(out, lhsT, rhs, start, stop, R, W, track):
            return fw.op("pe", lambda e: e.matmul(out, lhsT, rhs, start=start, stop=stop), reads=R, writes=W,
                         track=track)

        def TR_(out, in_, R, W, track):
            return fw.op("pe", lambda e: e.transpose(out, in_, ident[:]), reads=R + [R_const], writes=W, track=track)

        x2T = SB("x2T", [128, 8, L], BF16)
        R_x2T = [Res("x2T%d" % i) for i in range(8)]

        WS = []
        R_ws = []
        S_ws = [DSem(nc, "ws%d" % i) for i in range(3)]
        ws_rot = [0]

        def next_ws():
            i = ws_rot[0] % 2
            ws_rot[0] += 1
            return i

        pieces = []
        for eb in range(2):
            pieces.append((("u", eb), w_in_v[:, :, 3072 + eb * 512:3072 + (eb + 1) * 512], 16, 512))
        for eb in range(2):
            pieces.append((("v", eb), w_in_v[:, :, 4096 + eb * 512:4096 + (eb + 1) * 512], 16, 512))
        for br in range(2):
            gofs_ = 5120 + br * 2048
            wb_v_ = w_bh_v if br == 0 else w_bs_v
            for db in range(4):
                pieces.append((("g", br, db), w_in_v[:, :, gofs_ + db * 512:gofs_ + (db + 1) * 512], 16, 512))
                pieces.append((("b", br, db), wb_v_[:, :, db * 512:(db + 1) * 512], 8, 512))
        for db in range(4):
            pieces.append((("o", db), w_out_v[:, :, db * 512:(db + 1) * 512], 16, 512))
        for s_ in range(4):
            f_ = 0
            while f_ < 11:
                nfl_ = min(2, 11 - f_)
                fg_ = s_ * 11 + f_
                pieces.append((("fg", s_, f_), w_fi_v[:, :, fg_ * 128:(fg_ + nfl_) * 128], 16, nfl_ * 128))
                pieces.append((("fu", s_, f_), w_fi_v[:, :, DFF + fg_ * 128:DFF + (fg_ + nfl_) * 128], 16, nfl_ * 128))
                f_ += nfl_
            for db in range(4):
                pieces.append((("fo", s_, db), w_fo_v[:, s_ * 11:(s_ + 1) * 11, db * 512:(db + 1) * 512], 11, 512))
        wc_off = {}
        off_ = 0
        for key, src_, K_, C_ in pieces:
            wc_off[key] = (off_, K_, C_)
            off_ += 128 * K_ * C_
        wc_d = nc.dram_tensor("wcache", [off_], BF16, kind="Internal")

        def wc_ap(key):
            o_, K_, C_ = wc_off[key]
            return wc_d.ap()[o_:o_ + 128 * K_ * C_].rearrange("(p k c) -> p k c", p=128, k=K_)

        S_conv = DSem(nc, "wconv")
        conv_pos = [0]

        def emit_conv(n):
            for _ in range(n):
                if conv_pos[0] >= len(pieces):
                    return
                key, src_, K_, C_ = pieces[conv_pos[0]]
                conv_pos[0] += 1
                fw.dma("pool", wc_ap(key), src_, S_conv)

        S_wsT = [DSem(nc, "wsT%d" % i) for i in range(3)]

        def load_piece(key, dst, wi):
            fw.dma("sp", dst, wc_ap(key), S_wsT[wi], writes=[R_ws[wi]])

        scr_d = nc.dram_tensor("scr_vx1", [16, 128, L], BF16, kind="Internal")
        R_scr = [Res() for _ in range(16)]
        S_scr = [DSem(nc, "scr0"), DSem(nc, "scr1")]

        def norm_transpose(xt_ap, R_x, dstT, R_dst, tcol, gofs, scr, R_scr, bank0):
            xn, ss, rstd = scr
            ACT_(xn[:], xt_ap, AF.Square, [R_x], [R_scr], accum=ss[:, 0:1])
            ACT_(rstd[:, 0:1], ss[:, 0:1], AF.Sqrt, [R_scr, R_misc], [R_scr], scale=1.0 / D, bias=epsc[:, 0:1])
            fw.op("dve", lambda e: e.reciprocal(out=rstd[:, 0:1], in_=rstd[:, 0:1]), reads=[R_scr], writes=[R_scr])
            ACT_(xn[:], xt_ap, AF.Copy, [R_x, R_scr], [R_scr], scale=rstd[:, 0:1])
            pv = pbf(bank0)
            Rb = [R_bank[bank0], R_bank[bank0 + 1]]
            for k in range(16):
                TR_(pv[:, k, :], xn[:, k * 128:(k + 1) * 128], [R_scr], Rb, track=(k == 15))
            for k in range(16):
                if k % 2 == 0:
                    ACT_(dstT[:, k, tcol:tcol + 128], pv[:, k, :], AF.Copy, Rb + [R_const], [R_dst],
                         scale=gcol[:, gofs + k:gofs + k + 1])
                else:
                    TS_("dve", dstT[:, k, tcol:tcol + 128], pv[:, k, :], gcol[:, gofs + k:gofs + k + 1], None,
                        ALU.mult, None, Rb + [R_const], [R_dst])

        sh = top.enter_context(ExitStack())
        h3T = sh.enter_context(sb("h3T", [64, L], F32))
        R_h3T = Res("h3T")
        with ExitStack() as st:
            zT = st.enter_context(sb("zT", [33, L], F32))
            hA = st.enter_context(sb("hA", [64, L], F32))
            hB = st.enter_context(sb("hB", [64, L], F32))
            utmp = st.enter_context(sb("utmp", [64, 2, 512], F32))
            rtmp = st.enter_context(sb("rtmp", [64, 2, 512], F32))
            R_zT, R_hA, R_hB = Res(), Res(), Res()
            R_ut = [Res(), Res()]
            R_rt = [Res(), Res()]
            S_zT = DSem(nc, "zT")
            fw.dma("sp", zT[:], zT_d.ap(), S_zT, writes=[R_zT])
            layers = [(zT, R_zT, 33, 0, hA, R_hA), (hA, R_hA, 64, 64, hB, R_hB), (hB, R_hB, 64, 128, h3T, R_h3T)]
            it = 0
            for li, (src, R_src, K, wo, dst, R_dst) in enumerate(layers):
                for tb in range(4):
                    bk = it % 2
                    it += 1
                    MM_(psum[0:64, bk, :], fmlp[0:K, wo:wo + 64], src[0:K, tb * 512:(tb + 1) * 512], True, True,
                        [R_const, R_src], [R_bank[bk]], True)
                    TS_("dve", utmp[:, bk, :], psum[0:64, bk, :], fsc[:, 0:1], fsc[:, li + 1:li + 2], ALU.mult,
                        ALU.add, [R_bank[bk], R_misc], [R_ut[bk]])
                    TS_("dve", rtmp[:, bk, :], utmp[:, bk, :], MAGIC, MAGIC, ALU.add, ALU.subtract, [R_ut[bk]],
                        [R_rt[bk]])
                    TT_("dve", utmp[:, bk, :], utmp[:, bk, :], rtmp[:, bk, :], ALU.subtract, [R_ut[bk], R_rt[bk]],
                        [R_ut[bk]])
                    ACT_(dst[:, tb * 512:(tb + 1) * 512], utmp[:, bk, :], AF.Sin, [R_ut[bk]], [R_dst],
                         scale=SIN_SCALE)
        dump("h3T", h3T[:], [R_h3T])
        fw.barrier()

        if True:

            with ExitStack() as s12:
                hT = s12.enter_context(sb("hT", [128, 16, L], BF16))
                R_hT = Res("hT")
                xin = s12.enter_context(sb("xin", [128, 2, D], F32))
                R_xin = [Res(), Res()]
                S_xin = [DSem(nc, "xin0"), DSem(nc, "xin1")]
                WS[:] = [s12.enter_context(sb("wsh%d" % i, [128, 16, 512], BF16)) for i in range(2)]
                R_ws[:] = [Res(), Res()]
                stg = s12.enter_context(sb("stg", [128, 2, L], BF16))
                R_stg = [Res(), Res()]
                xn_ = s12.enter_context(sb("xn0", [128, D], BF16))
                xn2 = [xn_, xn_]
                ss2 = [s12.enter_context(sb("ss%d" % i, [128, 1], F32)) for i in range(2)]
                rs2 = [s12.enter_context(sb("rs%d" % i, [128, 1], F32)) for i in range(2)]
                R_one = Res()
                R_scr2 = [R_one, R_one]
                for tt in range(NT):
                    i = tt % 2
                    fw.dma("sp", xin[:, i, :], x_ap[tt * 128:(tt + 1) * 128, :], S_xin[i], writes=[R_xin[i]])
                    norm_transpose(xin[:, i, :], R_xin[i], hT, R_hT, tt * 128, 0, (xn2[i], ss2[i], rs2[i]),
                                   R_scr2[i], 4 + 2 * i)
                dump("hT", hT[:], [R_hT])

                pstage = s12.enter_context(sb("pstage", [128, 2, L + 2], F32))
                R_pst = [Res(), Res()]
                tmpc = s12.enter_context(sb("tmpc", [128, 1, L], F32))
                R_tmpc = [Res()]
                for i in range(2):
                    fw.op("pool", lambda e, i=i: e.memset(pstage[:, i, 0:1], 0.0), writes=[R_pst[i]])
                    fw.op("pool", lambda e, i=i: e.memset(pstage[:, i, L + 1:L + 2], 0.0), writes=[R_pst[i]])
                grp = 0
                for eb in range(6):
                    wi = next_ws()
                    fw.dma("pool", WS[wi][:, :, :], w_in_v[:, :, eb * 512:(eb + 1) * 512], S_ws[wi],
                           writes=[R_ws[wi]])
                    for ej in range(4):
                        et = eb * 4 + ej
                        si = et % 2
                        emit_conv(1)
                        for th in range(2):
                            b0 = (grp % 4) * 2
                            grp += 1
                            for tb in range(2):
                                for k in range(16):
                                    MM_(psum[:, b0 + tb, :], WS[wi][:, k, ej * 128:(ej + 1) * 128],
                                        hT[:, k, th * 1024 + tb * 512: th * 1024 + (tb + 1) * 512],
                                        k == 0, k == 15, [R_ws[wi], R_hT], [R_bank[b0 + tb]], k == 15)
                            ACT_(pstage[:, si, 1 + th * 1024: 1 + (th + 1) * 1024],
                                 psum[:, b0:b0 + 2, :].rearrange("p b c -> p (b c)"), AF.Copy,
                                 [R_bank[b0], R_bank[b0 + 1]], [R_pst[si]])
                        tci = 0
                        if et < 16:
                            dstap = stg[:, si, :]
                            Rd = R_stg[si]
                        else:
                            dstap = x2T[:, et - 16, :]
                            Rd = R_x2T[et - 16]
                        TS_("dve", tmpc[:, tci, :], pstage[:, si, 1:L + 1], scw[:, et, 1:2], scw[:, et, 3:4], ALU.mult,
                            ALU.add, [R_pst[si], R_const], [R_tmpc[tci]])
                        STT_("dve", tmpc[:, tci, :], pstage[:, si, 0:L], scw[:, et, 0:1], tmpc[:, tci, :], ALU.mult,
                             ALU.add, [R_pst[si], R_tmpc[tci], R_const], [R_tmpc[tci]])
                        STT_("dve", dstap, pstage[:, si, 2:L + 2], scw[:, et, 2:3], tmpc[:, tci, :], ALU.mult,
                             ALU.add, [R_pst[si], R_tmpc[tci], R_const], [Rd])
                        if et < 16:
                            fw.dma("sp", scr_d.ap()[et], stg[:, si, :], S_scr[si], reads=[R_stg[si]])
                            if et == 0:
                                dump("vT0", stg[:, si, :], [R_stg[si]])
            dump("x2T_pre", x2T[:], R_x2T)
            fw.barrier()

            with ExitStack() as s3:
                DHt = s3.enter_context(sb("DHt", [128, 16, 5, CG], BF16))
                R_DHd = Res("DHdata")
                R_DHf = [Res("DHfiltA"), Res("DHfiltB")]
                Yr = s3.enter_context(sb("Yr", [128, NFT, CG], BF16))
                Ys = s3.enter_context(sb("Ys", [128, NFT, CG], BF16))
                R_Y = [Res() for _ in range(NFT)]
                z1T = s3.enter_context(sb("z1T", [128, 2, L], BF16))
                R_z1T = [Res(), Res()]
                vg = s3.enter_context(sb("vg", [128, 2, L], BF16))
                x1g = s3.enter_context(sb("x1g", [128, 2, L], BF16))
                R_vg = [Res(), Res()]
                R_x1g = [Res(), Res()]
                S_vg = DSem(nc, "vg")
                S_x1g = DSem(nc, "x1g")
                tabs = [s3.enter_context(sb("tab%d" % i, [128, 2, L], BF16)) for i in range(3)]
                R_tab = [Res() for _ in range(3)]
                S_tab = [DSem(nc, "tab%d" % i) for i in range(3)]
                tab_rot = [0]
                w4g = s3.enter_context(sb("w4g", [64, 2, 2, CG], F32))
                R_w4g = [Res(), Res()]
                S_w4g = [DSem(nc, "w4g0"), DSem(nc, "w4g1")]
                hb_bc = s3.enter_context(sb("hb_bc", [128, 2, 2, CG], F32))
                absd_bc = s3.enter_context(sb("absd_bc", [128, 2, CG], F32))
                R_gc = [Res("groupconst0"), Res("groupconst1")]
                S_gc = [DSem(nc, "groupconst0"), DSem(nc, "groupconst1")]
                dec = s3.enter_context(sb("dec", [128, 2, CG], F32))
                R_dec = [Res(), Res()]
                hdec = s3.enter_context(sb("hdec", [128, 2, 2, CG], F32))
                R_hdec = [Res(), Res()]
                habs = s3.enter_context(sb("habs", [128, 2, 2, CG], BF16))
                R_habs = [Res(), Res()]
                rn = s3.enter_context(sb("rn", [128, 2, CG], F32))
                R_rn = [Res(), Res()]
                ytmp = s3.enter_context(sb("ytmp", [128, 2, 6, CG], F32))
                R_yt = [[Res() for _ in range(6)] for _ in range(2)]

                def load_group_consts(g):
                    gp = g % 2
                    c0 = g * CG
                    fw.dma("sp", absd_bc[:, gp, :], _bcast_rows(absd_d, c0, CG), S_gc[gp], writes=[R_gc[gp]])
                    for o in range(2):
                        fw.dma("sp", hb_bc[:, gp, o, :], _bcast_rows(hbias_d, o * DH + c0, CG), S_gc[gp],
                               writes=[R_gc[gp]])

                def filter_steps(g, o, fs):
                    gp = g % 2
                    c0 = g * CG
                    for d in range(2):
                        col = d * 2048 + o * DH + c0
                        fw.dma("sp", w4g[:, fs, d, :], w4_d.ap()[:, col:col + CG], S_w4g[fs], writes=[R_w4g[fs]])
                    nb = 6 + fs
                    for k in range(NT + 1):
                        if k < NT:
                            nt = k
                            i = nt % 2
                            fb = 4 + i
                            MM_(psum[:, fb, :], h3T[0:64, nt * 128:(nt + 1) * 128],
                                w4g[0:64, fs, :, :].rearrange("p d c -> p (d c)"), True, True,
                                [R_h3T, R_w4g[fs]], [R_bank[fb]], True)
                            ACT_(dec[:, i, :], absd_bc[:, gp, :], AF.Exp, [R_gc[gp], R_const], [R_dec[i]],
                                 scale=tneg[:, nt:nt + 1])
                            for d in range(2):
                                TT_("dve", hdec[:, i, d, :], psum[:, fb, d * CG:(d + 1) * CG], dec[:, i, :], ALU.mult,
                                    [R_bank[fb], R_dec[i]], [R_hdec[i]])
                            if nt == 0:
                                fw.op("pool", lambda e, i=i: e.memset(hdec[0:1, i, 1, :], 0.0), writes=[R_hdec[i]])
                            ACT_(habs[:, i, :, :], hdec[:, i, :, :], AF.Abs, [R_hdec[i]], [R_habs[i]])
                            TT_("pool", DHt[:, nt, 1 + 2 * fs, :], hdec[:, i, 0, :], hdec[:, i, 1, :], ALU.add,
                                [R_hdec[i]], [R_DHf[fs]])
                            TT_("pool", DHt[:, nt, 2 + 2 * fs, :], hdec[:, i, 1, :], hdec[:, i, 0, :], ALU.subtract,
                                [R_hdec[i]], [R_DHf[fs]])
                        if k >= 1:
                            nt = k - 1
                            i = nt % 2
                            for d in range(2):
                                MM_(psum[:, nb, 0:CG], ones_bf[:], habs[:, i, d, :], (nt == 0 and d == 0),
                                    (nt == NT - 1 and d == 1), [R_misc, R_habs[i]], [R_bank[nb]], d == 1)
                        if k == NT:
                            fw.op("dve", lambda e, fs=fs, nb=nb: e.reciprocal(out=rn[:, fs, :], in_=psum[:, nb, 0:CG]),
                                  reads=[R_bank[nb]], writes=[R_rn[fs]])
                        yield

                convs = [(g, o) for g in range(NG) for o in range(2)]
                load_group_consts(0)
                for _ in filter_steps(0, 0, 0):
                    pass
                for ci, (g, o) in enumerate(convs):
                    fs = ci % 2
                    gp = g % 2
                    if o == 0:
                        for ct in range(2):
                            fw.dma("sp", vg[:, ct, :], scr_d.ap()[g * 2 + ct], S_vg, writes=[R_vg[ct]])
                            fw.dma("sp", x1g[:, ct, :], scr_d.ap()[8 + g * 2 + ct], S_x1g,
                                   writes=[R_x1g[ct]])
                        for ct in range(2):
                            R_vg[ct].lw = (S_vg.sem, S_vg.count, "dma")
                            R_x1g[ct].lw = (S_x1g.sem, S_x1g.count, "dma")
                    nxt = None
                    if ci + 1 < len(convs):
                        g2, o2 = convs[ci + 1]
                        if o2 == 0:
                            load_group_consts(g2)
                        nxt = filter_steps(g2, o2, 1 - fs)

                    for ct in range(2):
                        src = vg[:, ct, :] if o == 0 else z1T[:, ct, :]
                        R_src = R_vg[ct] if o == 0 else R_z1T[ct]
                        b0 = 2 * ct
                        pv = pbf(b0)
                        Rb = [R_bank[b0], R_bank[b0 + 1]]
                        for tt in range(NT):
                            TR_(pv[:, tt, :], src[:, tt * 128:(tt + 1) * 128], [R_src], Rb, track=(tt == NT - 1))
                        ACT_(DHt[:, :, 0, ct * 128:(ct + 1) * 128], pv[:, :, :], AF.Copy, Rb, [R_DHd])

                    for ft in range(NFT):
                        if ft % 4 == 0:
                            emit_conv(1)
                        ti = tab_rot[0] % 3
                        tab_rot[0] += 1
                        tf = tabs[ti][:, :, :].rearrange("p s (a f) -> p s a f", f=128)
                        fw.dma("sp", tabs[ti][:, :, :], gf_d.ap()[ft].rearrange("p s a f -> p s (a f)"), S_tab[ti],
                               writes=[R_tab[ti]])
                        bc = (ft % 2) * 2
                        bs_ = bc + 1
                        hi_c = 1 + 2 * fs
                        hi_s = 2 + 2 * fs
                        for a in range(16):
                            MM_(psum[:, bc, :].rearrange("p (b c) -> p b c", b=2), tf[:, 0, a, :],
                                DHt[:, a, 0:hi_c + 1:hi_c, :], a == 0, a == 15, [R_tab[ti], R_DHd, R_DHf[fs]],
                                [R_bank[bc]], a == 15)
                        for a in range(16):
                            MM_(psum[:, bs_, :].rearrange("p (b c) -> p b c", b=2), tf[:, 1, a, :],
                                DHt[:, a, 0:hi_s + 1:hi_s, :], a == 0, a == 15, [R_tab[ti], R_DHd, R_DHf[fs]],
                                [R_bank[bs_]], a == 15)
                        if nxt is not None:
                            next(nxt, None)
                        yi = ft % 2
                        kr, ki, a1, b1, c1, d1 = [ytmp[:, yi, j, :] for j in range(6)]
                        Rk = R_yt[yi]
                        ZC = psum[:, bc, 0:CG]
                        KC = psum[:, bc, CG:2 * CG]
                        ZS = psum[:, bs_, 0:CG]
                        KS = psum[:, bs_, CG:2 * CG]
                        TT_("dve", kr, KC, rn[:, fs, :], ALU.mult, [R_bank[bc], R_rn[fs]], [Rk[0]])
                        TT_("pool", kr, kr, hb_bc[:, gp, o, :], ALU.add, [Rk[0], R_gc[gp]], [Rk[0]])
                        TT_("dve", ki, KS, rn[:, fs, :], ALU.mult, [R_bank[bs_], R_rn[fs]], [Rk[1]])
                        TT_("dve", a1, ZC, kr, ALU.mult, [R_bank[bc], Rk[0]], [Rk[2]])
                        TT_("dve", b1, ZS, ki, ALU.mult, [R_bank[bs_], Rk[1]], [Rk[3]])
                        TT_("pool", Yr[:, ft, :], a1, b1, ALU.add, [Rk[2], Rk[3]], [R_Y[ft]])
                        TT_("dve", c1, ZS, kr, ALU.mult, [R_bank[bs_], Rk[0]], [Rk[4]])
                        TT_("dve", d1, ZC, ki, ALU.mult, [R_bank[bc], Rk[1]], [Rk[5]])
                        TT_("pool", Ys[:, ft, :], c1, d1, ALU.subtract, [Rk[4], Rk[5]], [R_Y[ft]])
                    if nxt is not None:
                        for _ in nxt:
                            pass

                    for a in range(NFT):
                        if a % 4 == 1:
                            emit_conv(1)
                        ti = tab_rot[0] % 3
                        tab_rot[0] += 1
                        fw.dma("sp", tabs[ti][:, :, :], gi_d.ap()[a], S_tab[ti], writes=[R_tab[ti]])
                        for ct in range(2):
                            for tb in range(4):
                                bk = ct * 4 + tb
                                MM_(psum[:, bk, :], Yr[:, a, ct * 128:(ct + 1) * 128],
                                    tabs[ti][:, 0, tb * 512:(tb + 1) * 512], a == 0, False,
                                    [R_tab[ti], R_Y[a]], [R_bank[bk]], False)
                                MM_(psum[:, bk, :], Ys[:, a, ct * 128:(ct + 1) * 128],
                                    tabs[ti][:, 1, tb * 512:(tb + 1) * 512], False, a == NFT - 1,
                                    [R_tab[ti], R_Y[a]], [R_bank[bk]], (a == NFT - 1) or (ct == 1 and tb == 3))
                    for ct in range(2):
                        for tb in range(4):
                            bk = ct * 4 + tb
                            cs = slice(tb * 512, (tb + 1) * 512)
                            eng = "dve" if tb % 2 == 0 else "dve"
                            if o == 0:
                                TT_(eng, z1T[:, ct, cs], psum[:, bk, :], x1g[:, ct, cs], ALU.mult,
                                    [R_bank[bk], R_x1g[ct]], [R_z1T[ct]])
                            else:
                                TT_(eng, x2T[:, g * 2 + ct, cs], psum[:, bk, :], x2T[:, g * 2 + ct, cs], ALU.mult,
                                    [R_bank[bk], R_x2T[g * 2 + ct]], [R_x2T[g * 2 + ct]])
                    if g == 0 and o == 0:
                        dump("z1T", z1T[:], R_z1T)
        emit_conv(len(pieces))
        sh.close()
        dump("yhT", x2T[:], R_x2T)
        fw.barrier()

        if not HYENA_ONLY:
            with ExitStack() as s4:
                x1b = s4.enter_context(sb("x1b", [128, 4, D], F32))
                R_x1b = [Res() for _ in range(4)]
                S_x1b = [DSem(nc, "x1b%d" % i) for i in range(4)]
                S_out = [DSem(nc, "outs%d" % i) for i in range(4)]
                hTt = s4.enter_context(sb("hTt", [128, 16, TT], BF16))
                R_hTt = Res("hTt")
                WS[:] = [s4.enter_context(sb("wst%d" % i, [128, 16, 512], BF16)) for i in range(2)]
                WS.append(s4.enter_context(sb("wst2", [128, 8, 512], BF16)))
                R_ws[:] = [Res(), Res(), Res()]
                xn_ = s4.enter_context(sb("xnt0", [128, D], BF16))
                xn2 = [xn_, xn_]
                ss2 = [s4.enter_context(sb("sst%d" % i, [128, 1], F32)) for i in range(2)]
                rs2 = [s4.enter_context(sb("rst%d" % i, [128, 1], F32)) for i in range(2)]
                R_one = Res()
                R_scr2 = [R_one, R_one]
                gfin_bc = s4.enter_context(sb("gfin_bc", [128, D], F32))
                lng_bc = s4.enter_context(sb("lng_bc", [128, 2, DH], F32))
                bs_bc = s4.enter_context(sb("bs_bc", [128, 8, 128], F32))
                wsT = s4.enter_context(sb("wsT", [128, 8, 128], BF16))
                R_tc = Res("tconst")
                S_tc = DSem(nc, "tconst")
                fw.dma("sp", gfin_bc[:], _bcast_rows(gfin_d, 0, D), S_tc, writes=[R_tc])
                fw.dma("sp", lng_bc[:, :, :].rearrange("p a c -> p (a c)"), _bcast_rows(lngb_d, 0, 2 * DH), S_tc,
                       writes=[R_tc])
                fw.dma("sp", bs_bc[:, :, :].rearrange("p a c -> p (a c)"), _bcast_rows(bs_d, 0, 1024), S_tc,
                       writes=[R_tc])
                S_tc2 = DSem(nc, "tconst2")
                R_tc2 = Res("tconst2")
                fw.dma("pool", wsT[:], wsT_d.ap(), S_tc2, writes=[R_tc2])
                uT = s4.enter_context(sb("uT", [128, 8, TT], BF16))
                R_uT = [Res() for _ in range(8)]
                mT = s4.enter_context(sb("mT", [128, 16, TT], BF16))
                R_mT = [Res() for _ in range(16)]
                v2g = s4.enter_context(sb("v2g", [128, 1, DH], F32))
                R_v2g = [Res()] * 2
                v2n = s4.enter_context(sb("v2n", [128, 1, DH], BF16))
                R_v2n = [Res()] * 2
                lnst = s4.enter_context(sb("lnst", [128, 1, 8], F32))
                sgt = s4.enter_context(sb("sgt", [128, 1, 8, 128], F32))
                R_sgt = [Res()] * 2
                etmp = s4.enter_context(sb("etmp", [128, 2, TT], F32))
                R_et = [Res() for _ in range(2)]
                actT = s4.enter_context(sb("actT", [128, 1, 11, TT], BF16))
                R_actT = [[Res() for _ in range(11)]] * 2
                bank_rot = [0]

                def nb1():
                    b = bank_rot[0] % 8
                    bank_rot[0] += 1
                    return b

                def nb2():
                    if bank_rot[0] % 2:
                        bank_rot[0] += 1
                    b = bank_rot[0] % 8
                    bank_rot[0] += 2
                    return b

                et_rot = [0]

                def net():
                    i = et_rot[0] % 2
                    et_rot[0] += 1
                    return i

                for J in range(L // TT):
                    t0 = J * TT
                    for tt in range(4):
                        fw.dma("sp", x1b[:, tt, :], x_ap[t0 + tt * 128:t0 + (tt + 1) * 128, :], S_x1b[tt],
                               writes=[R_x1b[tt]])
                    for tt in range(4):
                        i = tt % 2
                        norm_transpose(x1b[:, tt, :], R_x1b[tt], hTt, R_hTt, tt * 128, 0,
                                       (xn2[i], ss2[i], rs2[i]), R_scr2[i], nb2())
                    if J == 0:
                        dump("hTt", hTt[:], [R_hTt])
                    for eb in range(2):
                        wi = next_ws()
                        load_piece(("u", eb), WS[wi][:, :, :], wi)
                        for ej in range(4):
                            et = eb * 4 + ej
                            bk = nb1()
                            for k in range(16):
                                MM_(psum[:, bk, :], WS[wi][:, k, ej * 128:(ej + 1) * 128], hTt[:, k, :], k == 0,
                                    k == 15, [R_ws[wi], R_hTt], [R_bank[bk]], k == 15)
                            ACT_(uT[:, et, :], psum[:, bk, :], AF.Gelu, [R_bank[bk]], [R_uT[et]])
                    wv = []
                    for eb in range(2):
                        wi = next_ws()
                        load_piece(("v", eb), WS[wi][:, :, :], wi)
                        wv.append(wi)
                    for tt in range(4):
                        i = 0
                        for eb in range(2):
                            wi = wv[eb]
                            bk = nb1()
                            for k in range(16):
                                MM_(psum[:, bk, :], hTt[:, k, tt * 128:(tt + 1) * 128], WS[wi][:, k, :], k == 0,
                                    k == 15, [R_ws[wi], R_hTt], [R_bank[bk]], k == 15)
                            ACT_(v2g[:, i, eb * 512:(eb + 1) * 512], psum[:, bk, :], AF.Gelu, [R_bank[bk]],
                                 [R_v2g[i]], accum=lnst[:, i, eb:eb + 1])
                        st_ = lnst[:, i, :]
                        TS_("dve", st_[:, 2:3], st_[:, 0:1], st_[:, 1:2], -1.0 / DH, ALU.add, ALU.mult, [R_v2g[i]],
                            [R_v2g[i]])
                        ACT_(v2n[:, i, :], v2g[:, i, :], AF.Square, [R_v2g[i]], [R_v2n[i]], bias=st_[:, 2:3],
                             accum=st_[:, 3:4])
                        ACT_(st_[:, 4:5], st_[:, 3:4], AF.Sqrt, [R_v2n[i], R_v2g[i], R_misc], [R_v2g[i]], scale=1.0 / DH,
                             bias=epsc[:, 0:1])
                        fw.op("dve", lambda e, st_=st_: e.reciprocal(out=st_[:, 4:5], in_=st_[:, 4:5]),
                              reads=[R_v2g[i]], writes=[R_v2g[i]])
                        TS_("dve", v2g[:, i, :], v2g[:, i, :], st_[:, 2:3], st_[:, 4:5], ALU.add, ALU.mult,
                            [R_v2g[i]], [R_v2g[i]])
                        TT_("pool", v2g[:, i, :], v2g[:, i, :], lng_bc[:, 0, :], ALU.mult, [R_v2g[i], R_tc],
                            [R_v2g[i]])
                        TT_("pool", v2n[:, i, :], v2g[:, i, :], lng_bc[:, 1, :], ALU.add, [R_v2g[i], R_tc],
                            [R_v2n[i]])
                        if J == 0 and tt == 0:
                            dump("v2n", v2n[:, 0, :], [R_v2n[0]])
                        b0 = nb2()
                        for h in range(8):
                            MM_(psum[:, b0 + h // 4, (h % 4) * 128:(h % 4 + 1) * 128], v2n[:, i, h * 128:(h + 1) * 128],
                                wsT[:, h, :], True, True, [R_v2n[i], R_tc2], [R_bank[b0], R_bank[b0 + 1]], h == 7)
                        TT_("dve", sgt[:, i, :, :], psum[:, b0:b0 + 2, :].rearrange("p b (h c) -> p (b h) c", c=128),
                            bs_bc[:, :, :], ALU.add, [R_bank[b0], R_bank[b0 + 1], R_tc], [R_sgt[i]])
                        TT_("pool", uT[:, :, tt * 128:(tt + 1) * 128], sgt[:, i, :, :], uT[:, :, tt * 128:(tt + 1) * 128],
                            ALU.mult, [R_sgt[i]] + R_uT, R_uT)
                    if J == 0:
                        dump("ysT", uT[:], R_uT)
                    for br in range(2):
                        gofs = 5120 + br * 2048
                        wb_v = w_bh_v if br == 0 else w_bs_v
                        for db in range(4):
                            wi = next_ws()
                            load_piece(("g", br, db), WS[wi][:, :, :], wi)
                            wj = 2
                            load_piece(("b", br, db), WS[wj][:, 0:8, :], wj)
                            for dj in range(4):
                                dt_ = db * 4 + dj
                                bA = nb1()
                                for k in range(16):
                                    MM_(psum[:, bA, :], WS[wi][:, k, dj * 128:(dj + 1) * 128], hTt[:, k, :], k == 0,
                                        k == 15, [R_ws[wi], R_hTt], [R_bank[bA]], k == 15)
                                bC = nb1()
                                for c in range(8):
                                    if br == 0:
                                        rhs = x2T[:, c, t0:t0 + TT]
                                        Rr = [R_x2T[c]]
                                    else:
                                        rhs = uT[:, c, :]
                                        Rr = [R_uT[c]]
                                    MM_(psum[:, bC, :], WS[wj][:, c, dj * 128:(dj + 1) * 128], rhs, c == 0, c == 7,
                                        [R_ws[wj]] + Rr, [R_bank[bC]], c == 7)
                                e1 = net()
                                ACT_(etmp[:, e1, :], psum[:, bA, :], AF.Sigmoid, [R_bank[bA]], [R_et[e1]])
                                if br == 0:
                                    TT_("dve", mT[:, dt_, :], psum[:, bC, :], etmp[:, e1, :], ALU.mult,
                                        [R_bank[bC], R_et[e1]], [R_mT[dt_]])
                                else:
                                    TT_("dve", etmp[:, e1, :], psum[:, bC, :], etmp[:, e1, :], ALU.mult,
                                        [R_bank[bC], R_et[e1]], [R_et[e1]])
                                    TT_("pool", mT[:, dt_, :], mT[:, dt_, :], etmp[:, e1, :], ALU.add,
                                        [R_et[e1], R_mT[dt_]], [R_mT[dt_]])
                    if J == 0:
                        dump("mT", mT[:], R_mT)
                    for db in range(4):
                        wi = next_ws()
                        load_piece(("o", db), WS[wi][:, :, :], wi)
                        for tt in range(4):
                            bk = nb1()
                            for k in range(16):
                                MM_(psum[:, bk, :], mT[:, k, tt * 128:(tt + 1) * 128], WS[wi][:, k, :], k == 0, k == 15,
                                    [R_ws[wi], R_mT[k]], [R_bank[bk]], k == 15)
                            TT_("dve", x1b[:, tt, db * 512:(db + 1) * 512], psum[:, bk, :],
                                x1b[:, tt, db * 512:(db + 1) * 512], ALU.add, [R_bank[bk], R_x1b[tt]], [R_x1b[tt]])
                    if J == 0:
                        dump("xmid", x1b[:, 0, :], [R_x1b[0]])
                    for tt in range(4):
                        i = tt % 2
                        norm_transpose(x1b[:, tt, :], R_x1b[tt], hTt, R_hTt, tt * 128, 16,
                                       (xn2[i], ss2[i], rs2[i]), R_scr2[i], nb2())
                    for s in range(4):
                        ai = 0
                        f = 0
                        while f < 11:
                            nfl = min(2, 11 - f)
                            fg = s * 11 + f
                            wi = next_ws()
                            load_piece(("fg", s, f), WS[wi][:, :, 0:nfl * 128], wi)
                            load_piece(("fu", s, f), WS[wi][:, :, 256:256 + nfl * 128], wi)
                            for j in range(nfl):
                                bG = nb1()
                                for k in range(16):
                                    MM_(psum[:, bG, :], WS[wi][:, k, j * 128:(j + 1) * 128], hTt[:, k, :], k == 0,
                                        k == 15, [R_ws[wi], R_hTt], [R_bank[bG]], k == 15)
                                bU = nb1()
                                for k in range(16):
                                    MM_(psum[:, bU, :], WS[wi][:, k, 256 + j * 128:256 + (j + 1) * 128], hTt[:, k, :],
                                        k == 0, k == 15, [R_ws[wi], R_hTt], [R_bank[bU]], k == 15)
                                e1 = net()
                                ACT_(etmp[:, e1, :], psum[:, bG, :], AF.Silu, [R_bank[bG]], [R_et[e1]])
                                TT_("dve", actT[:, ai, f + j, :], psum[:, bU, :], etmp[:, e1, :], ALU.mult,
                                    [R_bank[bU], R_et[e1]], [R_actT[ai][f + j]])
                            f += nfl
                        for db in range(4):
                            wi = next_ws()
                            load_piece(("fo", s, db), WS[wi][:, 0:11, :], wi)
                            for tt in range(4):
                                bk = nb1()
                                for j in range(11):
                                    MM_(psum[:, bk, :], actT[:, ai, j, tt * 128:(tt + 1) * 128], WS[wi][:, j, :],
                                        j == 0, j == 10, [R_ws[wi], R_actT[ai][j]], [R_bank[bk]], j == 10)
                                TT_("dve", x1b[:, tt, db * 512:(db + 1) * 512], psum[:, bk, :],
                                    x1b[:, tt, db * 512:(db + 1) * 512], ALU.add, [R_bank[bk], R_x1b[tt]],
                                    [R_x1b[tt]])
                    for tt in range(4):
                        i = tt % 2
                        ACT_(xn2[i][:], x1b[:, tt, :], AF.Square, [R_x1b[tt]], [R_scr2[i]], accum=ss2[i][:, 0:1])
                        ACT_(rs2[i][:, 0:1], ss2[i][:, 0:1], AF.Sqrt, [R_scr2[i], R_misc], [R_scr2[i]], scale=1.0 / D,
                             bias=epsc[:, 0:1])
                        fw.op("dve", lambda e, i=i: e.reciprocal(out=rs2[i][:, 0:1], in_=rs2[i][:, 0:1]),
                              reads=[R_scr2[i]], writes=[R_scr2[i]])
                        TS_("pool", x1b[:, tt, :], x1b[:, tt, :], rs2[i][:, 0:1], None, ALU.mult, None,
                            [R_x1b[tt], R_scr2[i]], [R_x1b[tt]])
                        TT_("pool", x1b[:, tt, :], x1b[:, tt, :], gfin_bc[:], ALU.mult, [R_x1b[tt], R_tc], [R_x1b[tt]])
                        fw.dma("sp", out_ap[t0 + tt * 128:t0 + (tt + 1) * 128, :], x1b[:, tt, :], S_out[tt],
                               reads=[R_x1b[tt]])
                for tt in range(4):
                    dbg_sems.append(S_out[tt])

        for k in ("pe", "act", "dve", "pool"):
            if fw.cnt[k] > 0:
                fw._wait("sp", (fw.sem[k], fw.cnt[k], k))
        for ds in dbg_sems:
            if ds.count > 0:
                fw._wait("sp", (ds.sem, ds.count, "dma"))
    return nc, fw


_CONST = {}


def _bf16(a):
    return np.asarray(a, dtype=np.float32).astype(ml_dtypes.bfloat16)


def host_constants():
    if _CONST:
        return _CONST
    N = 2 * L
    t = np.linspace(0.0, 1.0, L, dtype=np.float32)[:, None]
    w = (np.float32(2.0 * math.pi) * np.arange(L, dtype=np.float32)[:, None] / np.float32(L)).astype(np.float32)
    f = np.linspace(1e-4, 15, 16, dtype=np.float32)[None, :]
    z = np.concatenate([t, np.cos(f * w), -np.sin(f * w)], axis=-1).astype(np.float32)
    _CONST["zT"] = np.ascontiguousarray(z.T)
    max_decay = math.log(1e-2) / 0.3
    min_decay = math.log(1e-2) / 1.5
    deltas = np.linspace(min_decay, max_decay, DH, dtype=np.float32)
    _CONST["absd"] = np.abs(deltas).astype(np.float32)
    tn = np.linspace(0.0, 1.0, L, dtype=np.float32)
    _CONST["tneg"] = np.ascontiguousarray((-tn).reshape(16, 128).T)
    _CONST["ident"] = _bf16(np.eye(128))
    a = np.arange(NFT * 128, dtype=np.int64)
    tt_ = np.arange(L, dtype=np.int64)
    ff = a[:, None]
    ang = (2.0 * np.pi / N) * ((ff * tt_[None, :]) % N).astype(np.float64)
    valid = (a <= L)[:, None]
    C = np.where(valid, np.cos(ang), 0.0)
    S = np.where(valid, np.sin(ang), 0.0)
    Cf = C.reshape(NFT, 128, 16, 128)
    Sf = S.reshape(NFT, 128, 16, 128)
    gf = np.stack([Cf.transpose(0, 3, 2, 1), Sf.transpose(0, 3, 2, 1)], axis=2)
    _CONST["gf"] = _bf16(gf)
    wf = np.full((NFT * 128,), 2.0 / N)
    wf[0] = 1.0 / N
    wf[L] = 1.0 / N
    Ci = (C * wf[:, None]).reshape(NFT, 128, L)
    Si = (S * wf[:, None]).reshape(NFT, 128, L)
    _CONST["gi"] = _bf16(np.stack([Ci, Si], axis=2))
    return _CONST


def make_in_maps(inputs):
    c = host_constants()
    g = lambda k: np.asarray(inputs[k], dtype=np.float32)
    x = g("x")
    shared = {}
    shared["w_in"] = np.ascontiguousarray(g("w_in")[0])
    shared["w_bh"] = np.ascontiguousarray(g("w_branch_hyena")[0])
    shared["w_bs"] = np.ascontiguousarray(g("w_branch_sgu")[0])
    shared["w_out"] = np.ascontiguousarray(g("w_out")[0])
    shared["w_fi"] = np.ascontiguousarray(g("w_ffn_in")[0])
    shared["w_fo"] = np.ascontiguousarray(g("w_ffn_out")[0])
    gcol = np.concatenate([g("norm_mix_g")[0].reshape(16, 128).T, g("norm_ffn_g")[0].reshape(16, 128).T], axis=1)
    shared["gcol"] = np.ascontiguousarray(gcol)
    shared["gfin"] = np.ascontiguousarray(g("norm_final_g"))
    scw = np.concatenate([g("short_conv_w")[0], g("short_conv_b")], axis=0)
    shared["scw"] = np.ascontiguousarray(scw.reshape(4, 24, 128).transpose(2, 1, 0))
    fm = np.zeros((64, 200), np.float32)
    fm[0:33, 0:64] = g("filt_w1")[0]
    fm[:, 64:128] = g("filt_w2")[0]
    fm[:, 128:192] = g("filt_w3")[0]
    fm[:, 192] = g("filt_b1")[0]
    fm[:, 193] = g("filt_b2")[0]
    fm[:, 194] = g("filt_b3")[0]
    fm[:, 195] = g("filt_freq")[0]
    shared["fmlp"] = fm
    shared["zT"] = c["zT"]
    shared["w4"] = np.ascontiguousarray(g("filt_w4")[0])
    shared["hbias"] = np.ascontiguousarray(g("hyena_bias")[0].reshape(-1))
    shared["lngb"] = np.ascontiguousarray(np.concatenate([g("sgu_ln_g")[0], g("sgu_ln_b")[0]]))
    shared["wsT"] = np.ascontiguousarray(g("sgu_w_s")[0].transpose(2, 0, 1))
    shared["bs"] = np.ascontiguousarray(g("sgu_b_s")[0].reshape(-1))
    shared["absd"] = c["absd"]
    shared["tneg"] = c["tneg"]
    shared["ident"] = c["ident"]
    shared["gf"] = c["gf"]
    shared["gi"] = c["gi"]
    maps = []
    for b in range(x.shape[0]):
        m = dict(shared)
        m["x"] = np.ascontiguousarray(x[b])
        maps.append(m)
    return maps


_PROG = {}


def kernel(**inputs):
    if "nc" not in _PROG:
        _PROG["nc"] = build_program()[0]
    nc = _PROG["nc"]
    in_maps = make_in_maps(inputs)
    res = run_bass_kernel_spmd(nc, in_maps, core_ids=list(range(8)))
    out = np.stack([np.asarray(r["out"], dtype=np.float32) for r in res.results], axis=0)
    return out
```
